# Optimizing a Trainium2 kernel written in Bass

```python
import jax, jax.numpy as jnp
from jax import lax
import numpy as np

D_MODEL = 1024
BATCH = 8
SEQ = 2048
DEPTH = 2

HEAD_DIM = 64
NSA_HEADS = 8
NSA_KV_GROUPS = 2
NSA_HPG = NSA_HEADS // NSA_KV_GROUPS
NSA_WIDTH = NSA_HEADS * HEAD_DIM
KV_WIDTH = NSA_KV_GROUPS * HEAD_DIM
CMP_BLOCK = 32
CMP_STRIDE = 16
CMP_HIDDEN = 128
SEL_BLOCK = 64
SEL_TOPK = 16
WINDOW = 512
Q_BLOCK = 128
GM_GROUPS = 8
GM_CHUNK = 128
GM_WIDTH = GM_GROUPS * HEAD_DIM
N_BRANCH = 2
D_FF = 2816
ROPE_THETA = 10000.0
EPS = 1e-6
NEG = -1e30
FORCE = 1e4
IN_SIZES = [NSA_WIDTH, 6 * KV_WIDTH, 3 * NSA_HEADS, 2 * GM_WIDTH, D_MODEL, D_MODEL]
IN_WIDTH = sum(IN_SIZES)
IN_SPLITS = np.cumsum(IN_SIZES)[:-1].tolist()

kernel_name = "hybrid_nsa_gmlp_macaron_adaln"


def _rmsnorm(x, g):
    x32 = x.astype(jnp.float32)
    y = x32 * lax.rsqrt(jnp.mean(x32 * x32, axis=-1, keepdims=True) + EPS)
    return y.astype(x.dtype) * g


def _layernorm(x, g, b):
    x32 = x.astype(jnp.float32)
    mu = jnp.mean(x32, axis=-1, keepdims=True)
    var = jnp.mean(jnp.square(x32 - mu), axis=-1, keepdims=True)
    return ((x32 - mu) * lax.rsqrt(var + EPS)).astype(x.dtype) * g + b


def _rope_tables(pos):
    inv = 1.0 / (ROPE_THETA ** (jnp.arange(0, HEAD_DIM, 2, dtype=jnp.float32) / HEAD_DIM))
    ang = pos.astype(jnp.float32)[:, None] * inv[None, :]
    return jnp.cos(ang), jnp.sin(ang)


def _rope(x, cos, sin):
    x32 = x.astype(jnp.float32)
    x1, x2 = jnp.split(x32, 2, axis=-1)
    c = cos[None, :, None, :]
    s = sin[None, :, None, :]
    return jnp.concatenate([x1 * c - x2 * s, x2 * c + x1 * s], axis=-1).astype(x.dtype)


def _modulate(xn, shift, scale):
    return xn * (1.0 + scale) + shift


def _swiglu(x, w_in, w_out):
    a, b = jnp.split(x @ w_in, 2, axis=-1)
    return (jax.nn.silu(a) * b) @ w_out


def _nsa(q, k_cmp, v_cmp, k_sel, v_sel, k_win, v_win, gates, pe, w1, w2):
    B, S = q.shape[0], q.shape[1]
    G, HG = NSA_KV_GROUPS, NSA_HPG
    scale = HEAD_DIM ** -0.5
    pos = jnp.arange(S, dtype=jnp.int32)
    cos, sin = _rope_tables(pos)
    qr = _rope(q, cos, sin).reshape(B, S, G, HG, HEAD_DIM)

    n_cmp = (S - CMP_BLOCK) // CMP_STRIDE + 1
    starts = jnp.arange(n_cmp, dtype=jnp.int32) * CMP_STRIDE
    win_idx = starts[:, None] + jnp.arange(CMP_BLOCK, dtype=jnp.int32)[None, :]

    def compress(t, j):
        blk = t[:, win_idx] + pe[j][None, None, :, None, :]
        h = jax.nn.silu(jnp.einsum('bnlgd,ldf->bngf', blk, w1[j]))
        return h @ w2[j]

    kc = compress(k_cmp, 0)
    vc = compress(v_cmp, 1)
    cend = starts + CMP_BLOCK - 1
    ccos, csin = _rope_tables(cend)
    kc = _rope(kc, ccos, csin)
    s_c = jnp.einsum('bsghd,bngd->bsghn', qr, kc).astype(jnp.float32) * scale
    m_c = (cend[None, :] <= pos[:, None])[None, :, None, None, :]
    p_c = jax.nn.softmax(jnp.where(m_c, s_c, NEG), axis=-1) * m_c
    o_cmp = jnp.einsum('bsghn,bngd->bsghd', p_c.astype(vc.dtype), vc)

    n_sel = S // SEL_BLOCK
    top = min(SEL_TOPK, n_sel)
    sel_start = jnp.arange(n_sel, dtype=jnp.int32) * SEL_BLOCK
    overlap = jnp.clip(
        jnp.minimum(starts[:, None] + CMP_BLOCK, sel_start[None, :] + SEL_BLOCK)
        - jnp.maximum(starts[:, None], sel_start[None, :]), 0, None
    ).astype(jnp.float32) / CMP_BLOCK
    imp = jnp.einsum('bsgn,nj->bsgj', jnp.sum(p_c, axis=3), overlap)
    cur = pos // SEL_BLOCK
    blk = jnp.arange(n_sel, dtype=jnp.int32)
    forced = (blk[None, :] == 0) | (blk[None, :] == cur[:, None]) | (blk[None, :] == cur[:, None] - 1)
    valid = blk[None, :] <= cur[:, None]
    imp = jnp.where(forced[None, :, None, :], FORCE,
                    jnp.where(valid[None, :, None, :], imp, -FORCE))
    _, sel_idx = lax.top_k(imp, top)
    sel_idx = sel_idx.transpose(0, 2, 1, 3)

    ks = _rope(k_sel, cos, sin).reshape(B, n_sel, SEL_BLOCK, G, HEAD_DIM).transpose(0, 3, 1, 2, 4)
    vs = v_sel.reshape(B, n_sel, SEL_BLOCK, G, HEAD_DIM).transpose(0, 3, 1, 2, 4)
    pad = ((0, 0), (WINDOW, 0), (0, 0), (0, 0))
    kw = jnp.pad(_rope(k_win, cos, sin), pad)
    vw = jnp.pad(v_win, pad)
    bi = jnp.arange(B)[:, None, None, None]
    gi = jnp.arange(G)[None, :, None, None]
    jj = jnp.arange(SEL_BLOCK, dtype=jnp.int32)
    wj = jnp.arange(Q_BLOCK + WINDOW, dtype=jnp.int32)

    def block(qb):
        q0 = qb * Q_BLOCK
        qq = lax.dynamic_slice_in_dim(qr, q0, Q_BLOCK, axis=1)
        tq = q0 + jnp.arange(Q_BLOCK, dtype=jnp.int32)
        idx = lax.dynamic_slice_in_dim(sel_idx, q0, Q_BLOCK, axis=2)
        kg = ks[bi, gi, idx]
        vg = vs[bi, gi, idx]
        s = jnp.einsum('bqghd,bgqnjd->bgqhnj', qq, kg).astype(jnp.float32) * scale
        kpos = idx[..., None] * SEL_BLOCK + jj
        m = (kpos <= tq[None, None, :, None, None])[:, :, :, None]
        s = jnp.where(m, s, NEG).reshape(B, G, Q_BLOCK, HG, top * SEL_BLOCK)
        p = jax.nn.softmax(s, axis=-1).reshape(B, G, Q_BLOCK, HG, top, SEL_BLOCK)
        o_s = jnp.einsum('bgqhnj,bgqnjd->bqghd', p.astype(vg.dtype), vg)
        kk = lax.dynamic_slice_in_dim(kw, q0, Q_BLOCK + WINDOW, axis=1)
        vv = lax.dynamic_slice_in_dim(vw, q0, Q_BLOCK + WINDOW, axis=1)
        kp = q0 - WINDOW + wj
        d = tq[:, None] - kp[None, :]
        mw = ((d >= 0) & (d < WINDOW) & (kp[None, :] >= 0))[None, :, None, None, :]
        sw = jnp.einsum('bqghd,bkgd->bqghk', qq, kk).astype(jnp.float32) * scale
        pw = jax.nn.softmax(jnp.where(mw, sw, NEG), axis=-1)
        o_w = jnp.einsum('bqghk,bkgd->bqghd', pw.astype(vv.dtype), vv)
        return o_s, o_w

    o_sel, o_win = lax.map(block, jnp.arange(S // Q_BLOCK, dtype=jnp.int32))
    o_sel = o_sel.transpose(1, 0, 2, 3, 4, 5).reshape(B, S, G, HG, HEAD_DIM)
    o_win = o_win.transpose(1, 0, 2, 3, 4, 5).reshape(B, S, G, HG, HEAD_DIM)
    g = gates.reshape(B, S, G, HG, 3)
    o = g[..., 0:1] * o_cmp + g[..., 1:2] * o_sel + g[..., 2:3] * o_win
    return o.reshape(B, S, NSA_WIDTH)


def _gmlp(uv, ln_g, ln_b, ws, bs):
    B, S = uv.shape[0], uv.shape[1]
    u, v = jnp.split(jax.nn.gelu(uv), 2, axis=-1)
    v = _layernorm(v, ln_g, ln_b)
    v = v.reshape(B, S // GM_CHUNK, GM_CHUNK, GM_GROUPS, HEAD_DIM)
    w = ws * jnp.tril(jnp.ones((GM_CHUNK, GM_CHUNK), ws.dtype))
    sv = jnp.einsum('gts,bnsgc->bntgc', w, v) + bs.T[None, None, :, :, None]
    return u * sv.reshape(B, S, GM_WIDTH)


def setup_inputs(seed: int = 0) -> dict:
    key = jax.random.key(seed)
    k = jax.random.split(key, 24)
    L, D = DEPTH, D_MODEL

    def nrm(kk, shape, scale):
        return jax.random.normal(kk, shape, jnp.float32) * scale

    return {
        "x": nrm(k[0], (BATCH, SEQ, D), 1.0),
        "c": nrm(k[1], (BATCH, D), 1.0),
        "ada_w": nrm(k[2], (L, D, 9 * D), 0.5 * D ** -0.5),
        "ada_b": nrm(k[3], (L, 9 * D), 0.1),
        "norm_g": 1.0 + nrm(k[4], (L, 3, D), 0.1),
        "ffn_w_in": nrm(k[5], (L, 2, D, 2 * D_FF), D ** -0.5),
        "ffn_w_out": nrm(k[6], (L, 2, D_FF, D), D_FF ** -0.5),
        "mix_w_in": nrm(k[7], (L, D, IN_WIDTH), D ** -0.5),
        "cmp_pe": nrm(k[8], (L, 2, CMP_BLOCK, HEAD_DIM), 0.1),
        "cmp_w1": nrm(k[9], (L, 2, CMP_BLOCK, HEAD_DIM, CMP_HIDDEN), (CMP_BLOCK * HEAD_DIM) ** -0.5),
        "cmp_w2": nrm(k[10], (L, 2, CMP_HIDDEN, HEAD_DIM), CMP_HIDDEN ** -0.5),
        "gm_ln_g": 1.0 + nrm(k[11], (L, GM_WIDTH), 0.1),
        "gm_ln_b": nrm(k[12], (L, GM_WIDTH), 0.1),
        "gm_ws": nrm(k[13], (L, GM_GROUPS, GM_CHUNK, GM_CHUNK), GM_CHUNK ** -0.5),
        "gm_bs": 1.0 + nrm(k[14], (L, GM_GROUPS, GM_CHUNK), 0.1),
        "proj_a": nrm(k[15], (L, NSA_WIDTH, D), NSA_WIDTH ** -0.5),
        "proj_b": nrm(k[16], (L, GM_WIDTH, D), GM_WIDTH ** -0.5),
        "w_out": nrm(k[17], (L, D, D), D ** -0.5),
        "final_g": 1.0 + nrm(k[18], (D,), 0.1),
    }


def reference(x, c, ada_w, ada_b, norm_g, ffn_w_in, ffn_w_out, mix_w_in, cmp_pe, cmp_w1,
              cmp_w2, gm_ln_g, gm_ln_b, gm_ws, gm_bs, proj_a, proj_b, w_out, final_g):
    B, S, D = x.shape
    h = x
    for l in range(DEPTH):
        mod = (jax.nn.silu(c) @ ada_w[l] + ada_b[l]).reshape(B, 3, 3, 1, D)

        n = _modulate(_rmsnorm(h, norm_g[l, 0]), mod[:, 0, 0], mod[:, 0, 1])
        h = h + 0.5 * mod[:, 0, 2] * _swiglu(n, ffn_w_in[l, 0], ffn_w_out[l, 0])

        n = _modulate(_rmsnorm(h, norm_g[l, 1]), mod[:, 1, 0], mod[:, 1, 1])
        z = n @ mix_w_in[l]
        zq, zkv, zg, zuv, zga, zgb = jnp.split(z, IN_SPLITS, axis=-1)
        q = zq.reshape(B, S, NSA_HEADS, HEAD_DIM)
        kv = zkv.reshape(B, S, 6, NSA_KV_GROUPS, HEAD_DIM)
        nsa_gates = jax.nn.sigmoid(zg).reshape(B, S, NSA_HEADS, 3)
        y_a = _nsa(q, kv[:, :, 0], kv[:, :, 1], kv[:, :, 2], kv[:, :, 3], kv[:, :, 4], kv[:, :, 5],
                   nsa_gates, cmp_pe[l], cmp_w1[l], cmp_w2[l])
        y_b = _gmlp(zuv, gm_ln_g[l], gm_ln_b[l], gm_ws[l], gm_bs[l])
        merged = jax.nn.sigmoid(zga) * (y_a @ proj_a[l]) + jax.nn.sigmoid(zgb) * (y_b @ proj_b[l])
        h = h + mod[:, 1, 2] * (merged @ w_out[l])

        n = _modulate(_rmsnorm(h, norm_g[l, 2]), mod[:, 2, 0], mod[:, 2, 1])
        h = h + 0.5 * mod[:, 2, 2] * _swiglu(n, ffn_w_in[l, 1], ffn_w_out[l, 1])
    return _rmsnorm(h, final_g)
```

```python
import os
import numpy as np
import ml_dtypes
from contextlib import ExitStack
import concourse.bass as bass
import concourse.mybir as mybir
from concourse.bass_utils import run_bass_kernel_spmd

F32 = mybir.dt.float32
BF16 = mybir.dt.bfloat16
AF = mybir.ActivationFunctionType
ALU = mybir.AluOpType
DS = {F32: 4, BF16: 2}

S = 2048
D = 1024
DFF = 2816
NCMP = 127
EPS = 1e-6
ENGS = ["sync", "scalar", "vector", "gpsimd", "tensor"]
GRAN = {"hT": 2048, "nT": 1024, "arena": 1024, "ps": 2048}

OFF_U = 0
OFF_V = 512
OFF_KC = 1024
OFF_QG = [1280, 1280 + 908]
OFF_KG = [1280 + 512, 1280 + 908 + 512]
OFF_VG = [1280 + 768, 1280 + 908 + 768]
OFF_GAB = 1280 + 2 * 908
NEXT = OFF_GAB + 2048

ARENA_KIB = 96
STOP = os.environ.get("MK_STOP", "")


def ap_keys(ap):
    name = ap.tensor.name
    if name not in GRAN:
        return [name]
    g = GRAN[name]
    ds = DS[ap.dtype]
    pat = ap.ap
    pstep = pat[0][0]
    off = (ap.offset % pstep) if pstep else ap.offset
    dims = [(s, n) for (s, n) in pat[1:]]
    res = set()

    def rec(i, base):
        if i >= len(dims):
            res.add(base * ds // g)
            return
        s, n = dims[i]
        if i == len(dims) - 1:
            lo = base
            hi = base + (n - 1) * abs(s)
            for ch in range(lo * ds // g, (hi * ds + ds - 1) // g + 1):
                res.add(ch)
        else:
            if s == 0:
                n = 1
            for j in range(n):
                rec(i + 1, base + j * s)

    rec(0, off)
    if name == "arena":
        p0 = ap.offset // pstep if pstep else 0
        q0, q1 = p0 // 32, (p0 + pat[0][1] - 1) // 32
        return [f"{name}{c}q{q}" for c in sorted(res) for q in range(q0, q1 + 1)]
    return [f"{name}{c}" for c in sorted(res)]


class Prog:
    def __init__(self):
        self.ops = {e: [] for e in ENGS}
        self.cnt = {}
        self.last_w = {}
        self.readers = {}
        self.seen = {e: {} for e in ENGS}
        self.dma_sems = set()

    def op(self, eng, fn, reads=(), writes=(), dma=None):
        need = {}

        def add(ev):
            if ev is None:
                return
            s, v = ev
            if s in self.dma_sems:
                v = self.cnt[s]
            if need.get(s, 0) < v:
                need[s] = v

        for k in reads:
            add(self.last_w.get(k))
        for k in writes:
            add(self.last_w.get(k))
            for s, v in self.readers.get(k, {}).items():
                add((s, v))
        own = "e_" + eng
        waits = []
        for s, v in need.items():
            if s == own and eng == "tensor":
                continue
            if self.seen[eng].get(s, 0) >= v:
                continue
            self.seen[eng][s] = v
            waits.append((s, v))
        if dma is None:
            sem, inc = own, 1
        else:
            sem, inc = dma, 16
            self.dma_sems.add(dma)
        self.cnt[sem] = self.cnt.get(sem, 0) + inc
        ev = (sem, self.cnt[sem])
        for k in writes:
            self.last_w[k] = ev
            self.readers[k] = {}
        for k in reads:
            d = self.readers.setdefault(k, {})
            d[sem] = max(d.get(sem, 0), ev[1])
        self.ops[eng].append((waits, fn, sem, inc))
        return ev

    def prewait(self, eng, reads):
        need = {}
        for k in reads:
            ev = self.last_w.get(k)
            if ev is None:
                continue
            s_, v = ev
            if s_ in self.dma_sems:
                v = self.cnt[s_]
            if need.get(s_, 0) < v:
                need[s_] = v
        own = "e_" + eng
        waits = []
        for s_, v in need.items():
            if s_ == own and eng == "tensor":
                continue
            if self.seen[eng].get(s_, 0) >= v:
                continue
            self.seen[eng][s_] = v
            waits.append((s_, v))
        if waits:
            self.ops[eng].append((waits, None, None, 0))

    def final_wait(self, eng, sems_):
        self.ops[eng].append(([(s, self.cnt[s]) for s in sems_], None, None, 0))

    def emit(self, block, sems):
        for e in ENGS:
            ops = self.ops[e]

            def body(eng, ops=ops):
                for waits, fn, sem, inc in ops:
                    for s, v in waits:
                        eng.wait_ge(sems[s], v)
                    if fn is not None:
                        fn(eng).then_inc(sems[sem], inc)

            getattr(block, e)(body)


def build_program():
    nc = bass.Bass("TRN2", target_bir_lowering=False)
    P = Prog()

    def dram(name, shape, dt=F32, kind="ExternalInput"):
        return nc.dram_tensor(name, list(shape), dt, kind=kind).ap()

    xT_d = dram("xT", [D, S])
    cT_d = dram("cT", [128, 8])
    ada_w_d = dram("ada_w", [2, D, 9 * D])
    ada_bT_d = dram("ada_bT", [128, 2, 72])
    normg_d = dram("normg", [128, 56])
    ffn_w_in_d = dram("ffn_w_in", [2, 2, D, 2 * DFF])
    ffn_w_out_d = dram("ffn_w_out", [2, 2, DFF, D])
    mixw_d = dram("mixw", [2, D, NEXT])
    peT_d = dram("cmp_peT", [2, 64, 2, 32])
    w1_d = dram("cmp_w1", [2, 2, 32, 64, 128])
    w2e_d = dram("cmp_w2e", [2, 128, 192])
    gmln_d = dram("gm_ln", [2, 2, 512])
    wsT_d = dram("gm_wsT", [2, 128, 8, 128])
    bsT_d = dram("gm_bsT", [2, 128, 4, 128])
    lnbT_d = dram("gm_lnbT", [2, 128, 4])
    proj_a_d = dram("proj_a", [2, 512, D])
    proj_b_d = dram("proj_b", [2, 512, D])
    w_out_d = dram("w_out", [2, D, D])
    rope_d = dram("c_rope", [64, 2, S])
    crope_d = dram("c_crope", [64, 2, NCMP])
    tri_d = dram("c_tri", [128, 2, 128], BF16)
    mneg_d = dram("c_mneg", [128, 2, 512], BF16)
    cmk_d = dram("c_cmk", [8, 136 + 512], BF16)
    erows_d = dram("c_erows", [32, S], BF16)
    vcc_d = dram("c_vcc", [128, 33], BF16)
    fv_d = dram("c_fv", [128, 16, 2, 32])
    identb_d = dram("c_identb", [128, 128], BF16)
    outT_d = dram("outT", [D, S], F32, kind="ExternalOutput")

    es = ExitStack()
    with es:
        def sb(name, shape, dt):
            return es.enter_context(nc.sbuf_tensor(name, list(shape), dt))

        hT = sb("hT", [128, 8 * S], F32)
        nT = sb("nT", [128, 8 * S], BF16)
        arena = sb("arena", [128, ARENA_KIB * 256], F32)
        psum = es.enter_context(nc.psum_tensor("ps", [128, 4096], F32))
        tri = sb("tri", [128, 2, 128], BF16)
        mneg = sb("mneg", [128, 2, 512], BF16)
        cmk = sb("cmk", [8, 136 + 512], BF16)
        fv = sb("fv", [128, 16, 2, 32], F32)
        identb = sb("identb", [128, 128], BF16)
        onesm = sb("onesm", [128, 128], BF16)
        onesb = sb("onesb", [128, 128], BF16)
        lnbT = sb("lnbT", [128, 4], F32)
        epsc = sb("epsc", [128, 1], F32)
        nhalf = sb("nhalf", [128, 1], F32)
        cT = sb("cTs", [128, 8], F32)
        scb = sb("scb", [128, 8], BF16)
        ada_bT = sb("ada_bTs", [128, 2, 72], F32)
        normg = sb("normgs", [128, 56], F32)
        modT = [sb(f"modT{l}", [128, 72], F32) for l in range(2)]
        der = [sb(f"der{l}", [128, 3, 16], F32) for l in range(2)]
        sm = {}
        for nm, shp, dt in [("rden", [128, 4], F32), ("coef", [128, 4], F32), ("imp", [128, 32], F32),
                            ("impf", [128, 32], F32), ("imp2", [128, 32], F32), ("m8", [128, 16], F32),
                            ("selpen", [128, 96], BF16), ("bnst", [128, 6], F32), ("bnmv", [128, 2], F32),
                            ("lnr", [128, 1], F32), ("pebias", [128, 2], F32), ("peT", [64, 2, 32], BF16),
                            ("w2e", [128, 192], BF16), ("hcT", [128, 128], BF16), ("kcT", [64, 2, 128], BF16),
                            ("vcaug", [128, 2, 97], BF16), ("crope", [64, 2, NCMP], F32),
                            ("rden2", [128, 4], F32), ("coef2", [128, 4], F32),
                            ("rden3", [128, 4], F32), ("coef3", [128, 4], F32)]:
            sm[nm] = sb(nm, shp, dt)

        sem_names = ["e_" + e for e in ENGS] + ["d_r0", "d_r1", "d_r2", "d_c", "d_x", "d_o", "d_q", "d_m1s", "d_m1h", "d_m2s", "d_m2h", "d_m3h"]
        sems = {n: es.enter_context(nc.semaphore(n)) for n in sem_names}
        block = es.enter_context(nc.Block())

        hv = hT[:].rearrange("p (c t) -> p c t", c=8)
        nv = nT[:].rearrange("p (c t) -> p c t", c=8)

        def H(c, tb):
            return hv[:, c, tb * 512:(tb + 1) * 512]

        def N_(c, tb):
            return nv[:, c, tb * 512:(tb + 1) * 512]

        def av(off_b, size_b, dt):
            a = arena[:, off_b // 4:(off_b + size_b) // 4]
            return a.bitcast(BF16) if dt == BF16 else a

        KB = 1024

        def rk(*aps):
            ks = []
            for a in aps:
                if a is None or isinstance(a, (int, float)):
                    continue
                ks += ap_keys(a)
            return ks

        def mm(out, lhsT, rhs, start=True, stop=True, sgc=False):
            P.op("tensor", lambda e: e.matmul(out, lhsT=lhsT, rhs=rhs, start=start, stop=stop, skip_group_check=sgc),
                 reads=rk(lhsT, rhs), writes=rk(out))

        def transpose(out, in_, ident):
            P.op("tensor", lambda e: e.transpose(out, in_, ident), reads=rk(in_, ident), writes=rk(out))

        def act(out, in_, func, bias=None, scale=None):
            kw = {}
            if bias is not None:
                kw["bias"] = bias
            if scale is not None:
                kw["scale"] = scale
            P.op("scalar", lambda e: e.activation(out=out, in_=in_, func=func, **kw),
                 reads=rk(in_, bias, scale), writes=rk(out))

        def tt(out, in0, in1, op, eng="vector"):
            P.op(eng, lambda e: e.tensor_tensor(out=out, in0=in0, in1=in1, op=op), reads=rk(in0, in1), writes=rk(out))

        def ts(out, in0, s1, op0, s2=None, op1=None, eng="vector"):
            if op1 is None:
                P.op(eng, lambda e: e.tensor_scalar(out=out, in0=in0, scalar1=s1, scalar2=None, op0=op0),
                     reads=rk(in0, s1), writes=rk(out))
            else:
                P.op(eng, lambda e: e.tensor_scalar(out=out, in0=in0, scalar1=s1, scalar2=s2, op0=op0, op1=op1),
                     reads=rk(in0, s1, s2), writes=rk(out))

        def stt(out, in0, scalar, in1, op0, op1):
            P.op("vector", lambda e: e.scalar_tensor_tensor(out=out, in0=in0, scalar=scalar, in1=in1, op0=op0, op1=op1),
                 reads=rk(in0, scalar, in1), writes=rk(out))

        def vcopy(out, in_, eng="vector"):
            P.op(eng, lambda e: e.tensor_copy(out=out, in_=in_), reads=rk(in_), writes=rk(out))

        def memset(ap, val, eng="vector"):
            P.op(eng, lambda e: e.memset(ap, val), writes=rk(ap))

        def dma(eng, out, in_, sem, **kw):
            rd = rk(in_) if in_.tensor.name in SBN else []
            wr = rk(out) if out.tensor.name in SBN else [out.tensor.name]
            if sem.startswith("d_m"):
                sem = sem + ("s" if eng == "gpsimd" else "h")
            return P.op(eng, lambda e: e.dma_start(out=out, in_=in_, **kw), reads=rd, writes=wr, dma=sem)

        SBN = set(["hT", "nT", "arena", "tri", "mneg", "cmk", "fv", "identb", "onesm", "onesb", "lnbT", "epsc", "nhalf", "cTs", "scb", "ada_bTs", "normgs",
                   "modT0", "modT1", "der0", "der1"] + list(sm.keys()))

        bstate = {"next": 0, "held": set()}

        def bank(hold=False, fixed=None):
            if fixed is not None:
                if hold:
                    bstate["held"].add(fixed)
                return fixed
            for _ in range(16):
                i = bstate["next"]
                bstate["next"] = (i + 1) % 8
                if i not in bstate["held"]:
                    if hold:
                        bstate["held"].add(i)
                    return i
            raise RuntimeError("no psum bank")

        def bank_pair():
            for _ in range(16):
                i = bstate["next"]
                if i % 2 == 1:
                    i = (i + 1) % 8
                bstate["next"] = (i + 2) % 8
                if i not in bstate["held"] and (i + 1) not in bstate["held"]:
                    return i
            raise RuntimeError("no psum bank pair")

        def release(i):
            bstate["held"].discard(i)

        def PS(i, p0=0, p1=128, c0=0, c1=512):
            return psum[p0:p1, i * 512 + c0:i * 512 + c1]

        def PSB(i, p0, p1, c0, c1):
            return psum[p0:p1, i * 512:(i + 1) * 512].bitcast(BF16)[:, c0:c1]

        rstate = {"next": 0}

        def ring():
            i = rstate["next"]
            rstate["next"] = (i + 1) % 3
            return av(i * 8 * KB, 8 * KB, BF16), f"d_r{i}"

        for c in range(8):
            dma("sync", hv[:, c, :], xT_d[c * 128:(c + 1) * 128, :], "d_x")
        for dst, src in [(cT, cT_d), (ada_bT, ada_bT_d), (normg, normg_d), (tri, tri_d), (mneg, mneg_d), (cmk, cmk_d),
                         (fv, fv_d), (identb, identb_d)]:
            dma("sync", dst[:], src, "d_c")
        memset(onesm[:], 1.0 / 1024.0)
        memset(epsc[:], EPS)
        memset(onesb[:], 1.0)
        memset(nhalf[:], -0.5)
        act(scb[:], cT[:], AF.Silu)

        def mod_steps(l):
            bm = bank(hold=True, fixed=7)
            awv = ada_w_d[l].rearrange("(kc p) f -> p kc f", p=128)
            for s in range(18):
                slot, sem = ring()
                sv = slot.rearrange("p (k f) -> p k f", k=8)
                dma("gpsimd", sv, awv[:, :, s * 512:(s + 1) * 512], sem)
                for fc in range(4):
                    j = s * 4 + fc
                    for kc in range(8):
                        mm(PS(bm, 0, 128, j, j + 1), sv[:, kc, fc * 128:(fc + 1) * 128], scb[:, kc:kc + 1],
                           start=(kc == 0), stop=(kc == 7))
                if s % 6 == 5:
                    sub = s // 6
                    tt(modT[l][:, sub * 24:(sub + 1) * 24], PS(bm, 0, 128, sub * 24, (sub + 1) * 24),
                       ada_bT[:, l, sub * 24:(sub + 1) * 24], ALU.add)
                    stt(der[l][:, sub, 0:8], modT[l][:, (sub * 3 + 1) * 8:(sub * 3 + 2) * 8], 1.0,
                        normg[:, (l * 3 + sub) * 8:(l * 3 + sub + 1) * 8], ALU.add, ALU.mult)
                    ts(der[l][:, sub, 8:16], modT[l][:, (sub * 3 + 2) * 8:(sub * 3 + 3) * 8],
                       1.0 if sub == 1 else 0.5, ALU.mult)
                    if s == 17:
                        release(bm)
                yield

        NSCR = 84 * KB

        def rms_stats(tb, rstd, all_act=False):
            bk = bank()
            for c in range(8):
                sq = av(NSCR + (c % 4) * KB, KB, BF16)
                if all_act or c in (0, 2, 5, 7):
                    act(sq, H(c, tb), AF.Square)
                else:
                    tt(sq, H(c, tb), H(c, tb), ALU.mult)
                mm(PS(bk), onesm[:], sq, start=(c == 0), stop=(c == 7))
            act(rstd, PS(bk), AF.Ln, bias=epsc[:, 0:1])
            act(rstd, rstd, AF.Exp, scale=-0.5)

        def norm_mod(l, sub):
            for tb in range(4):
                rstd = av(NSCR + 4 * KB + (tb % 2) * 2 * KB, 2 * KB, F32)
                rms_stats(tb, rstd)
                for c in range(8):
                    tmp = av(NSCR + 8 * KB + (c % 2) * 2 * KB, 2 * KB, F32)
                    if False:
                        pass
                    else:
                        tt(tmp, H(c, tb), rstd, ALU.mult)
                        act(N_(c, tb), tmp, AF.Identity, bias=modT[l][:, sub * 24 + c:sub * 24 + c + 1],
                            scale=der[l][:, sub, c:c + 1])

        def final_norm():
            for tb in range(4):
                rstd = av(NSCR + 4 * KB + (tb % 2) * 2 * KB, 2 * KB, F32)
                rms_stats(tb, rstd, all_act=True)
                for c in range(8):
                    stt(H(c, tb), H(c, tb), normg[:, 48 + c:49 + c], rstd, ALU.mult, ALU.mult)

        def ffn(l, i, sub, interleave=None):
            winv = ffn_w_in_d[l, i].rearrange("(kc p) f -> p kc f", p=128)
            woutv = ffn_w_out_d[l, i].rearrange("(kc p) f -> p kc f", p=128)
            gbuf = av(24 * KB, 48 * KB, BF16).rearrange("p (c t) -> p c t", c=12)
            SA = 72 * KB
            nsa_ = [0]
            def load_pair(ca, il=True):
                if il and interleave is not None:
                    next(interleave, None)
                slot, sem = ring()
                sv = slot.rearrange("p (k a f) -> p k a f", k=8, a=2)
                dma("gpsimd", sv[:, :, 0, :], winv[:, :, ca * 128:ca * 128 + 256], sem)
                dma("gpsimd", sv[:, :, 1, :], winv[:, :, DFF + ca * 128:DFF + ca * 128 + 256], sem)
                return sv

            def pair_tb(sv, gi, cc, tb):
                bA = bank()
                for kc in range(8):
                    mm(PS(bA), sv[:, kc, 0, cc * 128:(cc + 1) * 128], N_(kc, tb), start=(kc == 0), stop=(kc == 7))
                bB = bank()
                for kc in range(8):
                    mm(PS(bB), sv[:, kc, 1, cc * 128:(cc + 1) * 128], N_(kc, tb), start=(kc == 0), stop=(kc == 7))
                sa = av(SA + (nsa_[0] % 3) * 2 * KB, 2 * KB, F32)
                nsa_[0] += 1
                act(sa, PS(bA), AF.Silu)
                tt(gbuf[:, gi, tb * 512:(tb + 1) * 512], sa, PS(bB), ALU.mult)

            for (c0, c1) in [(0, 12), (12, 22)]:
                npairs = (c1 - c0) // 2
                jp0 = 0
                if c0 == 0:
                    svs = [load_pair(c0 + 2 * jp, il=False) for jp in range(3)]
                    for tb in range(4):
                        for jp in range(3):
                            for cc in range(2):
                                pair_tb(svs[jp], 2 * jp + cc, cc, tb)
                    jp0 = 3
                for jp in range(jp0, npairs):
                    ca = c0 + 2 * jp
                    sv = load_pair(ca)
                    for cc in range(2):
                        gi = ca + cc - c0
                        for tb in range(4):
                            pair_tb(sv, gi, cc, tb)
                nk = c1 - c0
                for fp in range(4):
                    if interleave is not None:
                        next(interleave, None)
                    slot, sem = ring()
                    sv = slot[:, 0:nk * 256].rearrange("p (k f) -> p k f", k=nk)
                    dma("gpsimd", sv, woutv[:, c0:c1, fp * 256:(fp + 1) * 256], sem)
                    for fl in range(2):
                        fo = fp * 2 + fl
                        for tb in range(4):
                            bk = bank()
                            for k in range(nk):
                                mm(PS(bk), sv[:, k, fl * 128:(fl + 1) * 128], gbuf[:, k, tb * 512:(tb + 1) * 512],
                                   start=(k == 0), stop=(k == nk - 1))
                            stt(H(fo, tb), PS(bk), der[l][:, sub, 8 + fo:9 + fo], H(fo, tb), ALU.mult, ALU.add)

        yaT = av(24 * KB, 16 * KB, BF16).rearrange("p (c t) -> p c t", c=4)
        ybT = av(40 * KB, 16 * KB, BF16).rearrange("p (c t) -> p c t", c=4)
        SCR = 40 * KB
        cmpT = av(56 * KB, 16 * KB, BF16).rearrange("p (i t) -> p i t", i=4)
        w1v = av(72 * KB, 16 * KB, BF16).rearrange("p (j l f) -> p j l f", j=2, l=32)
        rope = av(56 * KB, 16 * KB, F32).rearrange("p (a t) -> p a t", a=2)
        Qaug = av(72 * KB, 16 * KB, BF16).rearrange("p (q h t) -> p q h t", q=16, h=4)
        Ksel = av(88 * KB, 4 * KB, BF16)
        Kwin = av(92 * KB, 4 * KB, BF16)
        Vs = av(40 * KB, 2112, BF16)[:, 0:16 * 65].rearrange("p (t d) -> p t d", t=16)
        Vw = av(40 * KB + 2112, 2112, BF16)[:, 0:16 * 65].rearrange("p (t d) -> p t d", t=16)
        gates = av(40 * KB + 4224, 768, F32).rearrange("p (t h b) -> p t h b", t=16, h=4)
        uT = av(56 * KB, 16 * KB, BF16).rearrange("p (c t) -> p c t", c=4)
        merged = av(56 * KB, 32 * KB, BF16).rearrange("p (c t) -> p c t", c=8)

        def mixv(l):
            return mixw_d[l].rearrange("(kc p) f -> p kc f", p=128)

        def stage_cmp(l):
            kcT, vcaug, w2e, peT, pebias, hcT, crope = (sm[k] for k in ["kcT", "vcaug", "w2e", "peT", "pebias", "hcT", "crope"])
            for j in range(2):
                dma("gpsimd", w1v[0:64, j, :, :], w1_d[l, j].rearrange("l d f -> d l f"), "d_m2")
            dma("gpsimd", w2e[:], w2e_d[l], "d_m2")
            dma("gpsimd", peT[:], peT_d[l], "d_m2")
            dma("sync", crope[:], crope_d, "d_m2")
            for g in range(2):
                dma("sync", vcaug[:, g, 64:97], vcc_d, "d_m2")
            slot, sem = ring()
            sv = slot[:, 0:2048].rearrange("p (k f) -> p k f", k=8)
            dma("gpsimd", sv, mixv(l)[:, :, OFF_KC:OFF_KC + 256], sem)
            for tb in range(4):
                for idx in range(4):
                    bk = bank()
                    for kc in range(8):
                        mm(PS(bk, 0, 64), sv[:, kc, idx * 64:(idx + 1) * 64], N_(kc, tb), start=(kc == 0), stop=(kc == 7))
                    act(cmpT[0:64, idx, tb * 512:(tb + 1) * 512], PS(bk, 0, 64), AF.Copy)
            for j in range(2):
                bp = bank()
                for ll in range(32):
                    mm(PS(bp, 0, 128, 0, 1), w1v[0:64, j, ll, :], peT[0:64, j, ll:ll + 1], start=(ll == 0), stop=(ll == 31))
                vcopy(pebias[:, j:j + 1], PS(bp, 0, 128, 0, 1))
                for g in range(2):
                    bh = bank()
                    for ll in range(32):
                        mm(PS(bh, 0, 128, 0, NCMP), w1v[0:64, j, ll, :], cmpT[0:64, j * 2 + g, ll:ll + 16 * (NCMP - 1) + 1:16],
                           start=(ll == 0), stop=(ll == 31))
                    act(hcT[:, 0:NCMP], PS(bh, 0, 128, 0, NCMP), AF.Silu, bias=pebias[:, j:j + 1])
                    if j == 0:
                        b1 = bank()
                        mm(PS(b1, 0, 64, 0, NCMP), w2e[:, 0:64], hcT[:, 0:NCMP])
                        b2 = bank()
                        mm(PS(b2, 0, 64, 0, NCMP), w2e[:, 64:128], hcT[:, 0:NCMP])
                        t1 = av(SCR, 2 * KB, F32)[0:64, 0:NCMP]
                        t2 = av(SCR + 2 * KB, 2 * KB, F32)[0:64, 0:NCMP]
                        tt(t1, PS(b1, 0, 64, 0, NCMP), crope[:, 0, :], ALU.mult)
                        tt(t2, PS(b2, 0, 64, 0, NCMP), crope[:, 1, :], ALU.mult)
                        tt(kcT[:, g, 0:NCMP], t1, t2, ALU.add)
                    else:
                        b2 = bank()
                        mm(PS(b2, 0, NCMP, 0, 64), hcT[:, 0:NCMP], w2e[:, 128:192])
                        act(vcaug[0:NCMP, g, 0:64], PS(b2, 0, NCMP, 0, 64), AF.Copy)

        def stage_nsa_group(l, g, interleave):
            kcT, vcaug = sm["kcT"], sm["vcaug"]
            rden, coef, imp, impf, imp2, m8, selpen = (sm[k] for k in ["rden", "coef", "imp", "impf", "imp2", "m8", "selpen"])
            rden2, coef2 = sm["rden2"], sm["coef2"]
            mv = mixv(l)
            slot, sem = ring()
            sv = slot.rearrange("p (k f) -> p k f", k=8)
            dma("gpsimd", sv, mv[:, :, OFF_QG[g]:OFF_QG[g] + 512], sem)
            T1 = SCR
            nrot = [0]

            def rope_pair(bq, br, tb):
                i = nrot[0] % 2
                nrot[0] += 1
                t1 = av(T1 + i * 4 * KB, 2 * KB, F32)
                t2 = av(T1 + i * 4 * KB + 2 * KB, 2 * KB, F32)
                stg = av(SCR + 14 * KB + i * KB, KB, BF16)
                tt(t1, PS(bq), rope[:, 0, tb * 512:(tb + 1) * 512], ALU.mult)
                tt(t2, PS(br), rope[:, 1, tb * 512:(tb + 1) * 512], ALU.mult)
                return t1, t2, stg

            for hp in range(2):
                for tb in range(4):
                    bq = bank()
                    for kc in range(8):
                        mm(PS(bq), sv[:, kc, hp * 128:(hp + 1) * 128], N_(kc, tb), start=(kc == 0), stop=(kc == 7))
                    br = bank()
                    for kc in range(8):
                        mm(PS(br), sv[:, kc, 256 + hp * 128:256 + (hp + 1) * 128], N_(kc, tb), start=(kc == 0), stop=(kc == 7))
                    t1, t2, stg = rope_pair(bq, br, tb)
                    tt(Qaug[0:64, tb * 4:(tb + 1) * 4, 2 * hp, :], t1[0:64].rearrange("p (q t) -> p q t", q=4),
                       t2[0:64].rearrange("p (q t) -> p q t", q=4), ALU.add)
                    tt(stg[64:128], t1[64:128], t2[64:128], ALU.add)
                    dma("sync", Qaug[0:64, tb * 4:(tb + 1) * 4, 2 * hp + 1, :], stg[64:128].rearrange("p (q t) -> p q t", q=4), "d_q")
            slot, sem = ring()
            sv = slot[:, 0:2048].rearrange("p (k f) -> p k f", k=8)
            dma("gpsimd", sv, mv[:, :, OFF_KG[g]:OFF_KG[g] + 256], sem)
            for tb in range(4):
                bq = bank()
                for kc in range(8):
                    mm(PS(bq), sv[:, kc, 0:128], N_(kc, tb), start=(kc == 0), stop=(kc == 7))
                br = bank()
                for kc in range(8):
                    mm(PS(br), sv[:, kc, 128:256], N_(kc, tb), start=(kc == 0), stop=(kc == 7))
                t1, t2, stg = rope_pair(bq, br, tb)
                tt(Ksel[0:64, tb * 512:(tb + 1) * 512], t1[0:64], t2[0:64], ALU.add)
                tt(stg[64:128], t1[64:128], t2[64:128], ALU.add)
                dma("sync", Kwin[0:64, tb * 512:(tb + 1) * 512], stg[64:128], "d_q")
            memset(Vs[:, :, 64:65], 1.0)
            memset(Vw[:, :, 64:65], 1.0)
            slot, sem = ring()
            sv = slot[:, 0:8 * 140].rearrange("p (k f) -> p k f", k=8)
            dma("gpsimd", sv, mv[:, :, OFF_VG[g]:OFF_VG[g] + 140], sem)
            for tq in range(16):
                bv = bank()
                for kc in range(8):
                    mm(PS(bv, 0, 128, 0, 140), nv[:, kc, tq * 128:(tq + 1) * 128], sv[:, kc, :], start=(kc == 0), stop=(kc == 7))
                act(Vs[:, tq, 0:64], PS(bv, 0, 128, 0, 64), AF.Copy)
                act(Vw[:, tq, 0:64], PS(bv, 0, 128, 64, 128), AF.Copy)
                act(gates[:, tq, :, :], PS(bv, 0, 128, 128, 140).rearrange("p (h b) -> p h b", h=4), AF.Sigmoid)
            PT0 = SCR + 8 * KB
            npt = [0]
            nsc = [0]

            def new_pT():
                i = npt[0] % 4
                npt[0] += 1
                return av(PT0 + i * KB, KB, BF16)

            def yatok_of(qt):
                return av(PT0 + 6 * KB + (qt % 2) * KB, KB, F32).rearrange("p (h d) -> p h d", h=4)

            def cmp_qk(qt):
                ncq = min(NCMP, 8 * qt + 7)
                q64 = Qaug[0:64, qt, :, :].rearrange("p h t -> p (h t)")
                bc = bank(fixed=6)
                mm(PS(bc, 0, ncq), kcT[0:64, g, 0:ncq], q64, start=True, stop=False)
                s0 = 128 - 8 * qt
                mm(PS(bc, 0, ncq), cmk[0:8, s0:s0 + ncq], cmk[0:8, 136:136 + 512], start=False, stop=True)
                eb = av(PT0 + 5 * KB, KB, BF16)
                act(eb[0:ncq, :], PS(bc, 0, ncq), AF.Exp, scale=0.125)

            def cmp_pv(qt):
                ncq = min(NCMP, 8 * qt + 7)
                yatok = yatok_of(qt)
                eb = av(PT0 + 5 * KB, KB, BF16)
                bo = bank(fixed=6)
                for hh in range(4):
                    mm(PS(bo, 0, 128, hh * 128, hh * 128 + 97), eb[0:ncq, hh * 128:(hh + 1) * 128], vcaug[0:ncq, g, :])
                bo3 = PS(bo).rearrange("p (h c) -> p h c", h=4)
                ts(rden[:].unsqueeze(2), bo3[:, :, 64:65], 1e-30, ALU.max)
                P.op("vector", lambda e: e.reciprocal(out=rden[:], in_=rden[:]), reads=rk(rden[:]), writes=rk(rden[:]))
                ts(imp[:], bo3[:, 0, 65:97], rden[:, 0:1], ALU.mult)
                for hh in range(1, 4):
                    stt(imp[:], bo3[:, hh, 65:97], rden[:, hh:hh + 1], imp[:], ALU.mult, ALU.add)
                tt(impf[:], imp[:], fv[:, qt, 1, :], ALU.mult)
                tt(impf[:], impf[:], fv[:, qt, 0, :], ALU.add)
                P.op("vector", lambda e: e.max(out=m8[:, 0:8], in_=impf[:]), reads=rk(impf[:]), writes=rk(m8[:]))
                P.op("vector", lambda e: e.match_replace(out=imp2[:], in_to_replace=m8[:, 0:8], in_values=impf[:], imm_value=-1e30),
                     reads=rk(impf[:], m8[:]), writes=rk(imp2[:]))
                P.op("vector", lambda e: e.max(out=m8[:, 8:16], in_=imp2[:]), reads=rk(imp2[:], m8[:]), writes=rk(m8[:]))
                ts(selpen[:, 64:96], impf[:], m8[:, 15:16], ALU.is_ge, 1.0, ALU.subtract)
                tt(coef[:].unsqueeze(2), gates[:, qt, :, 0:1], rden[:].unsqueeze(2), ALU.mult)
                tt(yatok, bo3[:, :, 0:64], coef[:].unsqueeze(2).broadcast_to([128, 4, 64]), ALU.mult)

            def selpen_T(qt):
                bt = bank(fixed=6)
                transpose(PSB(bt, 0, 96, 0, 128), selpen[:, 0:96], identb[:])
                vcopy(Qaug[64:96, qt, :, :], PSB(bt, 64, 96, 0, 128).unsqueeze(1).broadcast_to([32, 4, 128]))

            BR = {2: (Kwin, 64, Vw, 4, sm["rden3"], sm["coef3"]), 1: (Ksel, 96, Vs, 5, rden2, coef2)}

            def issue(it):
                qt, br, kt = it["qt"], it["br"], it["kt"]
                Kd, krows = BR[br][0], BR[br][1]
                qrhs = Qaug[0:krows, qt, :, :].rearrange("p h t -> p (h t)")
                b = nsc[0] % 4
                nsc[0] += 1
                mi = None
                if kt == qt:
                    mi = 0
                elif br == 2 and kt == qt - 4:
                    mi = 1
                mm(PS(b), Kd[0:krows, kt * 128:(kt + 1) * 128], qrhs, start=True, stop=(mi is None))
                if mi is not None:
                    mm(PS(b), identb[:], mneg[:, mi, :], start=False, stop=True)
                return b

            def exp_tile(it, b):
                pT = av(PT0 + (npt[0] % 4) * KB, KB, BF16)
                npt[0] += 1
                act(pT, PS(b), AF.Exp, scale=0.125)
                P.prewait("tensor", rk(pT))
                return pT

            def process(it, pT):
                br, kt = it["br"], it["kt"]
                Vd, bacc = BR[br][2], BR[br][3]
                for hh in range(4):
                    mm(PS(bacc, 0, 128, hh * 128, hh * 128 + 65), pT[:, hh * 128:(hh + 1) * 128],
                       Vd[:, kt, :], start=(it["first"] and hh == 0), stop=it["last"], sgc=True)

            def fin_branch(qt, br):
                bacc, rd, cf = BR[br][3], BR[br][4], BR[br][5]
                yatok = yatok_of(qt)
                ba3 = PS(bacc).rearrange("p (h c) -> p h c", h=4)
                P.op("vector", lambda e: e.reciprocal(out=rd[:].unsqueeze(2), in_=ba3[:, :, 64:65]),
                     reads=rk(ba3[:, :, 64:65]), writes=rk(rd[:]))
                tt(cf[:].unsqueeze(2), gates[:, qt, :, br:br + 1], rd[:].unsqueeze(2), ALU.mult)
                tmp = av(SCR + 5 * KB, KB, F32).rearrange("p (h d) -> p h d", h=4)
                tt(tmp, ba3[:, :, 0:64], cf[:].unsqueeze(2).broadcast_to([128, 4, 64]), ALU.mult)
                tt(yatok, yatok, tmp, ALU.add)

            def yab_of(qt):
                return av(SCR + 6 * KB + (qt % 2) * KB, 512, BF16)

            def fin_dve(qt):
                vcopy(yab_of(qt), yatok_of(qt).rearrange("p h d -> p (h d)"))

            def fin_pe(qt):
                yab = yab_of(qt)
                by = bank(fixed=4)
                for j in range(2):
                    transpose(PSB(by, 0, 128, j * 128, (j + 1) * 128), yab[:, j * 128:(j + 1) * 128], identb[:])
                vcopy(yaT[:, 2 * g:2 * g + 2, qt * 128:(qt + 1) * 128],
                      PSB(by, 0, 128, 0, 256).rearrange("p (j t) -> p j t", j=2))

            items = []
            for qt in range(16):
                for br, kts in ((2, list(range(max(0, qt - 4), qt + 1))), (1, list(range(0, qt + 1)))):
                    for ki, kt in enumerate(kts):
                        items.append(dict(qt=qt, br=br, kt=kt, ki=ki, first=(ki == 0), last=(ki == len(kts) - 1)))
            for q0 in range(2):
                cmp_qk(q0)
                cmp_pv(q0)
                selpen_T(q0)
            LOOK = 3
            inflight = [issue(items[k]) for k in range(LOOK)]
            nexti = LOOK
            for idx, it in enumerate(items):
                qt, br = it["qt"], it["br"]
                if it["first"] and br == 2 and interleave is not None:
                    next(interleave, None)
                b = inflight.pop(0)
                pT = exp_tile(it, b)
                if nexti < len(items):
                    inflight.append(issue(items[nexti]))
                    nexti += 1
                process(it, pT)
                if br == 2 and 2 <= qt + 1 < 16:
                    if it["ki"] == 0:
                        cmp_qk(qt + 1)
                    elif it["ki"] == 1:
                        cmp_pv(qt + 1)
                if it["last"]:
                    fin_branch(qt, br)
                    if br == 1:
                        if 2 <= qt + 1 < 16:
                            selpen_T(qt + 1)
                        fin_dve(qt)
                        if qt >= 1:
                            fin_pe(qt - 1)
            fin_pe(15)

        def stage_nsa(l, interleave):
            dma("sync", rope[0:64, :, :], rope_d, "d_m3")
            dma("sync", rope[64:128, :, :], rope_d, "d_m3")
            dma("sync", Ksel[64:96, :], erows_d, "d_m3")
            memset(sm["selpen"][:, 0:64], 0.0)
            for g in range(2):
                stage_nsa_group(l, g, interleave)

        def stage_gmlp(l):
            G0 = 72 * KB
            wsT = av(G0, 2 * KB, BF16).rearrange("p (g t) -> p g t", g=8)
            bsT = av(G0 + 2 * KB, 2 * KB, F32).rearrange("p (j t) -> p j t", j=4)
            lng = av(G0 + 4 * KB, 2 * KB, F32)
            lnb = av(G0 + 6 * KB, 2 * KB, F32)
            dma("gpsimd", wsT, wsT_d[l], "d_m1")
            dma("sync", bsT, bsT_d[l], "d_m1")
            dma("sync", lng, gmln_d[l, 0].partition_broadcast(128), "d_m1")
            dma("sync", lnbT[:], lnbT_d[l], "d_m1")
            tt(wsT, wsT, tri[:, 0, :].unsqueeze(1).broadcast_to([128, 8, 128]), ALU.mult)
            BTp = lnb.rearrange("p (j t) -> p j t", j=4)
            bp0 = bank_pair()
            bq4 = psum[:, bp0 * 512:(bp0 + 2) * 512].rearrange("p (j a t) -> p j a t", j=4, a=2)
            for j in range(4):
                for a in range(2):
                    mm(bq4[:, j, a, :], onesb[:], wsT[:, 2 * j + a, :])
            for j in range(4):
                stt(BTp[0:64, j, :], bq4[0:64, j, 0, :], lnbT[0:64, j:j + 1], bsT[0:64, j, :], ALU.mult, ALU.add)
                stt(BTp[64:128, j, :], bq4[64:128, j, 1, :], lnbT[64:128, j:j + 1], bsT[64:128, j, :], ALU.mult, ALU.add)
            mv = mixv(l)
            slot, sem = ring()
            su = slot.rearrange("p (k f) -> p k f", k=8)
            dma("gpsimd", su, mv[:, :, OFF_U:OFF_U + 512], sem)
            for fc in range(4):
                for tb in range(4):
                    bk = bank()
                    for kc in range(8):
                        mm(PS(bk), su[:, kc, fc * 128:(fc + 1) * 128], N_(kc, tb), start=(kc == 0), stop=(kc == 7))
                    act(uT[:, fc, tb * 512:(tb + 1) * 512], PS(bk), AF.Gelu_apprx_tanh)
            slot, sem = ring()
            svv = slot.rearrange("p (k f) -> p k f", k=8)
            dma("gpsimd", svv, mv[:, :, OFF_V:OFF_V + 512], sem)
            bnst, bnmv, lnr = sm["bnst"], sm["bnmv"], sm["lnr"]

            def vproj(tq):
                bk = bank()
                for kc in range(8):
                    mm(PS(bk), nv[:, kc, tq * 128:(tq + 1) * 128], svv[:, kc, :], start=(kc == 0), stop=(kc == 7))
                return bk

            bk_next = vproj(0)
            for tq in range(16):
                bk = bk_next
                if tq + 1 < 16:
                    bk_next = vproj(tq + 1)
                vg = av(G0 + 8 * KB + (tq % 2) * 2 * KB, 2 * KB, F32)
                act(vg, PS(bk), AF.Gelu_apprx_tanh)
                P.op("vector", lambda e, vg=vg: e.bn_stats(out=bnst[:], in_=vg), reads=rk(vg), writes=rk(bnst[:]))
                P.op("vector", lambda e: e.bn_aggr(out=bnmv[:], in_=bnst[:]), reads=rk(bnst[:]), writes=rk(bnmv[:]))
                ts(lnr[:], bnmv[:, 1:2], EPS, ALU.add)
                tt(lnr[:], lnr[:], nhalf[:, 0:1], ALU.pow, eng="gpsimd")
                ts(vg, vg, bnmv[:, 0:1], ALU.subtract, lnr[:, 0:1], ALU.mult)
                vtok = av(G0 + 12 * KB + (tq % 2) * KB, KB, BF16)
                tt(vtok, vg, lng, ALU.mult)
                bp = bank_pair()
                bp4 = psum[:, bp * 512:(bp + 2) * 512].rearrange("p (j a t) -> p j a t", j=4, a=2)
                for j in range(4):
                    for a in range(2):
                        mm(bp4[:, j, a, :], vtok[:, j * 128:(j + 1) * 128], wsT[:, 2 * j + a, :])
                tmp = av(G0 + 14 * KB, 2 * KB, F32).rearrange("p (j t) -> p j t", j=4)
                tt(tmp[0:64], bp4[0:64, :, 0, :], BTp[0:64], ALU.add)
                tt(tmp[64:128], bp4[64:128, :, 1, :], BTp[64:128], ALU.add)
                tt(ybT[:, :, tq * 128:(tq + 1) * 128], tmp, uT[:, :, tq * 128:(tq + 1) * 128], ALU.mult)

        def stage_merge(l):
            mv = mixv(l)
            pav = proj_a_d[l].rearrange("(kc p) f -> p kc f", p=128)
            pbv = proj_b_d[l].rearrange("(kc p) f -> p kc f", p=128)
            X0 = 88 * KB
            nx = [0]
            for fc in range(8):
                slot, sem = ring()
                sg = slot[:, 0:2048].rearrange("p (k f) -> p k f", k=8)
                spa = slot[:, 2048:2560].rearrange("p (k f) -> p k f", k=4)
                spb = slot[:, 2560:3072].rearrange("p (k f) -> p k f", k=4)
                dma("gpsimd", sg, mv[:, :, OFF_GAB + fc * 256:OFF_GAB + (fc + 1) * 256], sem)
                dma("gpsimd", spa, pav[:, :, fc * 128:(fc + 1) * 128], sem)
                dma("gpsimd", spb, pbv[:, :, fc * 128:(fc + 1) * 128], sem)
                for tb in range(4):
                    bA = bank()
                    for kc in range(8):
                        mm(PS(bA), sg[:, kc, 0:128], N_(kc, tb), start=(kc == 0), stop=(kc == 7))
                    bB = bank()
                    for kc in range(8):
                        mm(PS(bB), sg[:, kc, 128:256], N_(kc, tb), start=(kc == 0), stop=(kc == 7))
                    bPA = bank()
                    for kc in range(4):
                        mm(PS(bPA), spa[:, kc, :], yaT[:, kc, tb * 512:(tb + 1) * 512], start=(kc == 0), stop=(kc == 3))
                    bPB = bank()
                    for kc in range(4):
                        mm(PS(bPB), spb[:, kc, :], ybT[:, kc, tb * 512:(tb + 1) * 512], start=(kc == 0), stop=(kc == 3))
                    i = nx[0] % 2
                    nx[0] += 1
                    sga = av(X0 + i * 4 * KB, 2 * KB, F32)
                    sgb = av(X0 + i * 4 * KB + 2 * KB, 2 * KB, F32)
                    act(sga, PS(bA), AF.Sigmoid)
                    act(sgb, PS(bB), AF.Sigmoid)
                    tt(sga, sga, PS(bPA), ALU.mult)
                    tt(sgb, sgb, PS(bPB), ALU.mult)
                    tt(merged[:, fc, tb * 512:(tb + 1) * 512], sga, sgb, ALU.add)
            if STOP == "merged":
                return
            wov = w_out_d[l].rearrange("(kc p) f -> p kc f", p=128)
            for half in range(2):
                slot, sem = ring()
                sv = slot.rearrange("p (k f) -> p k f", k=8)
                dma("gpsimd", sv, wov[:, :, half * 512:(half + 1) * 512], sem)
                for fl in range(4):
                    fo = half * 4 + fl
                    for tb in range(4):
                        bk = bank()
                        for kc in range(8):
                            mm(PS(bk), sv[:, kc, fl * 128:(fl + 1) * 128], merged[:, kc, tb * 512:(tb + 1) * 512],
                               start=(kc == 0), stop=(kc == 7))
                        stt(H(fo, tb), PS(bk), der[l][:, 1, 8 + fo:9 + fo], H(fo, tb), ALU.mult, ALU.add)

        def dump_bf16(view3, nchunks):
            for c in range(nchunks):
                for tb in range(4):
                    vcopy(H(c, tb), view3[:, c, tb * 512:(tb + 1) * 512])

        def program():
            mod0 = mod_steps(0)
            for _ in range(6 if STOP != "mod" else 18):
                next(mod0)
            if STOP == "mod":
                vcopy(hv[:, 0, 0:72], modT[0][:])
                vcopy(hv[:, 0, 72:120], der[0][:].rearrange("p a b -> p (a b)"))
                return
            for l in range(2):
                norm_mod(l, 0)
                if STOP == "n0" and l == 0:
                    dump_bf16(nv, 8)
                    return
                ffn(l, 0, 0, mod0 if l == 0 else None)
                if l == 0:
                    for _ in mod0:
                        pass
                if STOP == "h1" and l == 0:
                    return
                norm_mod(l, 1)
                stage_cmp(l)
                if STOP == "cmp" and l == 0:
                    vcopy(hv[0:64, 0, 0:256], sm["kcT"][:].rearrange("p a b -> p (a b)"))
                    vcopy(hv[:, 1, 0:194], sm["vcaug"][:].rearrange("p a b -> p (a b)"))
                    return
                inter = mod_steps(1) if l == 0 else None
                stage_nsa(l, inter)
                if inter is not None:
                    for _ in inter:
                        pass
                if STOP == "ya" and l == 0:
                    dump_bf16(yaT, 4)
                    return
                stage_gmlp(l)
                if STOP == "yb" and l == 0:
                    dump_bf16(ybT, 4)
                    return
                stage_merge(l)
                if STOP == "merged" and l == 0:
                    dump_bf16(merged, 8)
                    return
                if STOP == "h2" and l == 0:
                    return
                norm_mod(l, 2)
                ffn(l, 1, 2)
                if STOP == "h3" and l == 0:
                    return
            final_norm()

        program()
        outv = outT_d.rearrange("(c p) t -> p c t", p=128)
        for tb in range(4):
            dma("sync", outv[:, :, tb * 512:(tb + 1) * 512], hv[:, :, tb * 512:(tb + 1) * 512], "d_o")
        P.final_wait("sync", ["d_o"])
        P.emit(block, sems)
    return nc


def _consts():
    inv = 1.0 / (10000.0 ** (np.arange(0, 64, 2, dtype=np.float32) / 64.0))
    pos = np.arange(S, dtype=np.float32)
    ang = pos[:, None] * inv[None, :]
    cos, sin = np.cos(ang).astype(np.float32), np.sin(ang).astype(np.float32)
    rope = np.stack([np.concatenate([cos, cos], 1).T, np.concatenate([-sin, sin], 1).T], 1)
    cend = (np.arange(NCMP) * 16 + 31).astype(np.float32)
    angc = cend[:, None] * inv[None, :]
    cc, cs = np.cos(angc).astype(np.float32), np.sin(angc).astype(np.float32)
    crope = np.stack([np.concatenate([cc, cc], 1).T, np.concatenate([-cs, cs], 1).T], 1)
    cmask = np.zeros((128, S), np.float32)
    cmask[:NCMP] = (cend[:, None] <= pos[None, :]).astype(np.float32)
    k = np.arange(128)
    triS = (k[:, None] <= k[None, :]).astype(np.float32)
    triW = (k[:, None] > k[None, :]).astype(np.float32)
    tri = np.stack([triS, triW], 1)
    mneg = np.stack([np.tile((1.0 - triS) * -30000.0, (1, 4)), np.tile((1.0 - triW) * -30000.0, (1, 4))], 1)
    cmk = np.zeros((8, 136 + 512), np.float32)
    for kk in range(8):
        cmk[kk, kk + 127] = 1.0
        cmk[kk, 136:] = np.tile(np.where(np.arange(128) < 16 * kk + 15, -30000.0, 0.0), 4)
    erows = (np.arange(S)[None, :] // 64 == np.arange(32)[:, None]).astype(np.float32) * 32768.0
    starts = np.arange(NCMP) * 16
    sel_start = np.arange(32) * 64
    ov = np.clip(np.minimum(starts[:, None] + 32, sel_start[None, :] + 64) - np.maximum(starts[:, None], sel_start[None, :]),
                 0, None).astype(np.float32) / 32.0
    vcc = np.zeros((128, 33), np.float32)
    vcc[:, 0] = 1.0
    vcc[:NCMP, 1:] = ov
    t = np.arange(S)
    cur = t // 64
    blk = np.arange(32)
    forced = (blk[None, :] == 0) | (blk[None, :] == cur[:, None]) | (blk[None, :] == cur[:, None] - 1)
    valid = blk[None, :] <= cur[:, None]
    fadd = np.where(forced, 1e4, np.where(valid, 0.0, -1e4)).astype(np.float32)
    vnf = (valid & ~forced).astype(np.float32)
    fv = np.stack([fadd, vnf], 1).reshape(16, 128, 2, 32).transpose(1, 0, 2, 3)
    bf = ml_dtypes.bfloat16
    return {
        "c_rope": np.ascontiguousarray(rope, np.float32), "c_crope": np.ascontiguousarray(crope, np.float32),
        "c_tri": np.ascontiguousarray(tri).astype(bf), "c_mneg": np.ascontiguousarray(mneg).astype(bf), "c_cmk": cmk.astype(bf), "c_erows": erows.astype(bf),
        "c_vcc": vcc.astype(bf), "c_fv": np.ascontiguousarray(fv, np.float32), "c_identb": np.eye(128, dtype=np.float32).astype(bf),
    }


def _prep_shared(inp):
    perm = np.concatenate([np.arange(32, 64), np.arange(0, 32)])
    mw = inp["mix_w_in"]

    def kvcol(s, g):
        return 512 + (s * 2 + g) * 64 + np.arange(64)

    cols = [np.arange(1304, 1816), np.arange(1816, 2328), np.arange(512, 768)]
    for g in range(2):
        qh = [g * 256 + hh * 64 + np.arange(64) for hh in range(4)]
        cols += qh + [q[perm] for q in qh]
        cols += [kvcol(2, g), kvcol(4, g), kvcol(2, g)[perm], kvcol(4, g)[perm]]
        cols += [kvcol(3, g), kvcol(5, g), 1280 + g * 12 + np.arange(12)]
    for fc in range(8):
        cols += [2328 + fc * 128 + np.arange(128), 3352 + fc * 128 + np.arange(128)]
    cols = np.concatenate(cols)
    assert cols.shape[0] == NEXT
    mixw = np.ascontiguousarray(mw[:, :, cols])
    w2 = inp["cmp_w2"]
    w2e = np.ascontiguousarray(np.concatenate([w2[:, 0], w2[:, 0][:, :, perm], w2[:, 1]], axis=2))
    sh = {
        "ada_w": inp["ada_w"],
        "ada_bT": np.ascontiguousarray(inp["ada_b"].reshape(2, 72, 128).transpose(2, 0, 1)),
        "normg": np.ascontiguousarray(np.concatenate([inp["norm_g"].reshape(6, 8, 128), inp["final_g"].reshape(1, 8, 128)], 0)
                                      .reshape(56, 128).T),
        "ffn_w_in": inp["ffn_w_in"], "ffn_w_out": inp["ffn_w_out"], "mixw": mixw,
        "cmp_peT": np.ascontiguousarray(inp["cmp_pe"].transpose(0, 3, 1, 2)),
        "cmp_w1": inp["cmp_w1"], "cmp_w2e": w2e,
        "gm_ln": np.ascontiguousarray(np.stack([inp["gm_ln_g"], inp["gm_ln_b"]], 1)),
        "gm_wsT": np.ascontiguousarray(inp["gm_ws"].transpose(0, 3, 1, 2)),
        "gm_lnbT": np.ascontiguousarray(inp["gm_ln_b"].reshape(2, 4, 128).transpose(0, 2, 1)),
        "gm_bsT": np.ascontiguousarray(np.repeat(inp["gm_bs"], 64, axis=1).reshape(2, 4, 128, 128).transpose(0, 2, 1, 3)),
        "proj_a": inp["proj_a"], "proj_b": inp["proj_b"], "w_out": inp["w_out"],
    }
    sh.update(_consts())
    return sh


def kernel(**inputs):
    inp = {k: np.asarray(v) for k, v in inputs.items()}
    shared = _prep_shared(inp)
    ncores = int(os.environ.get("MK_NCORES", "8"))
    in_maps = []
    for b in range(ncores):
        m = dict(shared)
        m["xT"] = np.ascontiguousarray(inp["x"][b].T)
        m["cT"] = np.ascontiguousarray(inp["c"][b].reshape(8, 128).T)
        in_maps.append(m)
    nc = build_program()
    res = run_bass_kernel_spmd(nc, in_maps, core_ids=list(range(ncores)))
    out = np.stack([np.ascontiguousarray(r["outT"].T) for r in res.results], 0)
    return out.astype(np.float32)
```

```python
import os
import numpy as np
import ml_dtypes
from contextlib import ExitStack
import concourse.bass as bass
import concourse.mybir as mybir
from concourse.bass_utils import run_bass_kernel_spmd

F32 = mybir.dt.float32
BF16 = mybir.dt.bfloat16
AF = mybir.ActivationFunctionType
ALU = mybir.AluOpType
DS = {F32: 4, BF16: 2}

S = 2048
D = 1024
DFF = 2816
NCMP = 127
EPS = 1e-6
ENGS = ["sync", "scalar", "vector", "gpsimd", "tensor"]
GRAN = {"hT": 2048, "nT": 1024, "arena": 1024, "ps": 2048}

OFF_U = 0
OFF_V = 512
OFF_KC = 1024
OFF_QG = [1280, 1280 + 908]
OFF_KG = [1280 + 512, 1280 + 908 + 512]
OFF_VG = [1280 + 768, 1280 + 908 + 768]
OFF_GAB = 1280 + 2 * 908
NEXT = OFF_GAB + 2048

ARENA_KIB = 96
STOP = os.environ.get("MK_STOP", "")


def ap_keys(ap):
    name = ap.tensor.name
    if name not in GRAN:
        return [name]
    g = GRAN[name]
    ds = DS[ap.dtype]
    pat = ap.ap
    pstep = pat[0][0]
    off = (ap.offset % pstep) if pstep else ap.offset
    dims = [(s, n) for (s, n) in pat[1:]]
    res = set()

    def rec(i, base):
        if i >= len(dims):
            res.add(base * ds // g)
            return
        s, n = dims[i]
        if i == len(dims) - 1:
            lo = base
            hi = base + (n - 1) * abs(s)
            for ch in range(lo * ds // g, (hi * ds + ds - 1) // g + 1):
                res.add(ch)
        else:
            if s == 0:
                n = 1
            for j in range(n):
                rec(i + 1, base + j * s)

    rec(0, off)
    if name == "arena":
        p0 = ap.offset // pstep if pstep else 0
        q0, q1 = p0 // 32, (p0 + pat[0][1] - 1) // 32
        return [f"{name}{c}q{q}" for c in sorted(res) for q in range(q0, q1 + 1)]
    return [f"{name}{c}" for c in sorted(res)]


class Prog:
    def __init__(self):
        self.ops = {e: [] for e in ENGS}
        self.cnt = {}
        self.last_w = {}
        self.readers = {}
        self.seen = {e: {} for e in ENGS}
        self.dma_sems = set()

    def op(self, eng, fn, reads=(), writes=(), dma=None):
        need = {}

        def add(ev):
            if ev is None:
                return
            s, v = ev
            if s in self.dma_sems:
                v = self.cnt[s]
            if need.get(s, 0) < v:
                need[s] = v

        for k in reads:
            add(self.last_w.get(k))
        for k in writes:
            add(self.last_w.get(k))
            for s, v in self.readers.get(k, {}).items():
                add((s, v))
        own = "e_" + eng
        waits = []
        for s, v in need.items():
            if s == own and eng == "tensor":
                continue
            if self.seen[eng].get(s, 0) >= v:
                continue
            self.seen[eng][s] = v
            waits.append((s, v))
        if dma is None:
            sem, inc = own, 1
        else:
            sem, inc = dma, 16
            self.dma_sems.add(dma)
        self.cnt[sem] = self.cnt.get(sem, 0) + inc
        ev = (sem, self.cnt[sem])
        for k in writes:
            self.last_w[k] = ev
            self.readers[k] = {}
        for k in reads:
            d = self.readers.setdefault(k, {})
            d[sem] = max(d.get(sem, 0), ev[1])
        self.ops[eng].append((waits, fn, sem, inc))
        return ev

    def final_wait(self, eng, sems_):
        self.ops[eng].append(([(s, self.cnt[s]) for s in sems_], None, None, 0))

    def emit(self, block, sems):
        for e in ENGS:
            ops = self.ops[e]

            def body(eng, ops=ops):
                for waits, fn, sem, inc in ops:
                    for s, v in waits:
                        eng.wait_ge(sems[s], v)
                    if fn is not None:
                        fn(eng).then_inc(sems[sem], inc)

            getattr(block, e)(body)


def build_program():
    nc = bass.Bass("TRN2", target_bir_lowering=False)
    P = Prog()

    def dram(name, shape, dt=F32, kind="ExternalInput"):
        return nc.dram_tensor(name, list(shape), dt, kind=kind).ap()

    xT_d = dram("xT", [D, S])
    cT_d = dram("cT", [128, 8])
    ada_w_d = dram("ada_w", [2, D, 9 * D])
    ada_bT_d = dram("ada_bT", [128, 2, 72])
    normg_d = dram("normg", [128, 56])
    ffn_w_in_d = dram("ffn_w_in", [2, 2, D, 2 * DFF])
    ffn_w_out_d = dram("ffn_w_out", [2, 2, DFF, D])
    mixw_d = dram("mixw", [2, D, NEXT])
    peT_d = dram("cmp_peT", [2, 64, 2, 32])
    w1_d = dram("cmp_w1", [2, 2, 32, 64, 128])
    w2e_d = dram("cmp_w2e", [2, 128, 192])
    gmln_d = dram("gm_ln", [2, 2, 512])
    wsT_d = dram("gm_wsT", [2, 128, 8, 128])
    bsT_d = dram("gm_bsT", [2, 128, 4, 128])
    lnbT_d = dram("gm_lnbT", [2, 128, 4])
    proj_a_d = dram("proj_a", [2, 512, D])
    proj_b_d = dram("proj_b", [2, 512, D])
    w_out_d = dram("w_out", [2, D, D])
    rope_d = dram("c_rope", [64, 2, S])
    crope_d = dram("c_crope", [64, 2, NCMP])
    tri_d = dram("c_tri", [128, 2, 128], BF16)
    mneg_d = dram("c_mneg", [128, 2, 512], BF16)
    erows_d = dram("c_erows", [32, S], BF16)
    vcc_d = dram("c_vcc", [128, 33], BF16)
    fv_d = dram("c_fv", [128, 16, 2, 32])
    identb_d = dram("c_identb", [128, 128], BF16)
    outT_d = dram("outT", [D, S], F32, kind="ExternalOutput")

    es = ExitStack()
    with es:
        def sb(name, shape, dt):
            return es.enter_context(nc.sbuf_tensor(name, list(shape), dt))

        hT = sb("hT", [128, 8 * S], F32)
        nT = sb("nT", [128, 8 * S], BF16)
        arena = sb("arena", [128, ARENA_KIB * 256], F32)
        psum = es.enter_context(nc.psum_tensor("ps", [128, 4096], F32))
        tri = sb("tri", [128, 2, 128], BF16)
        mneg = sb("mneg", [128, 2, 512], BF16)
        fv = sb("fv", [128, 16, 2, 32], F32)
        identb = sb("identb", [128, 128], BF16)
        onesm = sb("onesm", [128, 128], BF16)
        onesb = sb("onesb", [128, 128], BF16)
        lnbT = sb("lnbT", [128, 4], F32)
        epsc = sb("epsc", [128, 1], F32)
        nhalf = sb("nhalf", [128, 1], F32)
        cT = sb("cTs", [128, 8], F32)
        scb = sb("scb", [128, 8], BF16)
        ada_bT = sb("ada_bTs", [128, 2, 72], F32)
        normg = sb("normgs", [128, 56], F32)
        modT = [sb(f"modT{l}", [128, 72], F32) for l in range(2)]
        der = [sb(f"der{l}", [128, 3, 16], F32) for l in range(2)]
        sm = {}
        for nm, shp, dt in [("rden", [128, 4], F32), ("coef", [128, 4], F32), ("imp", [128, 32], F32),
                            ("impf", [128, 32], F32), ("imp2", [128, 32], F32), ("m8", [128, 16], F32),
                            ("selpen", [128, 96], BF16), ("bnst", [128, 6], F32), ("bnmv", [128, 2], F32),
                            ("lnr", [128, 1], F32), ("pebias", [128, 2], F32), ("peT", [64, 2, 32], BF16),
                            ("w2e", [128, 192], BF16), ("hcT", [128, 128], BF16), ("kcT", [64, 2, 128], BF16),
                            ("vcaug", [128, 2, 97], BF16), ("crope", [64, 2, NCMP], F32),
                            ("rden2", [128, 4], F32), ("coef2", [128, 4], F32),
                            ("rden3", [128, 4], F32), ("coef3", [128, 4], F32)]:
            sm[nm] = sb(nm, shp, dt)

        sem_names = ["e_" + e for e in ENGS] + ["d_r0", "d_r1", "d_r2", "d_c", "d_x", "d_o", "d_q", "d_m1s", "d_m1h", "d_m2s", "d_m2h", "d_m3h"]
        sems = {n: es.enter_context(nc.semaphore(n)) for n in sem_names}
        block = es.enter_context(nc.Block())

        hv = hT[:].rearrange("p (c t) -> p c t", c=8)
        nv = nT[:].rearrange("p (c t) -> p c t", c=8)

        def H(c, tb):
            return hv[:, c, tb * 512:(tb + 1) * 512]

        def N_(c, tb):
            return nv[:, c, tb * 512:(tb + 1) * 512]

        def av(off_b, size_b, dt):
            a = arena[:, off_b // 4:(off_b + size_b) // 4]
            return a.bitcast(BF16) if dt == BF16 else a

        KB = 1024

        def rk(*aps):
            ks = []
            for a in aps:
                if a is None or isinstance(a, (int, float)):
                    continue
                ks += ap_keys(a)
            return ks

        def mm(out, lhsT, rhs, start=True, stop=True, sgc=False):
            P.op("tensor", lambda e: e.matmul(out, lhsT=lhsT, rhs=rhs, start=start, stop=stop, skip_group_check=sgc),
                 reads=rk(lhsT, rhs), writes=rk(out))

        def transpose(out, in_, ident):
            P.op("tensor", lambda e: e.transpose(out, in_, ident), reads=rk(in_, ident), writes=rk(out))

        def act(out, in_, func, bias=None, scale=None):
            kw = {}
            if bias is not None:
                kw["bias"] = bias
            if scale is not None:
                kw["scale"] = scale
            P.op("scalar", lambda e: e.activation(out=out, in_=in_, func=func, **kw),
                 reads=rk(in_, bias, scale), writes=rk(out))

        def tt(out, in0, in1, op, eng="vector"):
            P.op(eng, lambda e: e.tensor_tensor(out=out, in0=in0, in1=in1, op=op), reads=rk(in0, in1), writes=rk(out))

        def ts(out, in0, s1, op0, s2=None, op1=None, eng="vector"):
            if op1 is None:
                P.op(eng, lambda e: e.tensor_scalar(out=out, in0=in0, scalar1=s1, scalar2=None, op0=op0),
                     reads=rk(in0, s1), writes=rk(out))
            else:
                P.op(eng, lambda e: e.tensor_scalar(out=out, in0=in0, scalar1=s1, scalar2=s2, op0=op0, op1=op1),
                     reads=rk(in0, s1, s2), writes=rk(out))

        def stt(out, in0, scalar, in1, op0, op1):
            P.op("vector", lambda e: e.scalar_tensor_tensor(out=out, in0=in0, scalar=scalar, in1=in1, op0=op0, op1=op1),
                 reads=rk(in0, scalar, in1), writes=rk(out))

        def vcopy(out, in_, eng="vector"):
            P.op(eng, lambda e: e.tensor_copy(out=out, in_=in_), reads=rk(in_), writes=rk(out))

        def memset(ap, val, eng="vector"):
            P.op(eng, lambda e: e.memset(ap, val), writes=rk(ap))

        def dma(eng, out, in_, sem, **kw):
            rd = rk(in_) if in_.tensor.name in SBN else []
            wr = rk(out) if out.tensor.name in SBN else [out.tensor.name]
            if sem.startswith("d_m"):
                sem = sem + ("s" if eng == "gpsimd" else "h")
            return P.op(eng, lambda e: e.dma_start(out=out, in_=in_, **kw), reads=rd, writes=wr, dma=sem)

        SBN = set(["hT", "nT", "arena", "tri", "mneg", "fv", "identb", "onesm", "onesb", "lnbT", "epsc", "nhalf", "cTs", "scb", "ada_bTs", "normgs",
                   "modT0", "modT1", "der0", "der1"] + list(sm.keys()))

        bstate = {"next": 0, "held": set()}

        def bank(hold=False, fixed=None):
            if fixed is not None:
                if hold:
                    bstate["held"].add(fixed)
                return fixed
            for _ in range(16):
                i = bstate["next"]
                bstate["next"] = (i + 1) % 8
                if i not in bstate["held"]:
                    if hold:
                        bstate["held"].add(i)
                    return i
            raise RuntimeError("no psum bank")

        def bank_pair():
            for _ in range(16):
                i = bstate["next"]
                if i % 2 == 1:
                    i = (i + 1) % 8
                bstate["next"] = (i + 2) % 8
                if i not in bstate["held"] and (i + 1) not in bstate["held"]:
                    return i
            raise RuntimeError("no psum bank pair")

        def release(i):
            bstate["held"].discard(i)

        def PS(i, p0=0, p1=128, c0=0, c1=512):
            return psum[p0:p1, i * 512 + c0:i * 512 + c1]

        def PSB(i, p0, p1, c0, c1):
            return psum[p0:p1, i * 512:(i + 1) * 512].bitcast(BF16)[:, c0:c1]

        rstate = {"next": 0}

        def ring():
            i = rstate["next"]
            rstate["next"] = (i + 1) % 3
            return av(i * 8 * KB, 8 * KB, BF16), f"d_r{i}"

        for c in range(8):
            dma("sync", hv[:, c, :], xT_d[c * 128:(c + 1) * 128, :], "d_x")
        for dst, src in [(cT, cT_d), (ada_bT, ada_bT_d), (normg, normg_d), (tri, tri_d), (mneg, mneg_d),
                         (fv, fv_d), (identb, identb_d)]:
            dma("sync", dst[:], src, "d_c")
        memset(onesm[:], 1.0 / 1024.0)
        memset(epsc[:], EPS)
        memset(onesb[:], 1.0)
        memset(nhalf[:], -0.5)
        act(scb[:], cT[:], AF.Silu)

        def mod_steps(l):
            bm = bank(hold=True, fixed=7)
            awv = ada_w_d[l].rearrange("(kc p) f -> p kc f", p=128)
            for s in range(18):
                slot, sem = ring()
                sv = slot.rearrange("p (k f) -> p k f", k=8)
                dma("gpsimd", sv, awv[:, :, s * 512:(s + 1) * 512], sem)
                for fc in range(4):
                    j = s * 4 + fc
                    for kc in range(8):
                        mm(PS(bm, 0, 128, j, j + 1), sv[:, kc, fc * 128:(fc + 1) * 128], scb[:, kc:kc + 1],
                           start=(kc == 0), stop=(kc == 7))
                if s % 6 == 5:
                    sub = s // 6
                    tt(modT[l][:, sub * 24:(sub + 1) * 24], PS(bm, 0, 128, sub * 24, (sub + 1) * 24),
                       ada_bT[:, l, sub * 24:(sub + 1) * 24], ALU.add)
                    stt(der[l][:, sub, 0:8], modT[l][:, (sub * 3 + 1) * 8:(sub * 3 + 2) * 8], 1.0,
                        normg[:, (l * 3 + sub) * 8:(l * 3 + sub + 1) * 8], ALU.add, ALU.mult)
                    ts(der[l][:, sub, 8:16], modT[l][:, (sub * 3 + 2) * 8:(sub * 3 + 3) * 8],
                       1.0 if sub == 1 else 0.5, ALU.mult)
                    if s == 17:
                        release(bm)
                yield

        NSCR = 84 * KB

        def rms_stats(tb, rstd, all_act=False):
            bk = bank()
            for c in range(8):
                sq = av(NSCR + (c % 4) * KB, KB, BF16)
                if all_act or c in (0, 2, 5, 7):
                    act(sq, H(c, tb), AF.Square)
                else:
                    tt(sq, H(c, tb), H(c, tb), ALU.mult)
                mm(PS(bk), onesm[:], sq, start=(c == 0), stop=(c == 7))
            act(rstd, PS(bk), AF.Ln, bias=epsc[:, 0:1])
            act(rstd, rstd, AF.Exp, scale=-0.5)

        def norm_mod(l, sub):
            for tb in range(4):
                rstd = av(NSCR + 4 * KB + (tb % 2) * 2 * KB, 2 * KB, F32)
                rms_stats(tb, rstd)
                for c in range(8):
                    tmp = av(NSCR + 8 * KB + (c % 2) * 2 * KB, 2 * KB, F32)
                    if False:
                        pass
                    else:
                        tt(tmp, H(c, tb), rstd, ALU.mult)
                        act(N_(c, tb), tmp, AF.Identity, bias=modT[l][:, sub * 24 + c:sub * 24 + c + 1],
                            scale=der[l][:, sub, c:c + 1])

        def final_norm():
            for tb in range(4):
                rstd = av(NSCR + 4 * KB + (tb % 2) * 2 * KB, 2 * KB, F32)
                rms_stats(tb, rstd, all_act=True)
                for c in range(8):
                    stt(H(c, tb), H(c, tb), normg[:, 48 + c:49 + c], rstd, ALU.mult, ALU.mult)

        def ffn(l, i, sub, interleave=None):
            winv = ffn_w_in_d[l, i].rearrange("(kc p) f -> p kc f", p=128)
            woutv = ffn_w_out_d[l, i].rearrange("(kc p) f -> p kc f", p=128)
            gbuf = av(24 * KB, 48 * KB, BF16).rearrange("p (c t) -> p c t", c=12)
            SA = 72 * KB
            nsa_ = [0]
            def load_pair(ca, il=True):
                if il and interleave is not None:
                    next(interleave, None)
                slot, sem = ring()
                sv = slot.rearrange("p (k a f) -> p k a f", k=8, a=2)
                dma("gpsimd", sv[:, :, 0, :], winv[:, :, ca * 128:ca * 128 + 256], sem)
                dma("gpsimd", sv[:, :, 1, :], winv[:, :, DFF + ca * 128:DFF + ca * 128 + 256], sem)
                return sv

            def pair_tb(sv, gi, cc, tb):
                bA = bank()
                for kc in range(8):
                    mm(PS(bA), sv[:, kc, 0, cc * 128:(cc + 1) * 128], N_(kc, tb), start=(kc == 0), stop=(kc == 7))
                bB = bank()
                for kc in range(8):
                    mm(PS(bB), sv[:, kc, 1, cc * 128:(cc + 1) * 128], N_(kc, tb), start=(kc == 0), stop=(kc == 7))
                sa = av(SA + (nsa_[0] % 3) * 2 * KB, 2 * KB, F32)
                nsa_[0] += 1
                act(sa, PS(bA), AF.Silu)
                tt(gbuf[:, gi, tb * 512:(tb + 1) * 512], sa, PS(bB), ALU.mult)

            for (c0, c1) in [(0, 12), (12, 22)]:
                npairs = (c1 - c0) // 2
                jp0 = 0
                if c0 == 0:
                    svs = [load_pair(c0 + 2 * jp, il=False) for jp in range(3)]
                    for tb in range(4):
                        for jp in range(3):
                            for cc in range(2):
                                pair_tb(svs[jp], 2 * jp + cc, cc, tb)
                    jp0 = 3
                for jp in range(jp0, npairs):
                    ca = c0 + 2 * jp
                    sv = load_pair(ca)
                    for cc in range(2):
                        gi = ca + cc - c0
                        for tb in range(4):
                            pair_tb(sv, gi, cc, tb)
                nk = c1 - c0
                for fp in range(4):
                    if interleave is not None:
                        next(interleave, None)
                    slot, sem = ring()
                    sv = slot[:, 0:nk * 256].rearrange("p (k f) -> p k f", k=nk)
                    dma("gpsimd", sv, woutv[:, c0:c1, fp * 256:(fp + 1) * 256], sem)
                    for fl in range(2):
                        fo = fp * 2 + fl
                        for tb in range(4):
                            bk = bank()
                            for k in range(nk):
                                mm(PS(bk), sv[:, k, fl * 128:(fl + 1) * 128], gbuf[:, k, tb * 512:(tb + 1) * 512],
                                   start=(k == 0), stop=(k == nk - 1))
                            stt(H(fo, tb), PS(bk), der[l][:, sub, 8 + fo:9 + fo], H(fo, tb), ALU.mult, ALU.add)

        yaT = av(24 * KB, 16 * KB, BF16).rearrange("p (c t) -> p c t", c=4)
        ybT = av(40 * KB, 16 * KB, BF16).rearrange("p (c t) -> p c t", c=4)
        SCR = 40 * KB
        cmpT = av(56 * KB, 16 * KB, BF16).rearrange("p (i t) -> p i t", i=4)
        w1v = av(72 * KB, 16 * KB, BF16).rearrange("p (j l f) -> p j l f", j=2, l=32)
        rope = av(56 * KB, 16 * KB, F32).rearrange("p (a t) -> p a t", a=2)
        Qaug = av(72 * KB, 16 * KB, BF16).rearrange("p (q h t) -> p q h t", q=16, h=4)
        Ksel = av(88 * KB, 4 * KB, BF16)
        Kwin = av(92 * KB, 4 * KB, BF16)
        Vs = av(40 * KB, 2112, BF16)[:, 0:16 * 65].rearrange("p (t d) -> p t d", t=16)
        Vw = av(40 * KB + 2112, 2112, BF16)[:, 0:16 * 65].rearrange("p (t d) -> p t d", t=16)
        gates = av(40 * KB + 4224, 768, F32).rearrange("p (t h b) -> p t h b", t=16, h=4)
        uT = av(56 * KB, 16 * KB, BF16).rearrange("p (c t) -> p c t", c=4)
        merged = av(56 * KB, 32 * KB, BF16).rearrange("p (c t) -> p c t", c=8)

        def mixv(l):
            return mixw_d[l].rearrange("(kc p) f -> p kc f", p=128)

        def stage_cmp(l):
            kcT, vcaug, w2e, peT, pebias, hcT, crope = (sm[k] for k in ["kcT", "vcaug", "w2e", "peT", "pebias", "hcT", "crope"])
            for j in range(2):
                dma("gpsimd", w1v[0:64, j, :, :], w1_d[l, j].rearrange("l d f -> d l f"), "d_m2")
            dma("gpsimd", w2e[:], w2e_d[l], "d_m2")
            dma("gpsimd", peT[:], peT_d[l], "d_m2")
            dma("sync", crope[:], crope_d, "d_m2")
            for g in range(2):
                dma("sync", vcaug[:, g, 64:97], vcc_d, "d_m2")
            slot, sem = ring()
            sv = slot[:, 0:2048].rearrange("p (k f) -> p k f", k=8)
            dma("gpsimd", sv, mixv(l)[:, :, OFF_KC:OFF_KC + 256], sem)
            for tb in range(4):
                for idx in range(4):
                    bk = bank()
                    for kc in range(8):
                        mm(PS(bk, 0, 64), sv[:, kc, idx * 64:(idx + 1) * 64], N_(kc, tb), start=(kc == 0), stop=(kc == 7))
                    act(cmpT[0:64, idx, tb * 512:(tb + 1) * 512], PS(bk, 0, 64), AF.Copy)
            for j in range(2):
                bp = bank()
                for ll in range(32):
                    mm(PS(bp, 0, 128, 0, 1), w1v[0:64, j, ll, :], peT[0:64, j, ll:ll + 1], start=(ll == 0), stop=(ll == 31))
                vcopy(pebias[:, j:j + 1], PS(bp, 0, 128, 0, 1))
                for g in range(2):
                    bh = bank()
                    for ll in range(32):
                        mm(PS(bh, 0, 128, 0, NCMP), w1v[0:64, j, ll, :], cmpT[0:64, j * 2 + g, ll:ll + 16 * (NCMP - 1) + 1:16],
                           start=(ll == 0), stop=(ll == 31))
                    act(hcT[:, 0:NCMP], PS(bh, 0, 128, 0, NCMP), AF.Silu, bias=pebias[:, j:j + 1])
                    if j == 0:
                        b1 = bank()
                        mm(PS(b1, 0, 64, 0, NCMP), w2e[:, 0:64], hcT[:, 0:NCMP])
                        b2 = bank()
                        mm(PS(b2, 0, 64, 0, NCMP), w2e[:, 64:128], hcT[:, 0:NCMP])
                        t1 = av(SCR, 2 * KB, F32)[0:64, 0:NCMP]
                        t2 = av(SCR + 2 * KB, 2 * KB, F32)[0:64, 0:NCMP]
                        tt(t1, PS(b1, 0, 64, 0, NCMP), crope[:, 0, :], ALU.mult)
                        tt(t2, PS(b2, 0, 64, 0, NCMP), crope[:, 1, :], ALU.mult)
                        tt(kcT[:, g, 0:NCMP], t1, t2, ALU.add)
                    else:
                        b2 = bank()
                        mm(PS(b2, 0, NCMP, 0, 64), hcT[:, 0:NCMP], w2e[:, 128:192])
                        act(vcaug[0:NCMP, g, 0:64], PS(b2, 0, NCMP, 0, 64), AF.Copy)

        def stage_nsa_group(l, g, interleave):
            kcT, vcaug = sm["kcT"], sm["vcaug"]
            rden, coef, imp, impf, imp2, m8, selpen = (sm[k] for k in ["rden", "coef", "imp", "impf", "imp2", "m8", "selpen"])
            rden2, coef2 = sm["rden2"], sm["coef2"]
            mv = mixv(l)
            slot, sem = ring()
            sv = slot.rearrange("p (k f) -> p k f", k=8)
            dma("gpsimd", sv, mv[:, :, OFF_QG[g]:OFF_QG[g] + 512], sem)
            T1 = SCR
            nrot = [0]

            def rope_pair(bq, br, tb):
                i = nrot[0] % 2
                nrot[0] += 1
                t1 = av(T1 + i * 4 * KB, 2 * KB, F32)
                t2 = av(T1 + i * 4 * KB + 2 * KB, 2 * KB, F32)
                stg = av(SCR + 14 * KB + i * KB, KB, BF16)
                tt(t1, PS(bq), rope[:, 0, tb * 512:(tb + 1) * 512], ALU.mult)
                tt(t2, PS(br), rope[:, 1, tb * 512:(tb + 1) * 512], ALU.mult)
                return t1, t2, stg

            for hp in range(2):
                for tb in range(4):
                    bq = bank()
                    for kc in range(8):
                        mm(PS(bq), sv[:, kc, hp * 128:(hp + 1) * 128], N_(kc, tb), start=(kc == 0), stop=(kc == 7))
                    br = bank()
                    for kc in range(8):
                        mm(PS(br), sv[:, kc, 256 + hp * 128:256 + (hp + 1) * 128], N_(kc, tb), start=(kc == 0), stop=(kc == 7))
                    t1, t2, stg = rope_pair(bq, br, tb)
                    tt(Qaug[0:64, tb * 4:(tb + 1) * 4, 2 * hp, :], t1[0:64].rearrange("p (q t) -> p q t", q=4),
                       t2[0:64].rearrange("p (q t) -> p q t", q=4), ALU.add)
                    tt(stg[64:128], t1[64:128], t2[64:128], ALU.add)
                    dma("sync", Qaug[0:64, tb * 4:(tb + 1) * 4, 2 * hp + 1, :], stg[64:128].rearrange("p (q t) -> p q t", q=4), "d_q")
            slot, sem = ring()
            sv = slot[:, 0:2048].rearrange("p (k f) -> p k f", k=8)
            dma("gpsimd", sv, mv[:, :, OFF_KG[g]:OFF_KG[g] + 256], sem)
            for tb in range(4):
                bq = bank()
                for kc in range(8):
                    mm(PS(bq), sv[:, kc, 0:128], N_(kc, tb), start=(kc == 0), stop=(kc == 7))
                br = bank()
                for kc in range(8):
                    mm(PS(br), sv[:, kc, 128:256], N_(kc, tb), start=(kc == 0), stop=(kc == 7))
                t1, t2, stg = rope_pair(bq, br, tb)
                tt(Ksel[0:64, tb * 512:(tb + 1) * 512], t1[0:64], t2[0:64], ALU.add)
                tt(stg[64:128], t1[64:128], t2[64:128], ALU.add)
                dma("sync", Kwin[0:64, tb * 512:(tb + 1) * 512], stg[64:128], "d_q")
            memset(Vs[:, :, 64:65], 1.0)
            memset(Vw[:, :, 64:65], 1.0)
            slot, sem = ring()
            sv = slot[:, 0:8 * 140].rearrange("p (k f) -> p k f", k=8)
            dma("gpsimd", sv, mv[:, :, OFF_VG[g]:OFF_VG[g] + 140], sem)
            for tq in range(16):
                bv = bank()
                for kc in range(8):
                    mm(PS(bv, 0, 128, 0, 140), nv[:, kc, tq * 128:(tq + 1) * 128], sv[:, kc, :], start=(kc == 0), stop=(kc == 7))
                act(Vs[:, tq, 0:64], PS(bv, 0, 128, 0, 64), AF.Copy)
                act(Vw[:, tq, 0:64], PS(bv, 0, 128, 64, 128), AF.Copy)
                act(gates[:, tq, :, :], PS(bv, 0, 128, 128, 140).rearrange("p (h b) -> p h b", h=4), AF.Sigmoid)
            PT0 = SCR + 8 * KB
            npt = [0]
            nsc = [0]

            def new_pT():
                i = npt[0] % 4
                npt[0] += 1
                return av(PT0 + i * KB, KB, BF16)

            def yatok_of(qt):
                return av(PT0 + 6 * KB + (qt % 2) * KB, KB, F32).rearrange("p (h d) -> p h d", h=4)

            def cmp_qk(qt):
                ncq = min(NCMP, 8 * qt + 7)
                q64 = Qaug[0:64, qt, :, :].rearrange("p h t -> p (h t)")
                bc = bank(fixed=6)
                mm(PS(bc, 0, ncq), kcT[0:64, g, 0:ncq], q64)
                ea = av(PT0 + 4 * KB, KB, BF16)
                eb = av(PT0 + 5 * KB, KB, BF16)
                act(ea[0:ncq, :], PS(bc, 0, ncq), AF.Exp, scale=0.125)
                eb3 = eb[0:ncq, :].rearrange("p (h t) -> p h t", h=4)
                ea3 = ea[0:ncq, :].rearrange("p (h t) -> p h t", h=4)
                P.op("gpsimd", lambda e, eb3=eb3, ea3=ea3, qt=qt: e.affine_select(
                    out=eb3, in_=ea3, pattern=[[0, 4], [1, 128]], compare_op=ALU.is_ge, fill=0.0,
                    base=128 * qt - 31, channel_multiplier=-16), reads=rk(ea3), writes=rk(eb3))

            def cmp_pv(qt):
                ncq = min(NCMP, 8 * qt + 7)
                yatok = yatok_of(qt)
                eb = av(PT0 + 5 * KB, KB, BF16)
                bo = bank(fixed=6)
                for hh in range(4):
                    mm(PS(bo, 0, 128, hh * 128, hh * 128 + 97), eb[0:ncq, hh * 128:(hh + 1) * 128], vcaug[0:ncq, g, :])
                bo3 = PS(bo).rearrange("p (h c) -> p h c", h=4)
                ts(rden[:].unsqueeze(2), bo3[:, :, 64:65], 1e-30, ALU.max)
                P.op("vector", lambda e: e.reciprocal(out=rden[:], in_=rden[:]), reads=rk(rden[:]), writes=rk(rden[:]))
                ts(imp[:], bo3[:, 0, 65:97], rden[:, 0:1], ALU.mult)
                for hh in range(1, 4):
                    stt(imp[:], bo3[:, hh, 65:97], rden[:, hh:hh + 1], imp[:], ALU.mult, ALU.add)
                tt(impf[:], imp[:], fv[:, qt, 1, :], ALU.mult)
                tt(impf[:], impf[:], fv[:, qt, 0, :], ALU.add)
                P.op("vector", lambda e: e.max(out=m8[:, 0:8], in_=impf[:]), reads=rk(impf[:]), writes=rk(m8[:]))
                P.op("vector", lambda e: e.match_replace(out=imp2[:], in_to_replace=m8[:, 0:8], in_values=impf[:], imm_value=-1e30),
                     reads=rk(impf[:], m8[:]), writes=rk(imp2[:]))
                P.op("vector", lambda e: e.max(out=m8[:, 8:16], in_=imp2[:]), reads=rk(imp2[:], m8[:]), writes=rk(m8[:]))
                ts(selpen[:, 64:96], impf[:], m8[:, 15:16], ALU.is_ge, 1.0, ALU.subtract)
                tt(coef[:].unsqueeze(2), gates[:, qt, :, 0:1], rden[:].unsqueeze(2), ALU.mult)
                tt(yatok, bo3[:, :, 0:64], coef[:].unsqueeze(2).broadcast_to([128, 4, 64]), ALU.mult)

            def selpen_T(qt):
                bt = bank(fixed=6)
                transpose(PSB(bt, 0, 96, 0, 128), selpen[:, 0:96], identb[:])
                vcopy(Qaug[64:96, qt, :, :], PSB(bt, 64, 96, 0, 128).unsqueeze(1).broadcast_to([32, 4, 128]))

            BR = {2: (Kwin, 64, Vw, 4, sm["rden3"], sm["coef3"]), 1: (Ksel, 96, Vs, 5, rden2, coef2)}

            def issue(it):
                qt, br, kt = it["qt"], it["br"], it["kt"]
                Kd, krows = BR[br][0], BR[br][1]
                qrhs = Qaug[0:krows, qt, :, :].rearrange("p h t -> p (h t)")
                b = nsc[0] % 4
                nsc[0] += 1
                mi = None
                if kt == qt:
                    mi = 0
                elif br == 2 and kt == qt - 4:
                    mi = 1
                mm(PS(b), Kd[0:krows, kt * 128:(kt + 1) * 128], qrhs, start=True, stop=(mi is None))
                if mi is not None:
                    mm(PS(b), identb[:], mneg[:, mi, :], start=False, stop=True)
                return b

            def process(it, b):
                br, kt = it["br"], it["kt"]
                Vd, bacc = BR[br][2], BR[br][3]
                pT = av(PT0 + (npt[0] % 4) * KB, KB, BF16)
                npt[0] += 1
                act(pT, PS(b), AF.Exp, scale=0.125)
                for hh in range(4):
                    mm(PS(bacc, 0, 128, hh * 128, hh * 128 + 65), pT[:, hh * 128:(hh + 1) * 128],
                       Vd[:, kt, :], start=(it["first"] and hh == 0), stop=it["last"], sgc=True)

            def fin_branch(qt, br):
                bacc, rd, cf = BR[br][3], BR[br][4], BR[br][5]
                yatok = yatok_of(qt)
                ba3 = PS(bacc).rearrange("p (h c) -> p h c", h=4)
                P.op("vector", lambda e: e.reciprocal(out=rd[:].unsqueeze(2), in_=ba3[:, :, 64:65]),
                     reads=rk(ba3[:, :, 64:65]), writes=rk(rd[:]))
                tt(cf[:].unsqueeze(2), gates[:, qt, :, br:br + 1], rd[:].unsqueeze(2), ALU.mult)
                tmp = av(SCR + 5 * KB, KB, F32).rearrange("p (h d) -> p h d", h=4)
                tt(tmp, ba3[:, :, 0:64], cf[:].unsqueeze(2).broadcast_to([128, 4, 64]), ALU.mult)
                tt(yatok, yatok, tmp, ALU.add)

            def yab_of(qt):
                return av(SCR + 6 * KB + (qt % 2) * KB, 512, BF16)

            def fin_dve(qt):
                vcopy(yab_of(qt), yatok_of(qt).rearrange("p h d -> p (h d)"))

            def fin_pe(qt):
                yab = yab_of(qt)
                by = bank(fixed=4)
                for j in range(2):
                    transpose(PSB(by, 0, 128, j * 128, (j + 1) * 128), yab[:, j * 128:(j + 1) * 128], identb[:])
                act(yaT[:, 2 * g:2 * g + 2, qt * 128:(qt + 1) * 128],
                    PSB(by, 0, 128, 0, 256).rearrange("p (j t) -> p j t", j=2), AF.Copy)

            items = []
            for qt in range(16):
                for br, kts in ((2, list(range(max(0, qt - 4), qt + 1))), (1, list(range(0, qt + 1)))):
                    for ki, kt in enumerate(kts):
                        items.append(dict(qt=qt, br=br, kt=kt, ki=ki, first=(ki == 0), last=(ki == len(kts) - 1)))
            for q0 in range(2):
                cmp_qk(q0)
                cmp_pv(q0)
                selpen_T(q0)
            LOOK = 3
            inflight = [issue(items[k]) for k in range(LOOK)]
            nexti = LOOK
            for idx, it in enumerate(items):
                qt, br = it["qt"], it["br"]
                if it["first"] and br == 2 and interleave is not None:
                    next(interleave, None)
                b = inflight.pop(0)
                if nexti < len(items):
                    inflight.append(issue(items[nexti]))
                    nexti += 1
                process(it, b)
                if br == 2 and 2 <= qt + 1 < 16:
                    if it["ki"] == 0:
                        cmp_qk(qt + 1)
                    elif it["ki"] == 1:
                        cmp_pv(qt + 1)
                if it["last"]:
                    fin_branch(qt, br)
                    if br == 1:
                        if 2 <= qt + 1 < 16:
                            selpen_T(qt + 1)
                        fin_dve(qt)
                        if qt >= 1:
                            fin_pe(qt - 1)
            fin_pe(15)

        def stage_nsa(l, interleave):
            dma("sync", rope[0:64, :, :], rope_d, "d_m3")
            dma("sync", rope[64:128, :, :], rope_d, "d_m3")
            dma("sync", Ksel[64:96, :], erows_d, "d_m3")
            memset(sm["selpen"][:, 0:64], 0.0)
            for g in range(2):
                stage_nsa_group(l, g, interleave)

        def stage_gmlp(l):
            G0 = 72 * KB
            wsT = av(G0, 2 * KB, BF16).rearrange("p (g t) -> p g t", g=8)
            bsT = av(G0 + 2 * KB, 2 * KB, F32).rearrange("p (j t) -> p j t", j=4)
            lng = av(G0 + 4 * KB, 2 * KB, F32)
            lnb = av(G0 + 6 * KB, 2 * KB, F32)
            dma("gpsimd", wsT, wsT_d[l], "d_m1")
            dma("sync", bsT, bsT_d[l], "d_m1")
            dma("sync", lng, gmln_d[l, 0].partition_broadcast(128), "d_m1")
            dma("sync", lnbT[:], lnbT_d[l], "d_m1")
            tt(wsT, wsT, tri[:, 0, :].unsqueeze(1).broadcast_to([128, 8, 128]), ALU.mult)
            BTp = lnb.rearrange("p (j t) -> p j t", j=4)
            bp0 = bank_pair()
            bq4 = psum[:, bp0 * 512:(bp0 + 2) * 512].rearrange("p (j a t) -> p j a t", j=4, a=2)
            for j in range(4):
                for a in range(2):
                    mm(bq4[:, j, a, :], onesb[:], wsT[:, 2 * j + a, :])
            for j in range(4):
                stt(BTp[0:64, j, :], bq4[0:64, j, 0, :], lnbT[0:64, j:j + 1], bsT[0:64, j, :], ALU.mult, ALU.add)
                stt(BTp[64:128, j, :], bq4[64:128, j, 1, :], lnbT[64:128, j:j + 1], bsT[64:128, j, :], ALU.mult, ALU.add)
            mv = mixv(l)
            slot, sem = ring()
            su = slot.rearrange("p (k f) -> p k f", k=8)
            dma("gpsimd", su, mv[:, :, OFF_U:OFF_U + 512], sem)
            for fc in range(4):
                for tb in range(4):
                    bk = bank()
                    for kc in range(8):
                        mm(PS(bk), su[:, kc, fc * 128:(fc + 1) * 128], N_(kc, tb), start=(kc == 0), stop=(kc == 7))
                    act(uT[:, fc, tb * 512:(tb + 1) * 512], PS(bk), AF.Gelu_apprx_tanh)
            slot, sem = ring()
            svv = slot.rearrange("p (k f) -> p k f", k=8)
            dma("gpsimd", svv, mv[:, :, OFF_V:OFF_V + 512], sem)
            bnst, bnmv, lnr = sm["bnst"], sm["bnmv"], sm["lnr"]

            def vproj(tq):
                bk = bank()
                for kc in range(8):
                    mm(PS(bk), nv[:, kc, tq * 128:(tq + 1) * 128], svv[:, kc, :], start=(kc == 0), stop=(kc == 7))
                return bk

            bk_next = vproj(0)
            for tq in range(16):
                bk = bk_next
                if tq + 1 < 16:
                    bk_next = vproj(tq + 1)
                vg = av(G0 + 8 * KB + (tq % 2) * 2 * KB, 2 * KB, F32)
                act(vg, PS(bk), AF.Gelu_apprx_tanh)
                P.op("vector", lambda e, vg=vg: e.bn_stats(out=bnst[:], in_=vg), reads=rk(vg), writes=rk(bnst[:]))
                P.op("vector", lambda e: e.bn_aggr(out=bnmv[:], in_=bnst[:]), reads=rk(bnst[:]), writes=rk(bnmv[:]))
                ts(lnr[:], bnmv[:, 1:2], EPS, ALU.add)
                tt(lnr[:], lnr[:], nhalf[:, 0:1], ALU.pow, eng="gpsimd")
                ts(vg, vg, bnmv[:, 0:1], ALU.subtract, lnr[:, 0:1], ALU.mult)
                vtok = av(G0 + 12 * KB + (tq % 2) * KB, KB, BF16)
                tt(vtok, vg, lng, ALU.mult)
                bp = bank_pair()
                bp4 = psum[:, bp * 512:(bp + 2) * 512].rearrange("p (j a t) -> p j a t", j=4, a=2)
                for j in range(4):
                    for a in range(2):
                        mm(bp4[:, j, a, :], vtok[:, j * 128:(j + 1) * 128], wsT[:, 2 * j + a, :])
                tmp = av(G0 + 14 * KB, 2 * KB, F32).rearrange("p (j t) -> p j t", j=4)
                tt(tmp[0:64], bp4[0:64, :, 0, :], BTp[0:64], ALU.add)
                tt(tmp[64:128], bp4[64:128, :, 1, :], BTp[64:128], ALU.add)
                tt(ybT[:, :, tq * 128:(tq + 1) * 128], tmp, uT[:, :, tq * 128:(tq + 1) * 128], ALU.mult)

        def stage_merge(l):
            mv = mixv(l)
            pav = proj_a_d[l].rearrange("(kc p) f -> p kc f", p=128)
            pbv = proj_b_d[l].rearrange("(kc p) f -> p kc f", p=128)
            X0 = 88 * KB
            nx = [0]
            for fc in range(8):
                slot, sem = ring()
                sg = slot[:, 0:2048].rearrange("p (k f) -> p k f", k=8)
                spa = slot[:, 2048:2560].rearrange("p (k f) -> p k f", k=4)
                spb = slot[:, 2560:3072].rearrange("p (k f) -> p k f", k=4)
                dma("gpsimd", sg, mv[:, :, OFF_GAB + fc * 256:OFF_GAB + (fc + 1) * 256], sem)
                dma("gpsimd", spa, pav[:, :, fc * 128:(fc + 1) * 128], sem)
                dma("gpsimd", spb, pbv[:, :, fc * 128:(fc + 1) * 128], sem)
                for tb in range(4):
                    bA = bank()
                    for kc in range(8):
                        mm(PS(bA), sg[:, kc, 0:128], N_(kc, tb), start=(kc == 0), stop=(kc == 7))
                    bB = bank()
                    for kc in range(8):
                        mm(PS(bB), sg[:, kc, 128:256], N_(kc, tb), start=(kc == 0), stop=(kc == 7))
                    bPA = bank()
                    for kc in range(4):
                        mm(PS(bPA), spa[:, kc, :], yaT[:, kc, tb * 512:(tb + 1) * 512], start=(kc == 0), stop=(kc == 3))
                    bPB = bank()
                    for kc in range(4):
                        mm(PS(bPB), spb[:, kc, :], ybT[:, kc, tb * 512:(tb + 1) * 512], start=(kc == 0), stop=(kc == 3))
                    i = nx[0] % 2
                    nx[0] += 1
                    sga = av(X0 + i * 4 * KB, 2 * KB, F32)
                    sgb = av(X0 + i * 4 * KB + 2 * KB, 2 * KB, F32)
                    act(sga, PS(bA), AF.Sigmoid)
                    act(sgb, PS(bB), AF.Sigmoid)
                    tt(sga, sga, PS(bPA), ALU.mult)
                    tt(sgb, sgb, PS(bPB), ALU.mult)
                    tt(merged[:, fc, tb * 512:(tb + 1) * 512], sga, sgb, ALU.add)
            if STOP == "merged":
                return
            wov = w_out_d[l].rearrange("(kc p) f -> p kc f", p=128)
            for half in range(2):
                slot, sem = ring()
                sv = slot.rearrange("p (k f) -> p k f", k=8)
                dma("gpsimd", sv, wov[:, :, half * 512:(half + 1) * 512], sem)
                for fl in range(4):
                    fo = half * 4 + fl
                    for tb in range(4):
                        bk = bank()
                        for kc in range(8):
                            mm(PS(bk), sv[:, kc, fl * 128:(fl + 1) * 128], merged[:, kc, tb * 512:(tb + 1) * 512],
                               start=(kc == 0), stop=(kc == 7))
                        stt(H(fo, tb), PS(bk), der[l][:, 1, 8 + fo:9 + fo], H(fo, tb), ALU.mult, ALU.add)

        def dump_bf16(view3, nchunks):
            for c in range(nchunks):
                for tb in range(4):
                    vcopy(H(c, tb), view3[:, c, tb * 512:(tb + 1) * 512])

        def program():
            mod0 = mod_steps(0)
            for _ in range(6 if STOP != "mod" else 18):
                next(mod0)
            if STOP == "mod":
                vcopy(hv[:, 0, 0:72], modT[0][:])
                vcopy(hv[:, 0, 72:120], der[0][:].rearrange("p a b -> p (a b)"))
                return
            for l in range(2):
                norm_mod(l, 0)
                if STOP == "n0" and l == 0:
                    dump_bf16(nv, 8)
                    return
                ffn(l, 0, 0, mod0 if l == 0 else None)
                if l == 0:
                    for _ in mod0:
                        pass
                if STOP == "h1" and l == 0:
                    return
                norm_mod(l, 1)
                stage_cmp(l)
                if STOP == "cmp" and l == 0:
                    vcopy(hv[0:64, 0, 0:256], sm["kcT"][:].rearrange("p a b -> p (a b)"))
                    vcopy(hv[:, 1, 0:194], sm["vcaug"][:].rearrange("p a b -> p (a b)"))
                    return
                inter = mod_steps(1) if l == 0 else None
                stage_nsa(l, inter)
                if inter is not None:
                    for _ in inter:
                        pass
                if STOP == "ya" and l == 0:
                    dump_bf16(yaT, 4)
                    return
                stage_gmlp(l)
                if STOP == "yb" and l == 0:
                    dump_bf16(ybT, 4)
                    return
                stage_merge(l)
                if STOP == "merged" and l == 0:
                    dump_bf16(merged, 8)
                    return
                if STOP == "h2" and l == 0:
                    return
                norm_mod(l, 2)
                ffn(l, 1, 2)
                if STOP == "h3" and l == 0:
                    return
            final_norm()

        program()
        outv = outT_d.rearrange("(c p) t -> p c t", p=128)
        for tb in range(4):
            dma("sync", outv[:, :, tb * 512:(tb + 1) * 512], hv[:, :, tb * 512:(tb + 1) * 512], "d_o")
        P.final_wait("sync", ["d_o"])
        P.emit(block, sems)
    return nc


def _consts():
    inv = 1.0 / (10000.0 ** (np.arange(0, 64, 2, dtype=np.float32) / 64.0))
    pos = np.arange(S, dtype=np.float32)
    ang = pos[:, None] * inv[None, :]
    cos, sin = np.cos(ang).astype(np.float32), np.sin(ang).astype(np.float32)
    rope = np.stack([np.concatenate([cos, cos], 1).T, np.concatenate([-sin, sin], 1).T], 1)
    cend = (np.arange(NCMP) * 16 + 31).astype(np.float32)
    angc = cend[:, None] * inv[None, :]
    cc, cs = np.cos(angc).astype(np.float32), np.sin(angc).astype(np.float32)
    crope = np.stack([np.concatenate([cc, cc], 1).T, np.concatenate([-cs, cs], 1).T], 1)
    cmask = np.zeros((128, S), np.float32)
    cmask[:NCMP] = (cend[:, None] <= pos[None, :]).astype(np.float32)
    k = np.arange(128)
    triS = (k[:, None] <= k[None, :]).astype(np.float32)
    triW = (k[:, None] > k[None, :]).astype(np.float32)
    tri = np.stack([triS, triW], 1)
    mneg = np.stack([np.tile((1.0 - triS) * -30000.0, (1, 4)), np.tile((1.0 - triW) * -30000.0, (1, 4))], 1)
    erows = (np.arange(S)[None, :] // 64 == np.arange(32)[:, None]).astype(np.float32) * 32768.0
    starts = np.arange(NCMP) * 16
    sel_start = np.arange(32) * 64
    ov = np.clip(np.minimum(starts[:, None] + 32, sel_start[None, :] + 64) - np.maximum(starts[:, None], sel_start[None, :]),
                 0, None).astype(np.float32) / 32.0
    vcc = np.zeros((128, 33), np.float32)
    vcc[:, 0] = 1.0
    vcc[:NCMP, 1:] = ov
    t = np.arange(S)
    cur = t // 64
    blk = np.arange(32)
    forced = (blk[None, :] == 0) | (blk[None, :] == cur[:, None]) | (blk[None, :] == cur[:, None] - 1)
    valid = blk[None, :] <= cur[:, None]
    fadd = np.where(forced, 1e4, np.where(valid, 0.0, -1e4)).astype(np.float32)
    vnf = (valid & ~forced).astype(np.float32)
    fv = np.stack([fadd, vnf], 1).reshape(16, 128, 2, 32).transpose(1, 0, 2, 3)
    bf = ml_dtypes.bfloat16
    return {
        "c_rope": np.ascontiguousarray(rope, np.float32), "c_crope": np.ascontiguousarray(crope, np.float32),
        "c_tri": np.ascontiguousarray(tri).astype(bf), "c_mneg": np.ascontiguousarray(mneg).astype(bf), "c_erows": erows.astype(bf),
        "c_vcc": vcc.astype(bf), "c_fv": np.ascontiguousarray(fv, np.float32), "c_identb": np.eye(128, dtype=np.float32).astype(bf),
    }


def _prep_shared(inp):
    perm = np.concatenate([np.arange(32, 64), np.arange(0, 32)])
    mw = inp["mix_w_in"]

    def kvcol(s, g):
        return 512 + (s * 2 + g) * 64 + np.arange(64)

    cols = [np.arange(1304, 1816), np.arange(1816, 2328), np.arange(512, 768)]
    for g in range(2):
        qh = [g * 256 + hh * 64 + np.arange(64) for hh in range(4)]
        cols += qh + [q[perm] for q in qh]
        cols += [kvcol(2, g), kvcol(4, g), kvcol(2, g)[perm], kvcol(4, g)[perm]]
        cols += [kvcol(3, g), kvcol(5, g), 1280 + g * 12 + np.arange(12)]
    for fc in range(8):
        cols += [2328 + fc * 128 + np.arange(128), 3352 + fc * 128 + np.arange(128)]
    cols = np.concatenate(cols)
    assert cols.shape[0] == NEXT
    mixw = np.ascontiguousarray(mw[:, :, cols])
    w2 = inp["cmp_w2"]
    w2e = np.ascontiguousarray(np.concatenate([w2[:, 0], w2[:, 0][:, :, perm], w2[:, 1]], axis=2))
    sh = {
        "ada_w": inp["ada_w"],
        "ada_bT": np.ascontiguousarray(inp["ada_b"].reshape(2, 72, 128).transpose(2, 0, 1)),
        "normg": np.ascontiguousarray(np.concatenate([inp["norm_g"].reshape(6, 8, 128), inp["final_g"].reshape(1, 8, 128)], 0)
                                      .reshape(56, 128).T),
        "ffn_w_in": inp["ffn_w_in"], "ffn_w_out": inp["ffn_w_out"], "mixw": mixw,
        "cmp_peT": np.ascontiguousarray(inp["cmp_pe"].transpose(0, 3, 1, 2)),
        "cmp_w1": inp["cmp_w1"], "cmp_w2e": w2e,
        "gm_ln": np.ascontiguousarray(np.stack([inp["gm_ln_g"], inp["gm_ln_b"]], 1)),
        "gm_wsT": np.ascontiguousarray(inp["gm_ws"].transpose(0, 3, 1, 2)),
        "gm_lnbT": np.ascontiguousarray(inp["gm_ln_b"].reshape(2, 4, 128).transpose(0, 2, 1)),
        "gm_bsT": np.ascontiguousarray(np.repeat(inp["gm_bs"], 64, axis=1).reshape(2, 4, 128, 128).transpose(0, 2, 1, 3)),
        "proj_a": inp["proj_a"], "proj_b": inp["proj_b"], "w_out": inp["w_out"],
    }
    sh.update(_consts())
    return sh


def kernel(**inputs):
    inp = {k: np.asarray(v) for k, v in inputs.items()}
    shared = _prep_shared(inp)
    ncores = int(os.environ.get("MK_NCORES", "8"))
    in_maps = []
    for b in range(ncores):
        m = dict(shared)
        m["xT"] = np.ascontiguousarray(inp["x"][b].T)
        m["cT"] = np.ascontiguousarray(inp["c"][b].reshape(8, 128).T)
        in_maps.append(m)
    nc = build_program()
    res = run_bass_kernel_spmd(nc, in_maps, core_ids=list(range(ncores)))
    out = np.stack([np.ascontiguousarray(r["outT"].T) for r in res.results], 0)
    return out.astype(np.float32)
```

```python
import os
import numpy as np
import ml_dtypes
from contextlib import ExitStack
import concourse.bass as bass
import concourse.mybir as mybir
from concourse.bass_utils import run_bass_kernel_spmd

F32 = mybir.dt.float32
BF16 = mybir.dt.bfloat16
AF = mybir.ActivationFunctionType
ALU = mybir.AluOpType
DS = {F32: 4, BF16: 2}

S = 2048
D = 1024
DFF = 2816
NCMP = 127
EPS = 1e-6
ENGS = ["sync", "scalar", "vector", "gpsimd", "tensor"]
GRAN = {"hT": 2048, "nT": 1024, "arena": 1024, "ps": 2048}

OFF_U = 0
OFF_V = 512
OFF_KC = 1024
OFF_QG = [1280, 1280 + 908]
OFF_KG = [1280 + 512, 1280 + 908 + 512]
OFF_VG = [1280 + 768, 1280 + 908 + 768]
OFF_GAB = 1280 + 2 * 908
NEXT = OFF_GAB + 2048

ARENA_KIB = 96
STOP = os.environ.get("MK_STOP", "")


def ap_keys(ap):
    name = ap.tensor.name
    if name not in GRAN:
        return [name]
    g = GRAN[name]
    ds = DS[ap.dtype]
    pat = ap.ap
    pstep = pat[0][0]
    off = (ap.offset % pstep) if pstep else ap.offset
    dims = [(s, n) for (s, n) in pat[1:]]
    res = set()

    def rec(i, base):
        if i >= len(dims):
            res.add(base * ds // g)
            return
        s, n = dims[i]
        if i == len(dims) - 1:
            lo = base
            hi = base + (n - 1) * abs(s)
            for ch in range(lo * ds // g, (hi * ds + ds - 1) // g + 1):
                res.add(ch)
        else:
            if s == 0:
                n = 1
            for j in range(n):
                rec(i + 1, base + j * s)

    rec(0, off)
    if name == "arena":
        p0 = ap.offset // pstep if pstep else 0
        q0, q1 = p0 // 32, (p0 + pat[0][1] - 1) // 32
        return [f"{name}{c}q{q}" for c in sorted(res) for q in range(q0, q1 + 1)]
    return [f"{name}{c}" for c in sorted(res)]


class Prog:
    def __init__(self):
        self.ops = {e: [] for e in ENGS}
        self.cnt = {}
        self.last_w = {}
        self.readers = {}
        self.seen = {e: {} for e in ENGS}
        self.dma_sems = set()

    def op(self, eng, fn, reads=(), writes=(), dma=None):
        need = {}

        def add(ev):
            if ev is None:
                return
            s, v = ev
            if s in self.dma_sems:
                v = self.cnt[s]
            if need.get(s, 0) < v:
                need[s] = v

        for k in reads:
            add(self.last_w.get(k))
        for k in writes:
            add(self.last_w.get(k))
            for s, v in self.readers.get(k, {}).items():
                add((s, v))
        own = "e_" + eng
        waits = []
        for s, v in need.items():
            if s == own and eng == "tensor":
                continue
            if self.seen[eng].get(s, 0) >= v:
                continue
            self.seen[eng][s] = v
            waits.append((s, v))
        if dma is None:
            sem, inc = own, 1
        else:
            sem, inc = dma, 16
            self.dma_sems.add(dma)
        self.cnt[sem] = self.cnt.get(sem, 0) + inc
        ev = (sem, self.cnt[sem])
        for k in writes:
            self.last_w[k] = ev
            self.readers[k] = {}
        for k in reads:
            d = self.readers.setdefault(k, {})
            d[sem] = max(d.get(sem, 0), ev[1])
        self.ops[eng].append((waits, fn, sem, inc))
        return ev

    def prewait(self, eng, reads):
        need = {}
        for k in reads:
            ev = self.last_w.get(k)
            if ev is None:
                continue
            s_, v = ev
            if s_ in self.dma_sems:
                v = self.cnt[s_]
            if need.get(s_, 0) < v:
                need[s_] = v
        own = "e_" + eng
        waits = []
        for s_, v in need.items():
            if s_ == own and eng == "tensor":
                continue
            if self.seen[eng].get(s_, 0) >= v:
                continue
            self.seen[eng][s_] = v
            waits.append((s_, v))
        if waits:
            self.ops[eng].append((waits, None, None, 0))

    def final_wait(self, eng, sems_):
        self.ops[eng].append(([(s, self.cnt[s]) for s in sems_], None, None, 0))

    def emit(self, block, sems):
        for e in ENGS:
            ops = self.ops[e]

            def body(eng, ops=ops):
                for waits, fn, sem, inc in ops:
                    for s, v in waits:
                        eng.wait_ge(sems[s], v)
                    if fn is not None:
                        fn(eng).then_inc(sems[sem], inc)

            getattr(block, e)(body)


def build_program():
    nc = bass.Bass("TRN2", target_bir_lowering=False)
    P = Prog()

    def dram(name, shape, dt=F32, kind="ExternalInput"):
        return nc.dram_tensor(name, list(shape), dt, kind=kind).ap()

    xT_d = dram("xT", [D, S])
    cT_d = dram("cT", [128, 8])
    ada_w_d = dram("ada_w", [2, D, 9 * D])
    ada_bT_d = dram("ada_bT", [128, 2, 72])
    normg_d = dram("normg", [128, 56])
    ffn_w_in_d = dram("ffn_w_in", [2, 2, D, 2 * DFF])
    ffn_w_out_d = dram("ffn_w_out", [2, 2, DFF, D])
    mixw_d = dram("mixw", [2, D, NEXT])
    peT_d = dram("cmp_peT", [2, 64, 2, 32])
    w1_d = dram("cmp_w1", [2, 2, 32, 64, 128])
    w2e_d = dram("cmp_w2e", [2, 128, 192])
    gmln_d = dram("gm_ln", [2, 2, 512])
    wsT_d = dram("gm_wsT", [2, 128, 8, 128])
    bsT_d = dram("gm_bsT", [2, 128, 4, 128])
    lnbT_d = dram("gm_lnbT", [2, 128, 4])
    proj_a_d = dram("proj_a", [2, 512, D])
    proj_b_d = dram("proj_b", [2, 512, D])
    w_out_d = dram("w_out", [2, D, D])
    rope_d = dram("c_rope", [64, 2, S])
    crope_d = dram("c_crope", [64, 2, NCMP])
    tri_d = dram("c_tri", [128, 2, 128], BF16)
    mneg_d = dram("c_mneg", [128, 2, 512], BF16)
    erows_d = dram("c_erows", [32, S], BF16)
    vcc_d = dram("c_vcc", [128, 33], BF16)
    fv_d = dram("c_fv", [128, 16, 2, 32])
    identb_d = dram("c_identb", [128, 128], BF16)
    outT_d = dram("outT", [D, S], F32, kind="ExternalOutput")

    es = ExitStack()
    with es:
        def sb(name, shape, dt):
            return es.enter_context(nc.sbuf_tensor(name, list(shape), dt))

        hT = sb("hT", [128, 8 * S], F32)
        nT = sb("nT", [128, 8 * S], BF16)
        arena = sb("arena", [128, ARENA_KIB * 256], F32)
        psum = es.enter_context(nc.psum_tensor("ps", [128, 4096], F32))
        tri = sb("tri", [128, 2, 128], BF16)
        mneg = sb("mneg", [128, 2, 512], BF16)
        fv = sb("fv", [128, 16, 2, 32], F32)
        identb = sb("identb", [128, 128], BF16)
        onesm = sb("onesm", [128, 128], BF16)
        onesb = sb("onesb", [128, 128], BF16)
        lnbT = sb("lnbT", [128, 4], F32)
        epsc = sb("epsc", [128, 1], F32)
        nhalf = sb("nhalf", [128, 1], F32)
        cT = sb("cTs", [128, 8], F32)
        scb = sb("scb", [128, 8], BF16)
        ada_bT = sb("ada_bTs", [128, 2, 72], F32)
        normg = sb("normgs", [128, 56], F32)
        modT = [sb(f"modT{l}", [128, 72], F32) for l in range(2)]
        der = [sb(f"der{l}", [128, 3, 16], F32) for l in range(2)]
        sm = {}
        for nm, shp, dt in [("rden", [128, 4], F32), ("coef", [128, 4], F32), ("imp", [128, 32], F32),
                            ("impf", [128, 32], F32), ("imp2", [128, 32], F32), ("m8", [128, 16], F32),
                            ("selpen", [128, 96], BF16), ("bnst", [128, 6], F32), ("bnmv", [128, 2], F32),
                            ("lnr", [128, 1], F32), ("pebias", [128, 2], F32), ("peT", [64, 2, 32], BF16),
                            ("w2e", [128, 192], BF16), ("hcT", [128, 128], BF16), ("kcT", [64, 2, 128], BF16),
                            ("vcaug", [128, 2, 97], BF16), ("crope", [64, 2, NCMP], F32),
                            ("rden2", [128, 4], F32), ("coef2", [128, 4], F32),
                            ("rden3", [128, 4], F32), ("coef3", [128, 4], F32)]:
            sm[nm] = sb(nm, shp, dt)

        sem_names = ["e_" + e for e in ENGS] + ["d_r0", "d_r1", "d_r2", "d_c", "d_x", "d_o", "d_q", "d_m1s", "d_m1h", "d_m2s", "d_m2h", "d_m3h"]
        sems = {n: es.enter_context(nc.semaphore(n)) for n in sem_names}
        block = es.enter_context(nc.Block())

        hv = hT[:].rearrange("p (c t) -> p c t", c=8)
        nv = nT[:].rearrange("p (c t) -> p c t", c=8)

        def H(c, tb):
            return hv[:, c, tb * 512:(tb + 1) * 512]

        def N_(c, tb):
            return nv[:, c, tb * 512:(tb + 1) * 512]

        def av(off_b, size_b, dt):
            a = arena[:, off_b // 4:(off_b + size_b) // 4]
            return a.bitcast(BF16) if dt == BF16 else a

        KB = 1024

        def rk(*aps):
            ks = []
            for a in aps:
                if a is None or isinstance(a, (int, float)):
                    continue
                ks += ap_keys(a)
            return ks

        def mm(out, lhsT, rhs, start=True, stop=True, sgc=False):
            P.op("tensor", lambda e: e.matmul(out, lhsT=lhsT, rhs=rhs, start=start, stop=stop, skip_group_check=sgc),
                 reads=rk(lhsT, rhs), writes=rk(out))

        def transpose(out, in_, ident):
            P.op("tensor", lambda e: e.transpose(out, in_, ident), reads=rk(in_, ident), writes=rk(out))

        def act(out, in_, func, bias=None, scale=None):
            kw = {}
            if bias is not None:
                kw["bias"] = bias
            if scale is not None:
                kw["scale"] = scale
            P.op("scalar", lambda e: e.activation(out=out, in_=in_, func=func, **kw),
                 reads=rk(in_, bias, scale), writes=rk(out))

        def tt(out, in0, in1, op, eng="vector"):
            P.op(eng, lambda e: e.tensor_tensor(out=out, in0=in0, in1=in1, op=op), reads=rk(in0, in1), writes=rk(out))

        def ts(out, in0, s1, op0, s2=None, op1=None, eng="vector"):
            if op1 is None:
                P.op(eng, lambda e: e.tensor_scalar(out=out, in0=in0, scalar1=s1, scalar2=None, op0=op0),
                     reads=rk(in0, s1), writes=rk(out))
            else:
                P.op(eng, lambda e: e.tensor_scalar(out=out, in0=in0, scalar1=s1, scalar2=s2, op0=op0, op1=op1),
                     reads=rk(in0, s1, s2), writes=rk(out))

        def stt(out, in0, scalar, in1, op0, op1):
            P.op("vector", lambda e: e.scalar_tensor_tensor(out=out, in0=in0, scalar=scalar, in1=in1, op0=op0, op1=op1),
                 reads=rk(in0, scalar, in1), writes=rk(out))

        def vcopy(out, in_, eng="vector"):
            P.op(eng, lambda e: e.tensor_copy(out=out, in_=in_), reads=rk(in_), writes=rk(out))

        def memset(ap, val, eng="vector"):
            P.op(eng, lambda e: e.memset(ap, val), writes=rk(ap))

        def dma(eng, out, in_, sem, **kw):
            rd = rk(in_) if in_.tensor.name in SBN else []
            wr = rk(out) if out.tensor.name in SBN else [out.tensor.name]
            if sem.startswith("d_m"):
                sem = sem + ("s" if eng == "gpsimd" else "h")
            return P.op(eng, lambda e: e.dma_start(out=out, in_=in_, **kw), reads=rd, writes=wr, dma=sem)

        SBN = set(["hT", "nT", "arena", "tri", "mneg", "fv", "identb", "onesm", "onesb", "lnbT", "epsc", "nhalf", "cTs", "scb", "ada_bTs", "normgs",
                   "modT0", "modT1", "der0", "der1"] + list(sm.keys()))

        bstate = {"next": 0, "held": set()}

        def bank(hold=False, fixed=None):
            if fixed is not None:
                if hold:
                    bstate["held"].add(fixed)
                return fixed
            for _ in range(16):
                i = bstate["next"]
                bstate["next"] = (i + 1) % 8
                if i not in bstate["held"]:
                    if hold:
                        bstate["held"].add(i)
                    return i
            raise RuntimeError("no psum bank")

        def bank_pair():
            for _ in range(16):
                i = bstate["next"]
                if i % 2 == 1:
                    i = (i + 1) % 8
                bstate["next"] = (i + 2) % 8
                if i not in bstate["held"] and (i + 1) not in bstate["held"]:
                    return i
            raise RuntimeError("no psum bank pair")

        def release(i):
            bstate["held"].discard(i)

        def PS(i, p0=0, p1=128, c0=0, c1=512):
            return psum[p0:p1, i * 512 + c0:i * 512 + c1]

        def PSB(i, p0, p1, c0, c1):
            return psum[p0:p1, i * 512:(i + 1) * 512].bitcast(BF16)[:, c0:c1]

        rstate = {"next": 0}

        def ring():
            i = rstate["next"]
            rstate["next"] = (i + 1) % 3
            return av(i * 8 * KB, 8 * KB, BF16), f"d_r{i}"

        for c in range(8):
            dma("sync", hv[:, c, :], xT_d[c * 128:(c + 1) * 128, :], "d_x")
        for dst, src in [(cT, cT_d), (ada_bT, ada_bT_d), (normg, normg_d), (tri, tri_d), (mneg, mneg_d),
                         (fv, fv_d), (identb, identb_d)]:
            dma("sync", dst[:], src, "d_c")
        memset(onesm[:], 1.0 / 1024.0)
        memset(epsc[:], EPS)
        memset(onesb[:], 1.0)
        memset(nhalf[:], -0.5)
        act(scb[:], cT[:], AF.Silu)

        def mod_steps(l):
            bm = bank(hold=True, fixed=7)
            awv = ada_w_d[l].rearrange("(kc p) f -> p kc f", p=128)
            for s in range(18):
                slot, sem = ring()
                sv = slot.rearrange("p (k f) -> p k f", k=8)
                if l == 0 and s == 0:
                    P.prewait("gpsimd", rk(hv[:, :, :]))
                dma("gpsimd", sv, awv[:, :, s * 512:(s + 1) * 512], sem)
                for fc in range(4):
                    j = s * 4 + fc
                    for kc in range(8):
                        mm(PS(bm, 0, 128, j, j + 1), sv[:, kc, fc * 128:(fc + 1) * 128], scb[:, kc:kc + 1],
                           start=(kc == 0), stop=(kc == 7))
                if s % 6 == 5:
                    sub = s // 6
                    tt(modT[l][:, sub * 24:(sub + 1) * 24], PS(bm, 0, 128, sub * 24, (sub + 1) * 24),
                       ada_bT[:, l, sub * 24:(sub + 1) * 24], ALU.add)
                    stt(der[l][:, sub, 0:8], modT[l][:, (sub * 3 + 1) * 8:(sub * 3 + 2) * 8], 1.0,
                        normg[:, (l * 3 + sub) * 8:(l * 3 + sub + 1) * 8], ALU.add, ALU.mult)
                    ts(der[l][:, sub, 8:16], modT[l][:, (sub * 3 + 2) * 8:(sub * 3 + 3) * 8],
                       1.0 if sub == 1 else 0.5, ALU.mult)
                    if s == 17:
                        release(bm)
                yield

        NSCR = 84 * KB

        def rms_stats(tb, rstd, all_act=False):
            bk = bank()
            for c in range(8):
                sq = av(NSCR + (c % 4) * KB, KB, BF16)
                if all_act or c in (0, 2, 5, 7):
                    act(sq, H(c, tb), AF.Square)
                else:
                    tt(sq, H(c, tb), H(c, tb), ALU.mult)
                mm(PS(bk), onesm[:], sq, start=(c == 0), stop=(c == 7))
            act(rstd, PS(bk), AF.Ln, bias=epsc[:, 0:1])
            act(rstd, rstd, AF.Exp, scale=-0.5)

        def rstd_of(tb):
            return av(NSCR + 4 * KB + (tb % 2) * 2 * KB, 2 * KB, F32)

        def norm_mod(l, sub, pre=0):
            for tb in range(4):
                rstd = rstd_of(tb)
                if tb >= pre:
                    rms_stats(tb, rstd)
                for c in range(8):
                    tmp = av(NSCR + 8 * KB + (c % 2) * 2 * KB, 2 * KB, F32)
                    if False:
                        pass
                    else:
                        tt(tmp, H(c, tb), rstd, ALU.mult)
                        act(N_(c, tb), tmp, AF.Identity, bias=modT[l][:, sub * 24 + c:sub * 24 + c + 1],
                            scale=der[l][:, sub, c:c + 1])

        def final_norm():
            for tb in range(4):
                rstd = av(NSCR + 4 * KB + (tb % 2) * 2 * KB, 2 * KB, F32)
                rms_stats(tb, rstd, all_act=True)
                for c in range(8):
                    stt(H(c, tb), H(c, tb), normg[:, 48 + c:49 + c], rstd, ALU.mult, ALU.mult)

        def ffn(l, i, sub, interleave=None):
            winv = ffn_w_in_d[l, i].rearrange("(kc p) f -> p kc f", p=128)
            woutv = ffn_w_out_d[l, i].rearrange("(kc p) f -> p kc f", p=128)
            gbuf = av(24 * KB, 48 * KB, BF16).rearrange("p (c t) -> p c t", c=12)
            SA = 72 * KB
            nsa_ = [0]
            def load_pair(ca, il=True):
                if il and interleave is not None:
                    next(interleave, None)
                slot, sem = ring()
                sv = slot.rearrange("p (k a f) -> p k a f", k=8, a=2)
                dma("gpsimd", sv[:, :, 0, :], winv[:, :, ca * 128:ca * 128 + 256], sem)
                dma("gpsimd", sv[:, :, 1, :], winv[:, :, DFF + ca * 128:DFF + ca * 128 + 256], sem)
                return sv

            def pair_tb(sv, gi, cc, tb):
                bA = bank()
                for kc in range(8):
                    mm(PS(bA), sv[:, kc, 0, cc * 128:(cc + 1) * 128], N_(kc, tb), start=(kc == 0), stop=(kc == 7))
                bB = bank()
                for kc in range(8):
                    mm(PS(bB), sv[:, kc, 1, cc * 128:(cc + 1) * 128], N_(kc, tb), start=(kc == 0), stop=(kc == 7))
                sa = av(SA + (nsa_[0] % 3) * 2 * KB, 2 * KB, F32)
                nsa_[0] += 1
                act(sa, PS(bA), AF.Silu)
                tt(gbuf[:, gi, tb * 512:(tb + 1) * 512], sa, PS(bB), ALU.mult)

            for (c0, c1) in [(0, 12), (12, 22)]:
                npairs = (c1 - c0) // 2
                jp0 = 0
                if c0 == 0:
                    svs = [load_pair(c0 + 2 * jp, il=False) for jp in range(3)]
                    for tb in range(4):
                        for jp in range(3):
                            for cc in range(2):
                                pair_tb(svs[jp], 2 * jp + cc, cc, tb)
                    jp0 = 3
                for jp in range(jp0, npairs):
                    ca = c0 + 2 * jp
                    sv = load_pair(ca)
                    for cc in range(2):
                        gi = ca + cc - c0
                        for tb in range(4):
                            pair_tb(sv, gi, cc, tb)
                nk = c1 - c0
                for fp in range(4):
                    if interleave is not None:
                        next(interleave, None)
                    slot, sem = ring()
                    sv = slot[:, 0:nk * 256].rearrange("p (k f) -> p k f", k=nk)
                    dma("gpsimd", sv, woutv[:, c0:c1, fp * 256:(fp + 1) * 256], sem)
                    for fl in range(2):
                        fo = fp * 2 + fl
                        for tb in range(4):
                            bk = bank()
                            for k in range(nk):
                                mm(PS(bk), sv[:, k, fl * 128:(fl + 1) * 128], gbuf[:, k, tb * 512:(tb + 1) * 512],
                                   start=(k == 0), stop=(k == nk - 1))
                            stt(H(fo, tb), PS(bk), der[l][:, sub, 8 + fo:9 + fo], H(fo, tb), ALU.mult, ALU.add)

        yaT = av(24 * KB, 16 * KB, BF16).rearrange("p (c t) -> p c t", c=4)
        ybT = av(40 * KB, 16 * KB, BF16).rearrange("p (c t) -> p c t", c=4)
        SCR = 40 * KB
        cmpT = av(56 * KB, 16 * KB, BF16).rearrange("p (i t) -> p i t", i=4)
        w1v = av(72 * KB, 16 * KB, BF16).rearrange("p (j l f) -> p j l f", j=2, l=32)
        rope = av(56 * KB, 16 * KB, F32).rearrange("p (a t) -> p a t", a=2)
        Qaug = av(72 * KB, 16 * KB, BF16).rearrange("p (q h t) -> p q h t", q=16, h=4)
        Ksel = av(88 * KB, 4 * KB, BF16)
        Kwin = av(92 * KB, 4 * KB, BF16)
        Vs = av(40 * KB, 2112, BF16)[:, 0:16 * 65].rearrange("p (t d) -> p t d", t=16)
        Vw = av(40 * KB + 2112, 2112, BF16)[:, 0:16 * 65].rearrange("p (t d) -> p t d", t=16)
        gates = av(40 * KB + 4224, 768, F32).rearrange("p (t h b) -> p t h b", t=16, h=4)
        uT = av(56 * KB, 16 * KB, BF16).rearrange("p (c t) -> p c t", c=4)
        merged = av(56 * KB, 32 * KB, BF16).rearrange("p (c t) -> p c t", c=8)

        def mixv(l):
            return mixw_d[l].rearrange("(kc p) f -> p kc f", p=128)

        def stage_cmp(l):
            kcT, vcaug, w2e, peT, pebias, hcT, crope = (sm[k] for k in ["kcT", "vcaug", "w2e", "peT", "pebias", "hcT", "crope"])
            for j in range(2):
                dma("gpsimd", w1v[0:64, j, :, :], w1_d[l, j].rearrange("l d f -> d l f"), "d_m2")
            dma("gpsimd", w2e[:], w2e_d[l], "d_m2")
            dma("gpsimd", peT[:], peT_d[l], "d_m2")
            dma("sync", crope[:], crope_d, "d_m2")
            for g in range(2):
                dma("sync", vcaug[:, g, 64:97], vcc_d, "d_m2")
            slot, sem = ring()
            sv = slot[:, 0:2048].rearrange("p (k f) -> p k f", k=8)
            dma("gpsimd", sv, mixv(l)[:, :, OFF_KC:OFF_KC + 256], sem)
            for tb in range(4):
                for idx in range(4):
                    bk = bank()
                    for kc in range(8):
                        mm(PS(bk, 0, 64), sv[:, kc, idx * 64:(idx + 1) * 64], N_(kc, tb), start=(kc == 0), stop=(kc == 7))
                    act(cmpT[0:64, idx, tb * 512:(tb + 1) * 512], PS(bk, 0, 64), AF.Copy)
            for j in range(2):
                bp = bank()
                for ll in range(32):
                    mm(PS(bp, 0, 128, 0, 1), w1v[0:64, j, ll, :], peT[0:64, j, ll:ll + 1], start=(ll == 0), stop=(ll == 31))
                vcopy(pebias[:, j:j + 1], PS(bp, 0, 128, 0, 1))
                for g in range(2):
                    bh = bank()
                    for ll in range(32):
                        mm(PS(bh, 0, 128, 0, NCMP), w1v[0:64, j, ll, :], cmpT[0:64, j * 2 + g, ll:ll + 16 * (NCMP - 1) + 1:16],
                           start=(ll == 0), stop=(ll == 31))
                    act(hcT[:, 0:NCMP], PS(bh, 0, 128, 0, NCMP), AF.Silu, bias=pebias[:, j:j + 1])
                    if j == 0:
                        b1 = bank()
                        mm(PS(b1, 0, 64, 0, NCMP), w2e[:, 0:64], hcT[:, 0:NCMP])
                        b2 = bank()
                        mm(PS(b2, 0, 64, 0, NCMP), w2e[:, 64:128], hcT[:, 0:NCMP])
                        t1 = av(SCR, 2 * KB, F32)[0:64, 0:NCMP]
                        t2 = av(SCR + 2 * KB, 2 * KB, F32)[0:64, 0:NCMP]
                        tt(t1, PS(b1, 0, 64, 0, NCMP), crope[:, 0, :], ALU.mult)
                        tt(t2, PS(b2, 0, 64, 0, NCMP), crope[:, 1, :], ALU.mult)
                        tt(kcT[:, g, 0:NCMP], t1, t2, ALU.add)
                    else:
                        b2 = bank()
                        mm(PS(b2, 0, NCMP, 0, 64), hcT[:, 0:NCMP], w2e[:, 128:192])
                        act(vcaug[0:NCMP, g, 0:64], PS(b2, 0, NCMP, 0, 64), AF.Copy)

        def stage_nsa_group(l, g, interleave):
            kcT, vcaug = sm["kcT"], sm["vcaug"]
            rden, coef, imp, impf, imp2, m8, selpen = (sm[k] for k in ["rden", "coef", "imp", "impf", "imp2", "m8", "selpen"])
            rden2, coef2 = sm["rden2"], sm["coef2"]
            mv = mixv(l)
            slot, sem = ring()
            sv = slot.rearrange("p (k f) -> p k f", k=8)
            dma("gpsimd", sv, mv[:, :, OFF_QG[g]:OFF_QG[g] + 512], sem)
            T1 = SCR
            nrot = [0]

            def rope_pair(bq, br, tb):
                i = nrot[0] % 2
                nrot[0] += 1
                t1 = av(T1 + i * 4 * KB, 2 * KB, F32)
                t2 = av(T1 + i * 4 * KB + 2 * KB, 2 * KB, F32)
                stg = av(SCR + 14 * KB + i * KB, KB, BF16)
                tt(t1, PS(bq), rope[:, 0, tb * 512:(tb + 1) * 512], ALU.mult)
                tt(t2, PS(br), rope[:, 1, tb * 512:(tb + 1) * 512], ALU.mult)
                return t1, t2, stg

            for hp in range(2):
                for tb in range(4):
                    bq = bank()
                    for kc in range(8):
                        mm(PS(bq), sv[:, kc, hp * 128:(hp + 1) * 128], N_(kc, tb), start=(kc == 0), stop=(kc == 7))
                    br = bank()
                    for kc in range(8):
                        mm(PS(br), sv[:, kc, 256 + hp * 128:256 + (hp + 1) * 128], N_(kc, tb), start=(kc == 0), stop=(kc == 7))
                    t1, t2, stg = rope_pair(bq, br, tb)
                    tt(Qaug[0:64, tb * 4:(tb + 1) * 4, 2 * hp, :], t1[0:64].rearrange("p (q t) -> p q t", q=4),
                       t2[0:64].rearrange("p (q t) -> p q t", q=4), ALU.add)
                    tt(stg[64:128], t1[64:128], t2[64:128], ALU.add)
                    dma("sync", Qaug[0:64, tb * 4:(tb + 1) * 4, 2 * hp + 1, :], stg[64:128].rearrange("p (q t) -> p q t", q=4), "d_q")
            slot, sem = ring()
            sv = slot[:, 0:2048].rearrange("p (k f) -> p k f", k=8)
            dma("gpsimd", sv, mv[:, :, OFF_KG[g]:OFF_KG[g] + 256], sem)
            for tb in range(4):
                bq = bank()
                for kc in range(8):
                    mm(PS(bq), sv[:, kc, 0:128], N_(kc, tb), start=(kc == 0), stop=(kc == 7))
                br = bank()
                for kc in range(8):
                    mm(PS(br), sv[:, kc, 128:256], N_(kc, tb), start=(kc == 0), stop=(kc == 7))
                t1, t2, stg = rope_pair(bq, br, tb)
                tt(Ksel[0:64, tb * 512:(tb + 1) * 512], t1[0:64], t2[0:64], ALU.add)
                tt(stg[64:128], t1[64:128], t2[64:128], ALU.add)
                dma("sync", Kwin[0:64, tb * 512:(tb + 1) * 512], stg[64:128], "d_q")
            memset(Vs[:, :, 64:65], 1.0)
            memset(Vw[:, :, 64:65], 1.0)
            slot, sem = ring()
            sv = slot[:, 0:8 * 140].rearrange("p (k f) -> p k f", k=8)
            dma("gpsimd", sv, mv[:, :, OFF_VG[g]:OFF_VG[g] + 140], sem)
            for tq in range(16):
                bv = bank()
                for kc in range(8):
                    mm(PS(bv, 0, 128, 0, 140), nv[:, kc, tq * 128:(tq + 1) * 128], sv[:, kc, :], start=(kc == 0), stop=(kc == 7))
                act(Vs[:, tq, 0:64], PS(bv, 0, 128, 0, 64), AF.Copy)
                act(Vw[:, tq, 0:64], PS(bv, 0, 128, 64, 128), AF.Copy)
                act(gates[:, tq, :, :], PS(bv, 0, 128, 128, 140).rearrange("p (h b) -> p h b", h=4), AF.Sigmoid)
            PT0 = SCR + 8 * KB
            npt = [0]
            nsc = [0]

            def new_pT():
                i = npt[0] % 4
                npt[0] += 1
                return av(PT0 + i * KB, KB, BF16)

            def yatok_of(qt):
                return av(PT0 + 6 * KB + (qt % 2) * KB, KB, F32).rearrange("p (h d) -> p h d", h=4)

            def cmp_qk(qt):
                ncq = min(NCMP, 8 * qt + 7)
                q64 = Qaug[0:64, qt, :, :].rearrange("p h t -> p (h t)")
                bc = bank(fixed=6)
                mm(PS(bc, 0, ncq), kcT[0:64, g, 0:ncq], q64)
                ea = av(PT0 + 4 * KB, KB, BF16)
                eb = av(PT0 + 5 * KB, KB, BF16)
                act(ea[0:ncq, :], PS(bc, 0, ncq), AF.Exp, scale=0.125)
                eb3 = eb[0:ncq, :].rearrange("p (h t) -> p h t", h=4)
                ea3 = ea[0:ncq, :].rearrange("p (h t) -> p h t", h=4)
                P.op("gpsimd", lambda e, eb3=eb3, ea3=ea3, qt=qt: e.affine_select(
                    out=eb3, in_=ea3, pattern=[[0, 4], [1, 128]], compare_op=ALU.is_ge, fill=0.0,
                    base=128 * qt - 31, channel_multiplier=-16), reads=rk(ea3), writes=rk(eb3))

            def cmp_pv(qt):
                ncq = min(NCMP, 8 * qt + 7)
                yatok = yatok_of(qt)
                eb = av(PT0 + 5 * KB, KB, BF16)
                bo = bank(fixed=6)
                for hh in range(4):
                    mm(PS(bo, 0, 128, hh * 128, hh * 128 + 97), eb[0:ncq, hh * 128:(hh + 1) * 128], vcaug[0:ncq, g, :])
                bo3 = PS(bo).rearrange("p (h c) -> p h c", h=4)
                ts(rden[:].unsqueeze(2), bo3[:, :, 64:65], 1e-30, ALU.max)
                P.op("vector", lambda e: e.reciprocal(out=rden[:], in_=rden[:]), reads=rk(rden[:]), writes=rk(rden[:]))
                ts(imp[:], bo3[:, 0, 65:97], rden[:, 0:1], ALU.mult)
                for hh in range(1, 4):
                    stt(imp[:], bo3[:, hh, 65:97], rden[:, hh:hh + 1], imp[:], ALU.mult, ALU.add)
                tt(impf[:], imp[:], fv[:, qt, 1, :], ALU.mult)
                tt(impf[:], impf[:], fv[:, qt, 0, :], ALU.add)
                P.op("vector", lambda e: e.max(out=m8[:, 0:8], in_=impf[:]), reads=rk(impf[:]), writes=rk(m8[:]))
                P.op("vector", lambda e: e.match_replace(out=imp2[:], in_to_replace=m8[:, 0:8], in_values=impf[:], imm_value=-1e30),
                     reads=rk(impf[:], m8[:]), writes=rk(imp2[:]))
                P.op("vector", lambda e: e.max(out=m8[:, 8:16], in_=imp2[:]), reads=rk(imp2[:], m8[:]), writes=rk(m8[:]))
                ts(selpen[:, 64:96], impf[:], m8[:, 15:16], ALU.is_ge, 1.0, ALU.subtract)
                tt(coef[:].unsqueeze(2), gates[:, qt, :, 0:1], rden[:].unsqueeze(2), ALU.mult)
                tt(yatok, bo3[:, :, 0:64], coef[:].unsqueeze(2).broadcast_to([128, 4, 64]), ALU.mult)

            def selpen_T(qt):
                bt = bank(fixed=6)
                transpose(PSB(bt, 0, 96, 0, 128), selpen[:, 0:96], identb[:])
                vcopy(Qaug[64:96, qt, :, :], PSB(bt, 64, 96, 0, 128).unsqueeze(1).broadcast_to([32, 4, 128]))

            BR = {2: (Kwin, 64, Vw, 4, sm["rden3"], sm["coef3"]), 1: (Ksel, 96, Vs, 5, rden2, coef2)}

            def issue(it):
                qt, br, kt = it["qt"], it["br"], it["kt"]
                Kd, krows = BR[br][0], BR[br][1]
                qrhs = Qaug[0:krows, qt, :, :].rearrange("p h t -> p (h t)")
                b = nsc[0] % 4
                nsc[0] += 1
                mi = None
                if kt == qt:
                    mi = 0
                elif br == 2 and kt == qt - 4:
                    mi = 1
                mm(PS(b), Kd[0:krows, kt * 128:(kt + 1) * 128], qrhs, start=True, stop=(mi is None))
                if mi is not None:
                    mm(PS(b), identb[:], mneg[:, mi, :], start=False, stop=True)
                return b

            def process(it, b):
                br, kt = it["br"], it["kt"]
                Vd, bacc = BR[br][2], BR[br][3]
                pT = av(PT0 + (npt[0] % 4) * KB, KB, BF16)
                npt[0] += 1
                act(pT, PS(b), AF.Exp, scale=0.125)
                for hh in range(4):
                    mm(PS(bacc, 0, 128, hh * 128, hh * 128 + 65), pT[:, hh * 128:(hh + 1) * 128],
                       Vd[:, kt, :], start=(it["first"] and hh == 0), stop=it["last"], sgc=True)

            def fin_branch(qt, br):
                bacc, rd, cf = BR[br][3], BR[br][4], BR[br][5]
                yatok = yatok_of(qt)
                ba3 = PS(bacc).rearrange("p (h c) -> p h c", h=4)
                P.op("vector", lambda e: e.reciprocal(out=rd[:].unsqueeze(2), in_=ba3[:, :, 64:65]),
                     reads=rk(ba3[:, :, 64:65]), writes=rk(rd[:]))
                tt(cf[:].unsqueeze(2), gates[:, qt, :, br:br + 1], rd[:].unsqueeze(2), ALU.mult)
                tmp = av(SCR + 5 * KB, KB, F32).rearrange("p (h d) -> p h d", h=4)
                tt(tmp, ba3[:, :, 0:64], cf[:].unsqueeze(2).broadcast_to([128, 4, 64]), ALU.mult)
                tt(yatok, yatok, tmp, ALU.add)

            def yab_of(qt):
                return av(SCR + 6 * KB + (qt % 2) * KB, 512, BF16)

            def fin_dve(qt):
                vcopy(yab_of(qt), yatok_of(qt).rearrange("p h d -> p (h d)"))

            def fin_pe(qt):
                yab = yab_of(qt)
                by = bank(fixed=4)
                for j in range(2):
                    transpose(PSB(by, 0, 128, j * 128, (j + 1) * 128), yab[:, j * 128:(j + 1) * 128], identb[:])
                act(yaT[:, 2 * g:2 * g + 2, qt * 128:(qt + 1) * 128],
                    PSB(by, 0, 128, 0, 256).rearrange("p (j t) -> p j t", j=2), AF.Copy)

            items = []
            for qt in range(16):
                for br, kts in ((2, list(range(max(0, qt - 4), qt + 1))), (1, list(range(0, qt + 1)))):
                    for ki, kt in enumerate(kts):
                        items.append(dict(qt=qt, br=br, kt=kt, ki=ki, first=(ki == 0), last=(ki == len(kts) - 1)))
            for q0 in range(2):
                cmp_qk(q0)
                cmp_pv(q0)
                selpen_T(q0)
            LOOK = 3
            inflight = [issue(items[k]) for k in range(LOOK)]
            nexti = LOOK
            for idx, it in enumerate(items):
                qt, br = it["qt"], it["br"]
                if it["first"] and br == 2 and interleave is not None:
                    next(interleave, None)
                b = inflight.pop(0)
                if nexti < len(items):
                    inflight.append(issue(items[nexti]))
                    nexti += 1
                process(it, b)
                if br == 2 and 2 <= qt + 1 < 16:
                    if it["ki"] == 0:
                        cmp_qk(qt + 1)
                    elif it["ki"] == 1:
                        cmp_pv(qt + 1)
                if it["last"]:
                    fin_branch(qt, br)
                    if br == 1:
                        if 2 <= qt + 1 < 16:
                            selpen_T(qt + 1)
                        fin_dve(qt)
                        if qt >= 1:
                            fin_pe(qt - 1)
            fin_pe(15)

        def stage_nsa(l, interleave):
            dma("sync", rope[0:64, :, :], rope_d, "d_m3")
            dma("sync", rope[64:128, :, :], rope_d, "d_m3")
            dma("sync", Ksel[64:96, :], erows_d, "d_m3")
            memset(sm["selpen"][:, 0:64], 0.0)
            for g in range(2):
                stage_nsa_group(l, g, interleave)

        def stage_gmlp(l):
            G0 = 72 * KB
            wsT = av(G0, 2 * KB, BF16).rearrange("p (g t) -> p g t", g=8)
            bsT = av(G0 + 2 * KB, 2 * KB, F32).rearrange("p (j t) -> p j t", j=4)
            lng = av(G0 + 4 * KB, 2 * KB, F32)
            lnb = av(G0 + 6 * KB, 2 * KB, F32)
            dma("gpsimd", wsT, wsT_d[l], "d_m1")
            dma("sync", bsT, bsT_d[l], "d_m1")
            dma("sync", lng, gmln_d[l, 0].partition_broadcast(128), "d_m1")
            dma("sync", lnbT[:], lnbT_d[l], "d_m1")
            tt(wsT, wsT, tri[:, 0, :].unsqueeze(1).broadcast_to([128, 8, 128]), ALU.mult)
            BTp = lnb.rearrange("p (j t) -> p j t", j=4)
            bp0 = bank_pair()
            bq4 = psum[:, bp0 * 512:(bp0 + 2) * 512].rearrange("p (j a t) -> p j a t", j=4, a=2)
            for j in range(4):
                for a in range(2):
                    mm(bq4[:, j, a, :], onesb[:], wsT[:, 2 * j + a, :])
            for j in range(4):
                stt(BTp[0:64, j, :], bq4[0:64, j, 0, :], lnbT[0:64, j:j + 1], bsT[0:64, j, :], ALU.mult, ALU.add)
                stt(BTp[64:128, j, :], bq4[64:128, j, 1, :], lnbT[64:128, j:j + 1], bsT[64:128, j, :], ALU.mult, ALU.add)
            mv = mixv(l)
            slot, sem = ring()
            su = slot.rearrange("p (k f) -> p k f", k=8)
            dma("gpsimd", su, mv[:, :, OFF_U:OFF_U + 512], sem)
            for fc in range(4):
                for tb in range(4):
                    bk = bank()
                    for kc in range(8):
                        mm(PS(bk), su[:, kc, fc * 128:(fc + 1) * 128], N_(kc, tb), start=(kc == 0), stop=(kc == 7))
                    act(uT[:, fc, tb * 512:(tb + 1) * 512], PS(bk), AF.Gelu_apprx_tanh)
            slot, sem = ring()
            svv = slot.rearrange("p (k f) -> p k f", k=8)
            dma("gpsimd", svv, mv[:, :, OFF_V:OFF_V + 512], sem)
            bnst, bnmv, lnr = sm["bnst"], sm["bnmv"], sm["lnr"]

            def vproj(tq):
                bk = bank()
                for kc in range(8):
                    mm(PS(bk), nv[:, kc, tq * 128:(tq + 1) * 128], svv[:, kc, :], start=(kc == 0), stop=(kc == 7))
                return bk

            bk_next = vproj(0)
            for tq in range(16):
                bk = bk_next
                if tq + 1 < 16:
                    bk_next = vproj(tq + 1)
                vg = av(G0 + 8 * KB + (tq % 2) * 2 * KB, 2 * KB, F32)
                act(vg, PS(bk), AF.Gelu_apprx_tanh)
                P.op("vector", lambda e, vg=vg: e.bn_stats(out=bnst[:], in_=vg), reads=rk(vg), writes=rk(bnst[:]))
                P.op("vector", lambda e: e.bn_aggr(out=bnmv[:], in_=bnst[:]), reads=rk(bnst[:]), writes=rk(bnmv[:]))
                ts(lnr[:], bnmv[:, 1:2], EPS, ALU.add)
                tt(lnr[:], lnr[:], nhalf[:, 0:1], ALU.pow, eng="gpsimd")
                ts(vg, vg, bnmv[:, 0:1], ALU.subtract, lnr[:, 0:1], ALU.mult)
                vtok = av(G0 + 12 * KB + (tq % 2) * KB, KB, BF16)
                tt(vtok, vg, lng, ALU.mult)
                bp = bank_pair()
                bp4 = psum[:, bp * 512:(bp + 2) * 512].rearrange("p (j a t) -> p j a t", j=4, a=2)
                for j in range(4):
                    for a in range(2):
                        mm(bp4[:, j, a, :], vtok[:, j * 128:(j + 1) * 128], wsT[:, 2 * j + a, :])
                tmp = av(G0 + 14 * KB, 2 * KB, F32).rearrange("p (j t) -> p j t", j=4)
                tt(tmp[0:64], bp4[0:64, :, 0, :], BTp[0:64], ALU.add)
                tt(tmp[64:128], bp4[64:128, :, 1, :], BTp[64:128], ALU.add)
                tt(ybT[:, :, tq * 128:(tq + 1) * 128], tmp, uT[:, :, tq * 128:(tq + 1) * 128], ALU.mult)

        def stage_merge(l):
            mv = mixv(l)
            pav = proj_a_d[l].rearrange("(kc p) f -> p kc f", p=128)
            pbv = proj_b_d[l].rearrange("(kc p) f -> p kc f", p=128)
            X0 = 88 * KB
            nx = [0]
            for fc in range(8):
                slot, sem = ring()
                sg = slot[:, 0:2048].rearrange("p (k f) -> p k f", k=8)
                spa = slot[:, 2048:2560].rearrange("p (k f) -> p k f", k=4)
                spb = slot[:, 2560:3072].rearrange("p (k f) -> p k f", k=4)
                dma("gpsimd", sg, mv[:, :, OFF_GAB + fc * 256:OFF_GAB + (fc + 1) * 256], sem)
                dma("gpsimd", spa, pav[:, :, fc * 128:(fc + 1) * 128], sem)
                dma("gpsimd", spb, pbv[:, :, fc * 128:(fc + 1) * 128], sem)
                for tb in range(4):
                    bA = bank()
                    for kc in range(8):
                        mm(PS(bA), sg[:, kc, 0:128], N_(kc, tb), start=(kc == 0), stop=(kc == 7))
                    bB = bank()
                    for kc in range(8):
                        mm(PS(bB), sg[:, kc, 128:256], N_(kc, tb), start=(kc == 0), stop=(kc == 7))
                    bPA = bank()
                    for kc in range(4):
                        mm(PS(bPA), spa[:, kc, :], yaT[:, kc, tb * 512:(tb + 1) * 512], start=(kc == 0), stop=(kc == 3))
                    bPB = bank()
                    for kc in range(4):
                        mm(PS(bPB), spb[:, kc, :], ybT[:, kc, tb * 512:(tb + 1) * 512], start=(kc == 0), stop=(kc == 3))
                    i = nx[0] % 2
                    nx[0] += 1
                    sga = av(X0 + i * 4 * KB, 2 * KB, F32)
                    sgb = av(X0 + i * 4 * KB + 2 * KB, 2 * KB, F32)
                    act(sga, PS(bA), AF.Sigmoid)
                    act(sgb, PS(bB), AF.Sigmoid)
                    tt(sga, sga, PS(bPA), ALU.mult)
                    tt(sgb, sgb, PS(bPB), ALU.mult)
                    tt(merged[:, fc, tb * 512:(tb + 1) * 512], sga, sgb, ALU.add)
            if STOP == "merged":
                return
            wov = w_out_d[l].rearrange("(kc p) f -> p kc f", p=128)
            for half in range(2):
                slot, sem = ring()
                sv = slot.rearrange("p (k f) -> p k f", k=8)
                dma("gpsimd", sv, wov[:, :, half * 512:(half + 1) * 512], sem)
                for fl in range(4):
                    fo = half * 4 + fl
                    for tb in range(4):
                        bk = bank()
                        for kc in range(8):
                            mm(PS(bk), sv[:, kc, fl * 128:(fl + 1) * 128], merged[:, kc, tb * 512:(tb + 1) * 512],
                               start=(kc == 0), stop=(kc == 7))
                        stt(H(fo, tb), PS(bk), der[l][:, 1, 8 + fo:9 + fo], H(fo, tb), ALU.mult, ALU.add)

        def dump_bf16(view3, nchunks):
            for c in range(nchunks):
                for tb in range(4):
                    vcopy(H(c, tb), view3[:, c, tb * 512:(tb + 1) * 512])

        def program():
            rms_stats(0, rstd_of(0))
            rms_stats(1, rstd_of(1))
            mod0 = mod_steps(0)
            for _ in range(6 if STOP != "mod" else 18):
                next(mod0)
            if STOP == "mod":
                vcopy(hv[:, 0, 0:72], modT[0][:])
                vcopy(hv[:, 0, 72:120], der[0][:].rearrange("p a b -> p (a b)"))
                return
            for l in range(2):
                norm_mod(l, 0, pre=(2 if l == 0 else 0))
                if STOP == "n0" and l == 0:
                    dump_bf16(nv, 8)
                    return
                ffn(l, 0, 0, mod0 if l == 0 else None)
                if l == 0:
                    for _ in mod0:
                        pass
                if STOP == "h1" and l == 0:
                    return
                norm_mod(l, 1)
                stage_cmp(l)
                if STOP == "cmp" and l == 0:
                    vcopy(hv[0:64, 0, 0:256], sm["kcT"][:].rearrange("p a b -> p (a b)"))
                    vcopy(hv[:, 1, 0:194], sm["vcaug"][:].rearrange("p a b -> p (a b)"))
                    return
                inter = mod_steps(1) if l == 0 else None
                stage_nsa(l, inter)
                if inter is not None:
                    for _ in inter:
                        pass
                if STOP == "ya" and l == 0:
                    dump_bf16(yaT, 4)
                    return
                stage_gmlp(l)
                if STOP == "yb" and l == 0:
                    dump_bf16(ybT, 4)
                    return
                stage_merge(l)
                if STOP == "merged" and l == 0:
                    dump_bf16(merged, 8)
                    return
                if STOP == "h2" and l == 0:
                    return
                norm_mod(l, 2)
                ffn(l, 1, 2)
                if STOP == "h3" and l == 0:
                    return
            final_norm()

        program()
        outv = outT_d.rearrange("(c p) t -> p c t", p=128)
        for tb in range(4):
            dma("sync", outv[:, :, tb * 512:(tb + 1) * 512], hv[:, :, tb * 512:(tb + 1) * 512], "d_o")
        P.final_wait("sync", ["d_o"])
        P.emit(block, sems)
    return nc


def _consts():
    inv = 1.0 / (10000.0 ** (np.arange(0, 64, 2, dtype=np.float32) / 64.0))
    pos = np.arange(S, dtype=np.float32)
    ang = pos[:, None] * inv[None, :]
    cos, sin = np.cos(ang).astype(np.float32), np.sin(ang).astype(np.float32)
    rope = np.stack([np.concatenate([cos, cos], 1).T, np.concatenate([-sin, sin], 1).T], 1)
    cend = (np.arange(NCMP) * 16 + 31).astype(np.float32)
    angc = cend[:, None] * inv[None, :]
    cc, cs = np.cos(angc).astype(np.float32), np.sin(angc).astype(np.float32)
    crope = np.stack([np.concatenate([cc, cc], 1).T, np.concatenate([-cs, cs], 1).T], 1)
    cmask = np.zeros((128, S), np.float32)
    cmask[:NCMP] = (cend[:, None] <= pos[None, :]).astype(np.float32)
    k = np.arange(128)
    triS = (k[:, None] <= k[None, :]).astype(np.float32)
    triW = (k[:, None] > k[None, :]).astype(np.float32)
    tri = np.stack([triS, triW], 1)
    mneg = np.stack([np.tile((1.0 - triS) * -30000.0, (1, 4)), np.tile((1.0 - triW) * -30000.0, (1, 4))], 1)
    erows = (np.arange(S)[None, :] // 64 == np.arange(32)[:, None]).astype(np.float32) * 32768.0
    starts = np.arange(NCMP) * 16
    sel_start = np.arange(32) * 64
    ov = np.clip(np.minimum(starts[:, None] + 32, sel_start[None, :] + 64) - np.maximum(starts[:, None], sel_start[None, :]),
                 0, None).astype(np.float32) / 32.0
    vcc = np.zeros((128, 33), np.float32)
    vcc[:, 0] = 1.0
    vcc[:NCMP, 1:] = ov
    t = np.arange(S)
    cur = t // 64
    blk = np.arange(32)
    forced = (blk[None, :] == 0) | (blk[None, :] == cur[:, None]) | (blk[None, :] == cur[:, None] - 1)
    valid = blk[None, :] <= cur[:, None]
    fadd = np.where(forced, 1e4, np.where(valid, 0.0, -1e4)).astype(np.float32)
    vnf = (valid & ~forced).astype(np.float32)
    fv = np.stack([fadd, vnf], 1).reshape(16, 128, 2, 32).transpose(1, 0, 2, 3)
    bf = ml_dtypes.bfloat16
    return {
        "c_rope": np.ascontiguousarray(rope, np.float32), "c_crope": np.ascontiguousarray(crope, np.float32),
        "c_tri": np.ascontiguousarray(tri).astype(bf), "c_mneg": np.ascontiguousarray(mneg).astype(bf), "c_erows": erows.astype(bf),
        "c_vcc": vcc.astype(bf), "c_fv": np.ascontiguousarray(fv, np.float32), "c_identb": np.eye(128, dtype=np.float32).astype(bf),
    }


def _prep_shared(inp):
    perm = np.concatenate([np.arange(32, 64), np.arange(0, 32)])
    mw = inp["mix_w_in"]

    def kvcol(s, g):
        return 512 + (s * 2 + g) * 64 + np.arange(64)

    cols = [np.arange(1304, 1816), np.arange(1816, 2328), np.arange(512, 768)]
    for g in range(2):
        qh = [g * 256 + hh * 64 + np.arange(64) for hh in range(4)]
        cols += qh + [q[perm] for q in qh]
        cols += [kvcol(2, g), kvcol(4, g), kvcol(2, g)[perm], kvcol(4, g)[perm]]
        cols += [kvcol(3, g), kvcol(5, g), 1280 + g * 12 + np.arange(12)]
    for fc in range(8):
        cols += [2328 + fc * 128 + np.arange(128), 3352 + fc * 128 + np.arange(128)]
    cols = np.concatenate(cols)
    assert cols.shape[0] == NEXT
    mixw = np.ascontiguousarray(mw[:, :, cols])
    w2 = inp["cmp_w2"]
    w2e = np.ascontiguousarray(np.concatenate([w2[:, 0], w2[:, 0][:, :, perm], w2[:, 1]], axis=2))
    sh = {
        "ada_w": inp["ada_w"],
        "ada_bT": np.ascontiguousarray(inp["ada_b"].reshape(2, 72, 128).transpose(2, 0, 1)),
        "normg": np.ascontiguousarray(np.concatenate([inp["norm_g"].reshape(6, 8, 128), inp["final_g"].reshape(1, 8, 128)], 0)
                                      .reshape(56, 128).T),
        "ffn_w_in": inp["ffn_w_in"], "ffn_w_out": inp["ffn_w_out"], "mixw": mixw,
        "cmp_peT": np.ascontiguousarray(inp["cmp_pe"].transpose(0, 3, 1, 2)),
        "cmp_w1": inp["cmp_w1"], "cmp_w2e": w2e,
        "gm_ln": np.ascontiguousarray(np.stack([inp["gm_ln_g"], inp["gm_ln_b"]], 1)),
        "gm_wsT": np.ascontiguousarray(inp["gm_ws"].transpose(0, 3, 1, 2)),
        "gm_lnbT": np.ascontiguousarray(inp["gm_ln_b"].reshape(2, 4, 128).transpose(0, 2, 1)),
        "gm_bsT": np.ascontiguousarray(np.repeat(inp["gm_bs"], 64, axis=1).reshape(2, 4, 128, 128).transpose(0, 2, 1, 3)),
        "proj_a": inp["proj_a"], "proj_b": inp["proj_b"], "w_out": inp["w_out"],
    }
    sh.update(_consts())
    return sh


def kernel(**inputs):
    inp = {k: np.asarray(v) for k, v in inputs.items()}
    shared = _prep_shared(inp)
    ncores = int(os.environ.get("MK_NCORES", "8"))
    in_maps = []
    for b in range(ncores):
        m = dict(shared)
        m["xT"] = np.ascontiguousarray(inp["x"][b].T)
        m["cT"] = np.ascontiguousarray(inp["c"][b].reshape(8, 128).T)
        in_maps.append(m)
    nc = build_program()
    res = run_bass_kernel_spmd(nc, in_maps, core_ids=list(range(ncores)))
    out = np.stack([np.ascontiguousarray(r["outT"].T) for r in res.results], 0)
    return out.astype(np.float32)
```

```python
import os
import numpy as np
import ml_dtypes
from contextlib import ExitStack
import concourse.bass as bass
import concourse.mybir as mybir
from concourse.bass_utils import run_bass_kernel_spmd

F32 = mybir.dt.float32
BF16 = mybir.dt.bfloat16
AF = mybir.ActivationFunctionType
ALU = mybir.AluOpType
DS = {F32: 4, BF16: 2}

S = 2048
D = 1024
DFF = 2816
NCMP = 127
EPS = 1e-6
ENGS = ["sync", "scalar", "vector", "gpsimd", "tensor"]
GRAN = {"hT": 2048, "nT": 1024, "arena": 1024, "ps": 2048}

OFF_U = 0
OFF_V = 512
OFF_KC = 1024
OFF_QG = [1280, 1280 + 908]
OFF_KG = [1280 + 512, 1280 + 908 + 512]
OFF_VG = [1280 + 768, 1280 + 908 + 768]
OFF_GAB = 1280 + 2 * 908
NEXT = OFF_GAB + 2048

ARENA_KIB = 96
STOP = os.environ.get("MK_STOP", "")


def ap_keys(ap):
    name = ap.tensor.name
    if name not in GRAN:
        return [name]
    g = GRAN[name]
    ds = DS[ap.dtype]
    pat = ap.ap
    pstep = pat[0][0]
    off = (ap.offset % pstep) if pstep else ap.offset
    dims = [(s, n) for (s, n) in pat[1:]]
    res = set()

    def rec(i, base):
        if i >= len(dims):
            res.add(base * ds // g)
            return
        s, n = dims[i]
        if i == len(dims) - 1:
            lo = base
            hi = base + (n - 1) * abs(s)
            for ch in range(lo * ds // g, (hi * ds + ds - 1) // g + 1):
                res.add(ch)
        else:
            if s == 0:
                n = 1
            for j in range(n):
                rec(i + 1, base + j * s)

    rec(0, off)
    if name == "arena":
        p0 = ap.offset // pstep if pstep else 0
        q0, q1 = p0 // 32, (p0 + pat[0][1] - 1) // 32
        return [f"{name}{c}q{q}" for c in sorted(res) for q in range(q0, q1 + 1)]
    return [f"{name}{c}" for c in sorted(res)]


class Prog:
    def __init__(self):
        self.ops = {e: [] for e in ENGS}
        self.cnt = {}
        self.last_w = {}
        self.readers = {}
        self.seen = {e: {} for e in ENGS}
        self.dma_sems = set()

    def op(self, eng, fn, reads=(), writes=(), dma=None):
        need = {}

        def add(ev):
            if ev is None:
                return
            s, v = ev
            if s in self.dma_sems:
                v = self.cnt[s]
            if need.get(s, 0) < v:
                need[s] = v

        for k in reads:
            add(self.last_w.get(k))
        for k in writes:
            add(self.last_w.get(k))
            for s, v in self.readers.get(k, {}).items():
                add((s, v))
        own = "e_" + eng
        waits = []
        for s, v in need.items():
            if s == own and eng == "tensor":
                continue
            if self.seen[eng].get(s, 0) >= v:
                continue
            self.seen[eng][s] = v
            waits.append((s, v))
        if dma is None:
            sem, inc = own, 1
        else:
            sem, inc = dma, 16
            self.dma_sems.add(dma)
        self.cnt[sem] = self.cnt.get(sem, 0) + inc
        ev = (sem, self.cnt[sem])
        for k in writes:
            self.last_w[k] = ev
            self.readers[k] = {}
        for k in reads:
            d = self.readers.setdefault(k, {})
            d[sem] = max(d.get(sem, 0), ev[1])
        self.ops[eng].append((waits, fn, sem, inc))
        return ev

    def prewait(self, eng, reads):
        need = {}
        for k in reads:
            ev = self.last_w.get(k)
            if ev is None:
                continue
            s_, v = ev
            if s_ in self.dma_sems:
                v = self.cnt[s_]
            if need.get(s_, 0) < v:
                need[s_] = v
        own = "e_" + eng
        waits = []
        for s_, v in need.items():
            if s_ == own and eng == "tensor":
                continue
            if self.seen[eng].get(s_, 0) >= v:
                continue
            self.seen[eng][s_] = v
            waits.append((s_, v))
        if waits:
            self.ops[eng].append((waits, None, None, 0))

    def final_wait(self, eng, sems_):
        self.ops[eng].append(([(s, self.cnt[s]) for s in sems_], None, None, 0))

    def emit(self, block, sems):
        for e in ENGS:
            ops = self.ops[e]

            def body(eng, ops=ops):
                for waits, fn, sem, inc in ops:
                    for s, v in waits:
                        eng.wait_ge(sems[s], v)
                    if fn is not None:
                        fn(eng).then_inc(sems[sem], inc)

            getattr(block, e)(body)


def build_program():
    nc = bass.Bass("TRN2", target_bir_lowering=False)
    P = Prog()

    def dram(name, shape, dt=F32, kind="ExternalInput"):
        return nc.dram_tensor(name, list(shape), dt, kind=kind).ap()

    xT_d = dram("xT", [D, S])
    cT_d = dram("cT", [128, 8])
    ada_w_d = dram("ada_w", [2, D, 9 * D])
    ada_bT_d = dram("ada_bT", [128, 2, 72])
    normg_d = dram("normg", [128, 56])
    ffn_w_in_d = dram("ffn_w_in", [2, 2, D, 2 * DFF])
    ffn_w_out_d = dram("ffn_w_out", [2, 2, DFF, D])
    mixw_d = dram("mixw", [2, D, NEXT])
    peT_d = dram("cmp_peT", [2, 64, 2, 32])
    w1_d = dram("cmp_w1", [2, 2, 32, 64, 128])
    w2e_d = dram("cmp_w2e", [2, 128, 192])
    gmln_d = dram("gm_ln", [2, 2, 512])
    wsT_d = dram("gm_wsT", [2, 128, 8, 128])
    bsT_d = dram("gm_bsT", [2, 128, 4, 128])
    lnbT_d = dram("gm_lnbT", [2, 128, 4])
    proj_a_d = dram("proj_a", [2, 512, D])
    proj_b_d = dram("proj_b", [2, 512, D])
    w_out_d = dram("w_out", [2, D, D])
    rope_d = dram("c_rope", [64, 2, S])
    crope_d = dram("c_crope", [64, 2, NCMP])
    tri_d = dram("c_tri", [128, 2, 128], BF16)
    mneg_d = dram("c_mneg", [128, 2, 512], BF16)
    erows_d = dram("c_erows", [32, S], BF16)
    vcc_d = dram("c_vcc", [128, 33], BF16)
    fv_d = dram("c_fv", [128, 16, 2, 32])
    identb_d = dram("c_identb", [128, 128], BF16)
    outT_d = dram("outT", [D, S], F32, kind="ExternalOutput")

    es = ExitStack()
    with es:
        def sb(name, shape, dt):
            return es.enter_context(nc.sbuf_tensor(name, list(shape), dt))

        hT = sb("hT", [128, 8 * S], F32)
        nT = sb("nT", [128, 8 * S], BF16)
        arena = sb("arena", [128, ARENA_KIB * 256], F32)
        psum = es.enter_context(nc.psum_tensor("ps", [128, 4096], F32))
        tri = sb("tri", [128, 2, 128], BF16)
        mneg = sb("mneg", [128, 2, 512], BF16)
        fv = sb("fv", [128, 16, 2, 32], F32)
        identb = sb("identb", [128, 128], BF16)
        onesm = sb("onesm", [128, 128], BF16)
        onesb = sb("onesb", [128, 128], BF16)
        lnbT = sb("lnbT", [128, 4], F32)
        epsc = sb("epsc", [128, 1], F32)
        nhalf = sb("nhalf", [128, 1], F32)
        cT = sb("cTs", [128, 8], F32)
        scb = sb("scb", [128, 8], BF16)
        ada_bT = sb("ada_bTs", [128, 2, 72], F32)
        normg = sb("normgs", [128, 56], F32)
        modT = [sb(f"modT{l}", [128, 72], F32) for l in range(2)]
        der = [sb(f"der{l}", [128, 3, 16], F32) for l in range(2)]
        sm = {}
        for nm, shp, dt in [("rden", [128, 4], F32), ("coef", [128, 4], F32), ("imp", [128, 32], F32),
                            ("impf", [128, 32], F32), ("imp2", [128, 32], F32), ("m8", [128, 16], F32),
                            ("selpen", [128, 96], BF16), ("bnst", [128, 6], F32), ("bnmv", [128, 2], F32),
                            ("lnr", [128, 1], F32), ("pebias", [128, 2], F32), ("peT", [64, 2, 32], BF16),
                            ("w2e", [128, 192], BF16), ("hcT", [128, 128], BF16), ("kcT", [64, 2, 128], BF16),
                            ("vcaug", [128, 2, 97], BF16), ("crope", [64, 2, NCMP], F32),
                            ("rden2", [128, 4], F32), ("coef2", [128, 4], F32),
                            ("rden3", [128, 4], F32), ("coef3", [128, 4], F32)]:
            sm[nm] = sb(nm, shp, dt)

        sem_names = ["e_" + e for e in ENGS] + ["d_r0", "d_r1", "d_r2", "d_r3", "d_c", "d_x", "d_o", "d_q", "d_m1s", "d_m1h", "d_m2s", "d_m2h", "d_m3h"]
        sems = {n: es.enter_context(nc.semaphore(n)) for n in sem_names}
        block = es.enter_context(nc.Block())

        hv = hT[:].rearrange("p (c t) -> p c t", c=8)
        nv = nT[:].rearrange("p (c t) -> p c t", c=8)

        def H(c, tb):
            return hv[:, c, tb * 512:(tb + 1) * 512]

        def N_(c, tb):
            return nv[:, c, tb * 512:(tb + 1) * 512]

        def av(off_b, size_b, dt):
            a = arena[:, off_b // 4:(off_b + size_b) // 4]
            return a.bitcast(BF16) if dt == BF16 else a

        KB = 1024

        def rk(*aps):
            ks = []
            for a in aps:
                if a is None or isinstance(a, (int, float)):
                    continue
                ks += ap_keys(a)
            return ks

        def mm(out, lhsT, rhs, start=True, stop=True, sgc=False):
            P.op("tensor", lambda e: e.matmul(out, lhsT=lhsT, rhs=rhs, start=start, stop=stop, skip_group_check=sgc),
                 reads=rk(lhsT, rhs), writes=rk(out))

        def transpose(out, in_, ident):
            P.op("tensor", lambda e: e.transpose(out, in_, ident), reads=rk(in_, ident), writes=rk(out))

        def act(out, in_, func, bias=None, scale=None):
            kw = {}
            if bias is not None:
                kw["bias"] = bias
            if scale is not None:
                kw["scale"] = scale
            P.op("scalar", lambda e: e.activation(out=out, in_=in_, func=func, **kw),
                 reads=rk(in_, bias, scale), writes=rk(out))

        def tt(out, in0, in1, op, eng="vector"):
            P.op(eng, lambda e: e.tensor_tensor(out=out, in0=in0, in1=in1, op=op), reads=rk(in0, in1), writes=rk(out))

        def ts(out, in0, s1, op0, s2=None, op1=None, eng="vector"):
            if op1 is None:
                P.op(eng, lambda e: e.tensor_scalar(out=out, in0=in0, scalar1=s1, scalar2=None, op0=op0),
                     reads=rk(in0, s1), writes=rk(out))
            else:
                P.op(eng, lambda e: e.tensor_scalar(out=out, in0=in0, scalar1=s1, scalar2=s2, op0=op0, op1=op1),
                     reads=rk(in0, s1, s2), writes=rk(out))

        def stt(out, in0, scalar, in1, op0, op1):
            P.op("vector", lambda e: e.scalar_tensor_tensor(out=out, in0=in0, scalar=scalar, in1=in1, op0=op0, op1=op1),
                 reads=rk(in0, scalar, in1), writes=rk(out))

        def vcopy(out, in_, eng="vector"):
            P.op(eng, lambda e: e.tensor_copy(out=out, in_=in_), reads=rk(in_), writes=rk(out))

        def memset(ap, val, eng="vector"):
            P.op(eng, lambda e: e.memset(ap, val), writes=rk(ap))

        def dma(eng, out, in_, sem, **kw):
            rd = rk(in_) if in_.tensor.name in SBN else []
            wr = rk(out) if out.tensor.name in SBN else [out.tensor.name]
            if sem.startswith("d_m"):
                sem = sem + ("s" if eng == "gpsimd" else "h")
            return P.op(eng, lambda e: e.dma_start(out=out, in_=in_, **kw), reads=rd, writes=wr, dma=sem)

        SBN = set(["hT", "nT", "arena", "tri", "mneg", "fv", "identb", "onesm", "onesb", "lnbT", "epsc", "nhalf", "cTs", "scb", "ada_bTs", "normgs",
                   "modT0", "modT1", "der0", "der1"] + list(sm.keys()))

        bstate = {"next": 0, "held": set()}

        def bank(hold=False, fixed=None):
            if fixed is not None:
                if hold:
                    bstate["held"].add(fixed)
                return fixed
            for _ in range(16):
                i = bstate["next"]
                bstate["next"] = (i + 1) % 8
                if i not in bstate["held"]:
                    if hold:
                        bstate["held"].add(i)
                    return i
            raise RuntimeError("no psum bank")

        def bank_pair():
            for _ in range(16):
                i = bstate["next"]
                if i % 2 == 1:
                    i = (i + 1) % 8
                bstate["next"] = (i + 2) % 8
                if i not in bstate["held"] and (i + 1) not in bstate["held"]:
                    return i
            raise RuntimeError("no psum bank pair")

        def release(i):
            bstate["held"].discard(i)

        def PS(i, p0=0, p1=128, c0=0, c1=512):
            return psum[p0:p1, i * 512 + c0:i * 512 + c1]

        def PSB(i, p0, p1, c0, c1):
            return psum[p0:p1, i * 512:(i + 1) * 512].bitcast(BF16)[:, c0:c1]

        rstate = {"next": 0}

        def ring():
            i = rstate["next"]
            rstate["next"] = (i + 1) % 3
            return av(i * 8 * KB, 8 * KB, BF16), f"d_r{i}"

        for c in range(8):
            dma("sync", hv[:, c, :], xT_d[c * 128:(c + 1) * 128, :], "d_x")
        for dst, src in [(cT, cT_d), (ada_bT, ada_bT_d), (normg, normg_d), (tri, tri_d), (mneg, mneg_d),
                         (fv, fv_d), (identb, identb_d)]:
            dma("sync", dst[:], src, "d_c")
        memset(onesm[:], 1.0 / 1024.0)
        memset(epsc[:], EPS)
        memset(onesb[:], 1.0)
        memset(nhalf[:], -0.5)
        act(scb[:], cT[:], AF.Silu)

        def mod_steps(l):
            bm = bank(hold=True, fixed=7)
            awv = ada_w_d[l].rearrange("(kc p) f -> p kc f", p=128)
            for s in range(18):
                slot, sem = ring()
                sv = slot.rearrange("p (k f) -> p k f", k=8)
                if l == 0 and s == 0:
                    P.prewait("gpsimd", rk(hv[:, :, :]))
                dma("gpsimd", sv, awv[:, :, s * 512:(s + 1) * 512], sem)
                for fc in range(4):
                    j = s * 4 + fc
                    for kc in range(8):
                        mm(PS(bm, 0, 128, j, j + 1), sv[:, kc, fc * 128:(fc + 1) * 128], scb[:, kc:kc + 1],
                           start=(kc == 0), stop=(kc == 7))
                if s % 6 == 5:
                    sub = s // 6
                    tt(modT[l][:, sub * 24:(sub + 1) * 24], PS(bm, 0, 128, sub * 24, (sub + 1) * 24),
                       ada_bT[:, l, sub * 24:(sub + 1) * 24], ALU.add)
                    stt(der[l][:, sub, 0:8], modT[l][:, (sub * 3 + 1) * 8:(sub * 3 + 2) * 8], 1.0,
                        normg[:, (l * 3 + sub) * 8:(l * 3 + sub + 1) * 8], ALU.add, ALU.mult)
                    ts(der[l][:, sub, 8:16], modT[l][:, (sub * 3 + 2) * 8:(sub * 3 + 3) * 8],
                       1.0 if sub == 1 else 0.5, ALU.mult)
                    if s == 17:
                        release(bm)
                yield

        NSCR = 84 * KB

        def rms_stats(tb, rstd, all_act=False, scr=None):
            scr = NSCR if scr is None else scr
            bk = bank()
            for c in range(8):
                sq = av(scr + (c % 4) * KB, KB, BF16)
                if all_act or c in (0, 2, 5, 7):
                    act(sq, H(c, tb), AF.Square)
                else:
                    tt(sq, H(c, tb), H(c, tb), ALU.mult)
                mm(PS(bk), onesm[:], sq, start=(c == 0), stop=(c == 7))
            act(rstd, PS(bk), AF.Ln, bias=epsc[:, 0:1])
            act(rstd, rstd, AF.Exp, scale=-0.5)

        def rstd_of(tb, scr=None):
            scr = NSCR if scr is None else scr
            return av(scr + 4 * KB + (tb % 2) * 2 * KB, 2 * KB, F32)

        def norm_tb(l, sub, tb, pre=0, scr=None):
            scr = NSCR if scr is None else scr
            rstd = rstd_of(tb, scr)
            if tb >= pre:
                rms_stats(tb, rstd, scr=scr)
            for c in range(8):
                tmp = av(scr + 8 * KB + (c % 2) * 2 * KB, 2 * KB, F32)
                tt(tmp, H(c, tb), rstd, ALU.mult)
                act(N_(c, tb), tmp, AF.Identity, bias=modT[l][:, sub * 24 + c:sub * 24 + c + 1],
                    scale=der[l][:, sub, c:c + 1])

        def norm_mod(l, sub, pre=0):
            for tb in range(4):
                norm_tb(l, sub, tb, pre)

        outv = outT_d.rearrange("(c p) t -> p c t", p=128)

        def final_tb(tb):
            rstd = rstd_of(tb)
            rms_stats(tb, rstd, all_act=True)
            for c in range(8):
                stt(H(c, tb), H(c, tb), normg[:, 48 + c:49 + c], rstd, ALU.mult, ALU.mult)
            if not STOP:
                dma("sync", outv[:, :, tb * 512:(tb + 1) * 512], hv[:, :, tb * 512:(tb + 1) * 512], "d_o")

        def ffn(l, i, sub, interleave=None, tail=None):
            winv = ffn_w_in_d[l, i].rearrange("(kc p) f -> p kc f", p=128)
            woutv = ffn_w_out_d[l, i].rearrange("(kc p) f -> p kc f", p=128)
            gbuf = av(24 * KB, 48 * KB, BF16).rearrange("p (c t) -> p c t", c=12)
            SA = 72 * KB
            nsa_ = [0]
            def load_pair(ca, il=True):
                if il and interleave is not None:
                    next(interleave, None)
                slot, sem = ring()
                sv = slot.rearrange("p (k a f) -> p k a f", k=8, a=2)
                dma("gpsimd", sv[:, :, 0, :], winv[:, :, ca * 128:ca * 128 + 256], sem)
                dma("gpsimd", sv[:, :, 1, :], winv[:, :, DFF + ca * 128:DFF + ca * 128 + 256], sem)
                return sv

            def pair_tb(sv, gi, cc, tb):
                bA = bank()
                for kc in range(8):
                    mm(PS(bA), sv[:, kc, 0, cc * 128:(cc + 1) * 128], N_(kc, tb), start=(kc == 0), stop=(kc == 7))
                bB = bank()
                for kc in range(8):
                    mm(PS(bB), sv[:, kc, 1, cc * 128:(cc + 1) * 128], N_(kc, tb), start=(kc == 0), stop=(kc == 7))
                sa = av(SA + (nsa_[0] % 3) * 2 * KB, 2 * KB, F32)
                nsa_[0] += 1
                act(sa, PS(bA), AF.Silu)
                tt(gbuf[:, gi, tb * 512:(tb + 1) * 512], sa, PS(bB), ALU.mult)

            for (c0, c1) in [(0, 12), (12, 22)]:
                npairs = (c1 - c0) // 2
                jp0 = 0
                if c0 == 0:
                    svs = [load_pair(c0 + 2 * jp, il=False) for jp in range(3)]
                    for tb in range(4):
                        for jp in range(3):
                            for cc in range(2):
                                pair_tb(svs[jp], 2 * jp + cc, cc, tb)
                    jp0 = 3
                for jp in range(jp0, npairs):
                    ca = c0 + 2 * jp
                    sv = load_pair(ca)
                    for cc in range(2):
                        gi = ca + cc - c0
                        for tb in range(4):
                            pair_tb(sv, gi, cc, tb)
                nk = c1 - c0
                if c0 != 0 and tail is not None:
                    if interleave is not None:
                        for _ in interleave:
                            pass
                    svs_o = []
                    for fp in range(4):
                        if fp < 3:
                            slot, sem = ring()
                        else:
                            slot, sem = av(64 * KB, 8 * KB, BF16), "d_r3"
                        sv = slot[:, 0:nk * 256].rearrange("p (k f) -> p k f", k=nk)
                        dma("gpsimd", sv, woutv[:, c0:c1, fp * 256:(fp + 1) * 256], sem)
                        svs_o.append(sv)
                    for tb in range(4):
                        for fo in range(8):
                            sv, fl = svs_o[fo // 2], fo % 2
                            bk = bank()
                            for k in range(nk):
                                mm(PS(bk), sv[:, k, fl * 128:(fl + 1) * 128], gbuf[:, k, tb * 512:(tb + 1) * 512],
                                   start=(k == 0), stop=(k == nk - 1))
                            stt(H(fo, tb), PS(bk), der[l][:, sub, 8 + fo:9 + fo], H(fo, tb), ALU.mult, ALU.add)
                        tail(tb)
                    continue
                for fp in range(4):
                    if interleave is not None:
                        next(interleave, None)
                    slot, sem = ring()
                    sv = slot[:, 0:nk * 256].rearrange("p (k f) -> p k f", k=nk)
                    dma("gpsimd", sv, woutv[:, c0:c1, fp * 256:(fp + 1) * 256], sem)
                    for fl in range(2):
                        fo = fp * 2 + fl
                        for tb in range(4):
                            bk = bank()
                            for k in range(nk):
                                mm(PS(bk), sv[:, k, fl * 128:(fl + 1) * 128], gbuf[:, k, tb * 512:(tb + 1) * 512],
                                   start=(k == 0), stop=(k == nk - 1))
                            stt(H(fo, tb), PS(bk), der[l][:, sub, 8 + fo:9 + fo], H(fo, tb), ALU.mult, ALU.add)

        yaT = av(24 * KB, 16 * KB, BF16).rearrange("p (c t) -> p c t", c=4)
        ybT = av(40 * KB, 16 * KB, BF16).rearrange("p (c t) -> p c t", c=4)
        SCR = 40 * KB
        cmpT = av(56 * KB, 16 * KB, BF16).rearrange("p (i t) -> p i t", i=4)
        w1v = av(72 * KB, 16 * KB, BF16).rearrange("p (j l f) -> p j l f", j=2, l=32)
        rope = av(56 * KB, 16 * KB, F32).rearrange("p (a t) -> p a t", a=2)
        Qaug = av(72 * KB, 16 * KB, BF16).rearrange("p (q h t) -> p q h t", q=16, h=4)
        Ksel = av(88 * KB, 4 * KB, BF16)
        Kwin = av(92 * KB, 4 * KB, BF16)
        Vs = av(40 * KB, 2112, BF16)[:, 0:16 * 65].rearrange("p (t d) -> p t d", t=16)
        Vw = av(40 * KB + 2112, 2112, BF16)[:, 0:16 * 65].rearrange("p (t d) -> p t d", t=16)
        gates = av(40 * KB + 4224, 768, F32).rearrange("p (t h b) -> p t h b", t=16, h=4)
        uT = av(56 * KB, 16 * KB, BF16).rearrange("p (c t) -> p c t", c=4)
        merged = av(56 * KB, 32 * KB, BF16).rearrange("p (c t) -> p c t", c=8)

        def mixv(l):
            return mixw_d[l].rearrange("(kc p) f -> p kc f", p=128)

        def stage_cmp(l):
            kcT, vcaug, w2e, peT, pebias, hcT, crope = (sm[k] for k in ["kcT", "vcaug", "w2e", "peT", "pebias", "hcT", "crope"])
            for j in range(2):
                dma("gpsimd", w1v[0:64, j, :, :], w1_d[l, j].rearrange("l d f -> d l f"), "d_m2")
            dma("gpsimd", w2e[:], w2e_d[l], "d_m2")
            dma("gpsimd", peT[:], peT_d[l], "d_m2")
            dma("sync", crope[:], crope_d, "d_m2")
            for g in range(2):
                dma("sync", vcaug[:, g, 64:97], vcc_d, "d_m2")
            slot, sem = ring()
            sv = slot[:, 0:2048].rearrange("p (k f) -> p k f", k=8)
            dma("gpsimd", sv, mixv(l)[:, :, OFF_KC:OFF_KC + 256], sem)
            for tb in range(4):
                for idx in range(4):
                    bk = bank()
                    for kc in range(8):
                        mm(PS(bk, 0, 64), sv[:, kc, idx * 64:(idx + 1) * 64], N_(kc, tb), start=(kc == 0), stop=(kc == 7))
                    act(cmpT[0:64, idx, tb * 512:(tb + 1) * 512], PS(bk, 0, 64), AF.Copy)
            for j in range(2):
                bp = bank()
                for ll in range(32):
                    mm(PS(bp, 0, 128, 0, 1), w1v[0:64, j, ll, :], peT[0:64, j, ll:ll + 1], start=(ll == 0), stop=(ll == 31))
                vcopy(pebias[:, j:j + 1], PS(bp, 0, 128, 0, 1))
                for g in range(2):
                    bh = bank()
                    for ll in range(32):
                        mm(PS(bh, 0, 128, 0, NCMP), w1v[0:64, j, ll, :], cmpT[0:64, j * 2 + g, ll:ll + 16 * (NCMP - 1) + 1:16],
                           start=(ll == 0), stop=(ll == 31))
                    act(hcT[:, 0:NCMP], PS(bh, 0, 128, 0, NCMP), AF.Silu, bias=pebias[:, j:j + 1])
                    if j == 0:
                        b1 = bank()
                        mm(PS(b1, 0, 64, 0, NCMP), w2e[:, 0:64], hcT[:, 0:NCMP])
                        b2 = bank()
                        mm(PS(b2, 0, 64, 0, NCMP), w2e[:, 64:128], hcT[:, 0:NCMP])
                        t1 = av(SCR, 2 * KB, F32)[0:64, 0:NCMP]
                        t2 = av(SCR + 2 * KB, 2 * KB, F32)[0:64, 0:NCMP]
                        tt(t1, PS(b1, 0, 64, 0, NCMP), crope[:, 0, :], ALU.mult)
                        tt(t2, PS(b2, 0, 64, 0, NCMP), crope[:, 1, :], ALU.mult)
                        tt(kcT[:, g, 0:NCMP], t1, t2, ALU.add)
                    else:
                        b2 = bank()
                        mm(PS(b2, 0, NCMP, 0, 64), hcT[:, 0:NCMP], w2e[:, 128:192])
                        act(vcaug[0:NCMP, g, 0:64], PS(b2, 0, NCMP, 0, 64), AF.Copy)

        def stage_nsa_group(l, g, interleave):
            kcT, vcaug = sm["kcT"], sm["vcaug"]
            rden, coef, imp, impf, imp2, m8, selpen = (sm[k] for k in ["rden", "coef", "imp", "impf", "imp2", "m8", "selpen"])
            rden2, coef2 = sm["rden2"], sm["coef2"]
            mv = mixv(l)
            slot, sem = ring()
            sv = slot.rearrange("p (k f) -> p k f", k=8)
            dma("gpsimd", sv, mv[:, :, OFF_QG[g]:OFF_QG[g] + 512], sem)
            T1 = SCR
            nrot = [0]

            def rope_pair(bq, br, tb):
                i = nrot[0] % 2
                nrot[0] += 1
                t1 = av(T1 + i * 4 * KB, 2 * KB, F32)
                t2 = av(T1 + i * 4 * KB + 2 * KB, 2 * KB, F32)
                stg = av(SCR + 14 * KB + i * KB, KB, BF16)
                tt(t1, PS(bq), rope[:, 0, tb * 512:(tb + 1) * 512], ALU.mult)
                tt(t2, PS(br), rope[:, 1, tb * 512:(tb + 1) * 512], ALU.mult)
                return t1, t2, stg

            for hp in range(2):
                for tb in range(4):
                    bq = bank()
                    for kc in range(8):
                        mm(PS(bq), sv[:, kc, hp * 128:(hp + 1) * 128], N_(kc, tb), start=(kc == 0), stop=(kc == 7))
                    br = bank()
                    for kc in range(8):
                        mm(PS(br), sv[:, kc, 256 + hp * 128:256 + (hp + 1) * 128], N_(kc, tb), start=(kc == 0), stop=(kc == 7))
                    t1, t2, stg = rope_pair(bq, br, tb)
                    tt(Qaug[0:64, tb * 4:(tb + 1) * 4, 2 * hp, :], t1[0:64].rearrange("p (q t) -> p q t", q=4),
                       t2[0:64].rearrange("p (q t) -> p q t", q=4), ALU.add)
                    tt(stg[64:128], t1[64:128], t2[64:128], ALU.add)
                    dma("sync", Qaug[0:64, tb * 4:(tb + 1) * 4, 2 * hp + 1, :], stg[64:128].rearrange("p (q t) -> p q t", q=4), "d_q")
            slot, sem = ring()
            sv = slot[:, 0:2048].rearrange("p (k f) -> p k f", k=8)
            dma("gpsimd", sv, mv[:, :, OFF_KG[g]:OFF_KG[g] + 256], sem)
            for tb in range(4):
                bq = bank()
                for kc in range(8):
                    mm(PS(bq), sv[:, kc, 0:128], N_(kc, tb), start=(kc == 0), stop=(kc == 7))
                br = bank()
                for kc in range(8):
                    mm(PS(br), sv[:, kc, 128:256], N_(kc, tb), start=(kc == 0), stop=(kc == 7))
                t1, t2, stg = rope_pair(bq, br, tb)
                tt(Ksel[0:64, tb * 512:(tb + 1) * 512], t1[0:64], t2[0:64], ALU.add)
                tt(stg[64:128], t1[64:128], t2[64:128], ALU.add)
                dma("sync", Kwin[0:64, tb * 512:(tb + 1) * 512], stg[64:128], "d_q")
            memset(Vs[:, :, 64:65], 1.0)
            memset(Vw[:, :, 64:65], 1.0)
            slot, sem = ring()
            sv = slot[:, 0:8 * 140].rearrange("p (k f) -> p k f", k=8)
            dma("gpsimd", sv, mv[:, :, OFF_VG[g]:OFF_VG[g] + 140], sem)
            for tq in range(16):
                bv = bank()
                for kc in range(8):
                    mm(PS(bv, 0, 128, 0, 140), nv[:, kc, tq * 128:(tq + 1) * 128], sv[:, kc, :], start=(kc == 0), stop=(kc == 7))
                act(Vs[:, tq, 0:64], PS(bv, 0, 128, 0, 64), AF.Copy)
                act(Vw[:, tq, 0:64], PS(bv, 0, 128, 64, 128), AF.Copy)
                act(gates[:, tq, :, :], PS(bv, 0, 128, 128, 140).rearrange("p (h b) -> p h b", h=4), AF.Sigmoid)
            PT0 = SCR + 8 * KB
            npt = [0]
            nsc = [0]

            def new_pT():
                i = npt[0] % 4
                npt[0] += 1
                return av(PT0 + i * KB, KB, BF16)

            def yatok_of(qt):
                return av(PT0 + 6 * KB + (qt % 2) * KB, KB, F32).rearrange("p (h d) -> p h d", h=4)

            def cmp_qk(qt):
                ncq = min(NCMP, 8 * qt + 7)
                q64 = Qaug[0:64, qt, :, :].rearrange("p h t -> p (h t)")
                bc = bank(fixed=6)
                mm(PS(bc, 0, ncq), kcT[0:64, g, 0:ncq], q64)
                ea = av(PT0 + 4 * KB, KB, BF16)
                eb = av(PT0 + 5 * KB, KB, BF16)
                act(ea[0:ncq, :], PS(bc, 0, ncq), AF.Exp, scale=0.125)
                eb3 = eb[0:ncq, :].rearrange("p (h t) -> p h t", h=4)
                ea3 = ea[0:ncq, :].rearrange("p (h t) -> p h t", h=4)
                P.op("gpsimd", lambda e, eb3=eb3, ea3=ea3, qt=qt: e.affine_select(
                    out=eb3, in_=ea3, pattern=[[0, 4], [1, 128]], compare_op=ALU.is_ge, fill=0.0,
                    base=128 * qt - 31, channel_multiplier=-16), reads=rk(ea3), writes=rk(eb3))

            def cmp_pv(qt):
                ncq = min(NCMP, 8 * qt + 7)
                yatok = yatok_of(qt)
                eb = av(PT0 + 5 * KB, KB, BF16)
                bo = bank(fixed=6)
                for hh in range(4):
                    mm(PS(bo, 0, 128, hh * 128, hh * 128 + 97), eb[0:ncq, hh * 128:(hh + 1) * 128], vcaug[0:ncq, g, :])
                bo3 = PS(bo).rearrange("p (h c) -> p h c", h=4)
                ts(rden[:].unsqueeze(2), bo3[:, :, 64:65], 1e-30, ALU.max)
                P.op("vector", lambda e: e.reciprocal(out=rden[:], in_=rden[:]), reads=rk(rden[:]), writes=rk(rden[:]))
                ts(imp[:], bo3[:, 0, 65:97], rden[:, 0:1], ALU.mult)
                for hh in range(1, 4):
                    stt(imp[:], bo3[:, hh, 65:97], rden[:, hh:hh + 1], imp[:], ALU.mult, ALU.add)
                tt(impf[:], imp[:], fv[:, qt, 1, :], ALU.mult)
                tt(impf[:], impf[:], fv[:, qt, 0, :], ALU.add)
                P.op("vector", lambda e: e.max(out=m8[:, 0:8], in_=impf[:]), reads=rk(impf[:]), writes=rk(m8[:]))
                P.op("vector", lambda e: e.match_replace(out=imp2[:], in_to_replace=m8[:, 0:8], in_values=impf[:], imm_value=-1e30),
                     reads=rk(impf[:], m8[:]), writes=rk(imp2[:]))
                P.op("vector", lambda e: e.max(out=m8[:, 8:16], in_=imp2[:]), reads=rk(imp2[:], m8[:]), writes=rk(m8[:]))
                ts(selpen[:, 64:96], impf[:], m8[:, 15:16], ALU.is_ge, 1.0, ALU.subtract)
                tt(coef[:].unsqueeze(2), gates[:, qt, :, 0:1], rden[:].unsqueeze(2), ALU.mult)
                tt(yatok, bo3[:, :, 0:64], coef[:].unsqueeze(2).broadcast_to([128, 4, 64]), ALU.mult)

            def selpen_T(qt):
                bt = bank(fixed=6)
                transpose(PSB(bt, 0, 96, 0, 128), selpen[:, 0:96], identb[:])
                vcopy(Qaug[64:96, qt, :, :], PSB(bt, 64, 96, 0, 128).unsqueeze(1).broadcast_to([32, 4, 128]))

            BR = {2: (Kwin, 64, Vw, 4, sm["rden3"], sm["coef3"]), 1: (Ksel, 96, Vs, 5, rden2, coef2)}

            def issue(it):
                qt, br, kt = it["qt"], it["br"], it["kt"]
                Kd, krows = BR[br][0], BR[br][1]
                qrhs = Qaug[0:krows, qt, :, :].rearrange("p h t -> p (h t)")
                b = nsc[0] % 4
                nsc[0] += 1
                mi = None
                if kt == qt:
                    mi = 0
                elif br == 2 and kt == qt - 4:
                    mi = 1
                mm(PS(b), Kd[0:krows, kt * 128:(kt + 1) * 128], qrhs, start=True, stop=(mi is None))
                if mi is not None:
                    mm(PS(b), identb[:], mneg[:, mi, :], start=False, stop=True)
                return b

            def process(it, b):
                br, kt = it["br"], it["kt"]
                Vd, bacc = BR[br][2], BR[br][3]
                pT = av(PT0 + (npt[0] % 4) * KB, KB, BF16)
                npt[0] += 1
                act(pT, PS(b), AF.Exp, scale=0.125)
                for hh in range(4):
                    mm(PS(bacc, 0, 128, hh * 128, hh * 128 + 65), pT[:, hh * 128:(hh + 1) * 128],
                       Vd[:, kt, :], start=(it["first"] and hh == 0), stop=it["last"], sgc=True)

            def fin_branch(qt, br):
                bacc, rd, cf = BR[br][3], BR[br][4], BR[br][5]
                yatok = yatok_of(qt)
                ba3 = PS(bacc).rearrange("p (h c) -> p h c", h=4)
                P.op("vector", lambda e: e.reciprocal(out=rd[:].unsqueeze(2), in_=ba3[:, :, 64:65]),
                     reads=rk(ba3[:, :, 64:65]), writes=rk(rd[:]))
                tt(cf[:].unsqueeze(2), gates[:, qt, :, br:br + 1], rd[:].unsqueeze(2), ALU.mult)
                tmp = av(SCR + 5 * KB, KB, F32).rearrange("p (h d) -> p h d", h=4)
                tt(tmp, ba3[:, :, 0:64], cf[:].unsqueeze(2).broadcast_to([128, 4, 64]), ALU.mult)
                tt(yatok, yatok, tmp, ALU.add)

            def yab_of(qt):
                return av(SCR + 6 * KB + (qt % 2) * KB, 512, BF16)

            def fin_dve(qt):
                vcopy(yab_of(qt), yatok_of(qt).rearrange("p h d -> p (h d)"))

            def fin_pe(qt):
                yab = yab_of(qt)
                by = bank(fixed=4)
                for j in range(2):
                    transpose(PSB(by, 0, 128, j * 128, (j + 1) * 128), yab[:, j * 128:(j + 1) * 128], identb[:])
                act(yaT[:, 2 * g:2 * g + 2, qt * 128:(qt + 1) * 128],
                    PSB(by, 0, 128, 0, 256).rearrange("p (j t) -> p j t", j=2), AF.Copy)

            items = []
            for qt in range(16):
                for br, kts in ((2, list(range(max(0, qt - 4), qt + 1))), (1, list(range(0, qt + 1)))):
                    for ki, kt in enumerate(kts):
                        items.append(dict(qt=qt, br=br, kt=kt, ki=ki, first=(ki == 0), last=(ki == len(kts) - 1)))
            for q0 in range(2):
                cmp_qk(q0)
                cmp_pv(q0)
                selpen_T(q0)
            LOOK = 3
            inflight = [issue(items[k]) for k in range(LOOK)]
            nexti = LOOK
            for idx, it in enumerate(items):
                qt, br = it["qt"], it["br"]
                if it["first"] and br == 2 and interleave is not None:
                    next(interleave, None)
                b = inflight.pop(0)
                if nexti < len(items):
                    inflight.append(issue(items[nexti]))
                    nexti += 1
                process(it, b)
                if br == 2 and 2 <= qt + 1 < 16:
                    if it["ki"] == 0:
                        cmp_qk(qt + 1)
                    elif it["ki"] == 1:
                        cmp_pv(qt + 1)
                if it["last"]:
                    fin_branch(qt, br)
                    if br == 1:
                        if 2 <= qt + 1 < 16:
                            selpen_T(qt + 1)
                        fin_dve(qt)
                        if qt >= 1:
                            fin_pe(qt - 1)
            fin_pe(15)

        def stage_nsa(l, interleave):
            dma("sync", rope[0:64, :, :], rope_d, "d_m3")
            dma("sync", rope[64:128, :, :], rope_d, "d_m3")
            dma("sync", Ksel[64:96, :], erows_d, "d_m3")
            memset(sm["selpen"][:, 0:64], 0.0)
            for g in range(2):
                stage_nsa_group(l, g, interleave)

        def stage_gmlp(l):
            G0 = 72 * KB
            wsT = av(G0, 2 * KB, BF16).rearrange("p (g t) -> p g t", g=8)
            bsT = av(G0 + 2 * KB, 2 * KB, F32).rearrange("p (j t) -> p j t", j=4)
            lng = av(G0 + 4 * KB, 2 * KB, F32)
            lnb = av(G0 + 6 * KB, 2 * KB, F32)
            dma("gpsimd", wsT, wsT_d[l], "d_m1")
            dma("sync", bsT, bsT_d[l], "d_m1")
            dma("sync", lng, gmln_d[l, 0].partition_broadcast(128), "d_m1")
            dma("sync", lnbT[:], lnbT_d[l], "d_m1")
            tt(wsT, wsT, tri[:, 0, :].unsqueeze(1).broadcast_to([128, 8, 128]), ALU.mult)
            BTp = lnb.rearrange("p (j t) -> p j t", j=4)
            bp0 = bank_pair()
            bq4 = psum[:, bp0 * 512:(bp0 + 2) * 512].rearrange("p (j a t) -> p j a t", j=4, a=2)
            for j in range(4):
                for a in range(2):
                    mm(bq4[:, j, a, :], onesb[:], wsT[:, 2 * j + a, :])
            for j in range(4):
                stt(BTp[0:64, j, :], bq4[0:64, j, 0, :], lnbT[0:64, j:j + 1], bsT[0:64, j, :], ALU.mult, ALU.add)
                stt(BTp[64:128, j, :], bq4[64:128, j, 1, :], lnbT[64:128, j:j + 1], bsT[64:128, j, :], ALU.mult, ALU.add)
            mv = mixv(l)
            slot, sem = ring()
            su = slot.rearrange("p (k f) -> p k f", k=8)
            dma("gpsimd", su, mv[:, :, OFF_U:OFF_U + 512], sem)
            for fc in range(4):
                for tb in range(4):
                    bk = bank()
                    for kc in range(8):
                        mm(PS(bk), su[:, kc, fc * 128:(fc + 1) * 128], N_(kc, tb), start=(kc == 0), stop=(kc == 7))
                    act(uT[:, fc, tb * 512:(tb + 1) * 512], PS(bk), AF.Gelu_apprx_tanh)
            slot, sem = ring()
            svv = slot.rearrange("p (k f) -> p k f", k=8)
            dma("gpsimd", svv, mv[:, :, OFF_V:OFF_V + 512], sem)
            bnst, bnmv, lnr = sm["bnst"], sm["bnmv"], sm["lnr"]

            def vproj(tq):
                bk = bank()
                for kc in range(8):
                    mm(PS(bk), nv[:, kc, tq * 128:(tq + 1) * 128], svv[:, kc, :], start=(kc == 0), stop=(kc == 7))
                return bk

            bk_next = vproj(0)
            for tq in range(16):
                bk = bk_next
                if tq + 1 < 16:
                    bk_next = vproj(tq + 1)
                vg = av(G0 + 8 * KB + (tq % 2) * 2 * KB, 2 * KB, F32)
                act(vg, PS(bk), AF.Gelu_apprx_tanh)
                P.op("vector", lambda e, vg=vg: e.bn_stats(out=bnst[:], in_=vg), reads=rk(vg), writes=rk(bnst[:]))
                P.op("vector", lambda e: e.bn_aggr(out=bnmv[:], in_=bnst[:]), reads=rk(bnst[:]), writes=rk(bnmv[:]))
                ts(lnr[:], bnmv[:, 1:2], EPS, ALU.add)
                tt(lnr[:], lnr[:], nhalf[:, 0:1], ALU.pow, eng="gpsimd")
                ts(vg, vg, bnmv[:, 0:1], ALU.subtract, lnr[:, 0:1], ALU.mult)
                vtok = av(G0 + 12 * KB + (tq % 2) * KB, KB, BF16)
                tt(vtok, vg, lng, ALU.mult)
                bp = bank_pair()
                bp4 = psum[:, bp * 512:(bp + 2) * 512].rearrange("p (j a t) -> p j a t", j=4, a=2)
                for j in range(4):
                    for a in range(2):
                        mm(bp4[:, j, a, :], vtok[:, j * 128:(j + 1) * 128], wsT[:, 2 * j + a, :])
                tmp = av(G0 + 14 * KB, 2 * KB, F32).rearrange("p (j t) -> p j t", j=4)
                tt(tmp[0:64], bp4[0:64, :, 0, :], BTp[0:64], ALU.add)
                tt(tmp[64:128], bp4[64:128, :, 1, :], BTp[64:128], ALU.add)
                tt(ybT[:, :, tq * 128:(tq + 1) * 128], tmp, uT[:, :, tq * 128:(tq + 1) * 128], ALU.mult)

        def stage_merge(l, tail=None):
            mv = mixv(l)
            pav = proj_a_d[l].rearrange("(kc p) f -> p kc f", p=128)
            pbv = proj_b_d[l].rearrange("(kc p) f -> p kc f", p=128)
            X0 = 88 * KB
            nx = [0]
            for fc in range(8):
                slot, sem = ring()
                sg = slot[:, 0:2048].rearrange("p (k f) -> p k f", k=8)
                spa = slot[:, 2048:2560].rearrange("p (k f) -> p k f", k=4)
                spb = slot[:, 2560:3072].rearrange("p (k f) -> p k f", k=4)
                dma("gpsimd", sg, mv[:, :, OFF_GAB + fc * 256:OFF_GAB + (fc + 1) * 256], sem)
                dma("gpsimd", spa, pav[:, :, fc * 128:(fc + 1) * 128], sem)
                dma("gpsimd", spb, pbv[:, :, fc * 128:(fc + 1) * 128], sem)
                for tb in range(4):
                    bA = bank()
                    for kc in range(8):
                        mm(PS(bA), sg[:, kc, 0:128], N_(kc, tb), start=(kc == 0), stop=(kc == 7))
                    bB = bank()
                    for kc in range(8):
                        mm(PS(bB), sg[:, kc, 128:256], N_(kc, tb), start=(kc == 0), stop=(kc == 7))
                    bPA = bank()
                    for kc in range(4):
                        mm(PS(bPA), spa[:, kc, :], yaT[:, kc, tb * 512:(tb + 1) * 512], start=(kc == 0), stop=(kc == 3))
                    bPB = bank()
                    for kc in range(4):
                        mm(PS(bPB), spb[:, kc, :], ybT[:, kc, tb * 512:(tb + 1) * 512], start=(kc == 0), stop=(kc == 3))
                    i = nx[0] % 2
                    nx[0] += 1
                    sga = av(X0 + i * 4 * KB, 2 * KB, F32)
                    sgb = av(X0 + i * 4 * KB + 2 * KB, 2 * KB, F32)
                    act(sga, PS(bA), AF.Sigmoid)
                    act(sgb, PS(bB), AF.Sigmoid)
                    tt(sga, sga, PS(bPA), ALU.mult)
                    tt(sgb, sgb, PS(bPB), ALU.mult)
                    tt(merged[:, fc, tb * 512:(tb + 1) * 512], sga, sgb, ALU.add)
            if STOP == "merged":
                return
            wov = w_out_d[l].rearrange("(kc p) f -> p kc f", p=128)
            svs_o = []
            for half in range(2):
                slot, sem = ring()
                sv = slot.rearrange("p (k f) -> p k f", k=8)
                dma("gpsimd", sv, wov[:, :, half * 512:(half + 1) * 512], sem)
                svs_o.append(sv)
            for tb in range(4):
                for fo in range(8):
                    sv, fl = svs_o[fo // 4], fo % 4
                    bk = bank()
                    for kc in range(8):
                        mm(PS(bk), sv[:, kc, fl * 128:(fl + 1) * 128], merged[:, kc, tb * 512:(tb + 1) * 512],
                           start=(kc == 0), stop=(kc == 7))
                    stt(H(fo, tb), PS(bk), der[l][:, 1, 8 + fo:9 + fo], H(fo, tb), ALU.mult, ALU.add)
                if tail is not None:
                    tail(tb)

        def dump_bf16(view3, nchunks):
            for c in range(nchunks):
                for tb in range(4):
                    vcopy(H(c, tb), view3[:, c, tb * 512:(tb + 1) * 512])

        def program():
            rms_stats(0, rstd_of(0))
            rms_stats(1, rstd_of(1))
            mod0 = mod_steps(0)
            for _ in range(6 if STOP != "mod" else 18):
                next(mod0)
            if STOP == "mod":
                vcopy(hv[:, 0, 0:72], modT[0][:])
                vcopy(hv[:, 0, 72:120], der[0][:].rearrange("p a b -> p (a b)"))
                return
            for l in range(2):
                if l == 0:
                    norm_mod(l, 0, pre=2)
                    if STOP == "n0":
                        dump_bf16(nv, 8)
                        return
                ffn(l, 0, 0, mod0 if l == 0 else None, tail=(lambda tb, l=l: norm_tb(l, 1, tb)))
                if l == 0:
                    for _ in mod0:
                        pass
                if STOP == "h1" and l == 0:
                    return
                stage_cmp(l)
                inter = mod_steps(1) if l == 0 else None
                stage_nsa(l, inter)
                if inter is not None:
                    for _ in inter:
                        pass
                if STOP == "ya" and l == 0:
                    dump_bf16(yaT, 4)
                    return
                stage_gmlp(l)
                if STOP == "yb" and l == 0:
                    dump_bf16(ybT, 4)
                    return
                stage_merge(l, tail=(lambda tb, l=l: norm_tb(l, 2, tb, scr=24 * KB)))
                if STOP == "h2" and l == 0:
                    return
                if l == 0:
                    ffn(l, 1, 2, tail=(lambda tb: norm_tb(1, 0, tb)))
                else:
                    ffn(l, 1, 2, tail=final_tb)
                if STOP == "h3" and l == 0:
                    return

        program()
        if STOP:
            for tb in range(4):
                dma("sync", outv[:, :, tb * 512:(tb + 1) * 512], hv[:, :, tb * 512:(tb + 1) * 512], "d_o")
        P.final_wait("sync", ["d_o"])
        P.emit(block, sems)
    return nc


def _consts():
    inv = 1.0 / (10000.0 ** (np.arange(0, 64, 2, dtype=np.float32) / 64.0))
    pos = np.arange(S, dtype=np.float32)
    ang = pos[:, None] * inv[None, :]
    cos, sin = np.cos(ang).astype(np.float32), np.sin(ang).astype(np.float32)
    rope = np.stack([np.concatenate([cos, cos], 1).T, np.concatenate([-sin, sin], 1).T], 1)
    cend = (np.arange(NCMP) * 16 + 31).astype(np.float32)
    angc = cend[:, None] * inv[None, :]
    cc, cs = np.cos(angc).astype(np.float32), np.sin(angc).astype(np.float32)
    crope = np.stack([np.concatenate([cc, cc], 1).T, np.concatenate([-cs, cs], 1).T], 1)
    cmask = np.zeros((128, S), np.float32)
    cmask[:NCMP] = (cend[:, None] <= pos[None, :]).astype(np.float32)
    k = np.arange(128)
    triS = (k[:, None] <= k[None, :]).astype(np.float32)
    triW = (k[:, None] > k[None, :]).astype(np.float32)
    tri = np.stack([triS, triW], 1)
    mneg = np.stack([np.tile((1.0 - triS) * -30000.0, (1, 4)), np.tile((1.0 - triW) * -30000.0, (1, 4))], 1)
    erows = (np.arange(S)[None, :] // 64 == np.arange(32)[:, None]).astype(np.float32) * 32768.0
    starts = np.arange(NCMP) * 16
    sel_start = np.arange(32) * 64
    ov = np.clip(np.minimum(starts[:, None] + 32, sel_start[None, :] + 64) - np.maximum(starts[:, None], sel_start[None, :]),
                 0, None).astype(np.float32) / 32.0
    vcc = np.zeros((128, 33), np.float32)
    vcc[:, 0] = 1.0
    vcc[:NCMP, 1:] = ov
    t = np.arange(S)
    cur = t // 64
    blk = np.arange(32)
    forced = (blk[None, :] == 0) | (blk[None, :] == cur[:, None]) | (blk[None, :] == cur[:, None] - 1)
    valid = blk[None, :] <= cur[:, None]
    fadd = np.where(forced, 1e4, np.where(valid, 0.0, -1e4)).astype(np.float32)
    vnf = (valid & ~forced).astype(np.float32)
    fv = np.stack([fadd, vnf], 1).reshape(16, 128, 2, 32).transpose(1, 0, 2, 3)
    bf = ml_dtypes.bfloat16
    return {
        "c_rope": np.ascontiguousarray(rope, np.float32), "c_crope": np.ascontiguousarray(crope, np.float32),
        "c_tri": np.ascontiguousarray(tri).astype(bf), "c_mneg": np.ascontiguousarray(mneg).astype(bf), "c_erows": erows.astype(bf),
        "c_vcc": vcc.astype(bf), "c_fv": np.ascontiguousarray(fv, np.float32), "c_identb": np.eye(128, dtype=np.float32).astype(bf),
    }


def _prep_shared(inp):
    perm = np.concatenate([np.arange(32, 64), np.arange(0, 32)])
    mw = inp["mix_w_in"]

    def kvcol(s, g):
        return 512 + (s * 2 + g) * 64 + np.arange(64)

    cols = [np.arange(1304, 1816), np.arange(1816, 2328), np.arange(512, 768)]
    for g in range(2):
        qh = [g * 256 + hh * 64 + np.arange(64) for hh in range(4)]
        cols += qh + [q[perm] for q in qh]
        cols += [kvcol(2, g), kvcol(4, g), kvcol(2, g)[perm], kvcol(4, g)[perm]]
        cols += [kvcol(3, g), kvcol(5, g), 1280 + g * 12 + np.arange(12)]
    for fc in range(8):
        cols += [2328 + fc * 128 + np.arange(128), 3352 + fc * 128 + np.arange(128)]
    cols = np.concatenate(cols)
    assert cols.shape[0] == NEXT
    mixw = np.ascontiguousarray(mw[:, :, cols])
    w2 = inp["cmp_w2"]
    w2e = np.ascontiguousarray(np.concatenate([w2[:, 0], w2[:, 0][:, :, perm], w2[:, 1]], axis=2))
    sh = {
        "ada_w": inp["ada_w"],
        "ada_bT": np.ascontiguousarray(inp["ada_b"].reshape(2, 72, 128).transpose(2, 0, 1)),
        "normg": np.ascontiguousarray(np.concatenate([inp["norm_g"].reshape(6, 8, 128), inp["final_g"].reshape(1, 8, 128)], 0)
                                      .reshape(56, 128).T),
        "ffn_w_in": inp["ffn_w_in"], "ffn_w_out": inp["ffn_w_out"], "mixw": mixw,
        "cmp_peT": np.ascontiguousarray(inp["cmp_pe"].transpose(0, 3, 1, 2)),
        "cmp_w1": inp["cmp_w1"], "cmp_w2e": w2e,
        "gm_ln": np.ascontiguousarray(np.stack([inp["gm_ln_g"], inp["gm_ln_b"]], 1)),
        "gm_wsT": np.ascontiguousarray(inp["gm_ws"].transpose(0, 3, 1, 2)),
        "gm_lnbT": np.ascontiguousarray(inp["gm_ln_b"].reshape(2, 4, 128).transpose(0, 2, 1)),
        "gm_bsT": np.ascontiguousarray(np.repeat(inp["gm_bs"], 64, axis=1).reshape(2, 4, 128, 128).transpose(0, 2, 1, 3)),
        "proj_a": inp["proj_a"], "proj_b": inp["proj_b"], "w_out": inp["w_out"],
    }
    sh.update(_consts())
    return sh


def kernel(**inputs):
    inp = {k: np.asarray(v) for k, v in inputs.items()}
    shared = _prep_shared(inp)
    ncores = int(os.environ.get("MK_NCORES", "8"))
    in_maps = []
    for b in range(ncores):
        m = dict(shared)
        m["xT"] = np.ascontiguousarray(inp["x"][b].T)
        m["cT"] = np.ascontiguousarray(inp["c"][b].reshape(8, 128).T)
        in_maps.append(m)
    nc = build_program()
    res = run_bass_kernel_spmd(nc, in_maps, core_ids=list(range(ncores)))
    out = np.stack([np.ascontiguousarray(r["outT"].T) for r in res.results], 0)
    return out.astype(np.float32)
```

```python
import os
import numpy as np
import ml_dtypes
from contextlib import ExitStack
import concourse.bass as bass
import concourse.mybir as mybir
from concourse.bass_utils import run_bass_kernel_spmd

F32 = mybir.dt.float32
BF16 = mybir.dt.bfloat16
AF = mybir.ActivationFunctionType
ALU = mybir.AluOpType
DS = {F32: 4, BF16: 2}

S = 2048
D = 1024
DFF = 2816
NCMP = 127
EPS = 1e-6
ENGS = ["sync", "scalar", "vector", "gpsimd", "tensor"]
GRAN = {"hT": 2048, "nT": 1024, "arena": 1024, "ps": 2048}

OFF_U = 0
OFF_V = 512
OFF_KC = 1024
OFF_QG = [1280, 1280 + 908]
OFF_KG = [1280 + 512, 1280 + 908 + 512]
OFF_VG = [1280 + 768, 1280 + 908 + 768]
OFF_GAB = 1280 + 2 * 908
NEXT = OFF_GAB + 2048

ARENA_KIB = 96
STOP = os.environ.get("MK_STOP", "")


def ap_keys(ap):
    name = ap.tensor.name
    if name not in GRAN:
        return [name]
    g = GRAN[name]
    ds = DS[ap.dtype]
    pat = ap.ap
    pstep = pat[0][0]
    off = (ap.offset % pstep) if pstep else ap.offset
    dims = [(s, n) for (s, n) in pat[1:]]
    res = set()

    def rec(i, base):
        if i >= len(dims):
            res.add(base * ds // g)
            return
        s, n = dims[i]
        if i == len(dims) - 1:
            lo = base
            hi = base + (n - 1) * abs(s)
            for ch in range(lo * ds // g, (hi * ds + ds - 1) // g + 1):
                res.add(ch)
        else:
            if s == 0:
                n = 1
            for j in range(n):
                rec(i + 1, base + j * s)

    rec(0, off)
    if name == "arena":
        p0 = ap.offset // pstep if pstep else 0
        q0, q1 = p0 // 32, (p0 + pat[0][1] - 1) // 32
        return [f"{name}{c}q{q}" for c in sorted(res) for q in range(q0, q1 + 1)]
    return [f"{name}{c}" for c in sorted(res)]


class Prog:
    def __init__(self):
        self.ops = {e: [] for e in ENGS}
        self.cnt = {}
        self.last_w = {}
        self.readers = {}
        self.seen = {e: {} for e in ENGS}
        self.dma_sems = set()

    def op(self, eng, fn, reads=(), writes=(), dma=None):
        need = {}

        def add(ev):
            if ev is None:
                return
            s, v = ev
            if s in self.dma_sems:
                v = self.cnt[s]
            if need.get(s, 0) < v:
                need[s] = v

        for k in reads:
            add(self.last_w.get(k))
        for k in writes:
            add(self.last_w.get(k))
            for s, v in self.readers.get(k, {}).items():
                add((s, v))
        own = "e_" + eng
        waits = []
        for s, v in need.items():
            if s == own and eng == "tensor":
                continue
            if self.seen[eng].get(s, 0) >= v:
                continue
            self.seen[eng][s] = v
            waits.append((s, v))
        if dma is None:
            sem, inc = own, 1
        else:
            sem, inc = dma, 16
            self.dma_sems.add(dma)
        self.cnt[sem] = self.cnt.get(sem, 0) + inc
        ev = (sem, self.cnt[sem])
        for k in writes:
            self.last_w[k] = ev
            self.readers[k] = {}
        for k in reads:
            d = self.readers.setdefault(k, {})
            d[sem] = max(d.get(sem, 0), ev[1])
        self.ops[eng].append((waits, fn, sem, inc))
        return ev

    def prewait(self, eng, reads):
        need = {}
        for k in reads:
            ev = self.last_w.get(k)
            if ev is None:
                continue
            s_, v = ev
            if s_ in self.dma_sems:
                v = self.cnt[s_]
            if need.get(s_, 0) < v:
                need[s_] = v
        own = "e_" + eng
        waits = []
        for s_, v in need.items():
            if s_ == own and eng == "tensor":
                continue
            if self.seen[eng].get(s_, 0) >= v:
                continue
            self.seen[eng][s_] = v
            waits.append((s_, v))
        if waits:
            self.ops[eng].append((waits, None, None, 0))

    def final_wait(self, eng, sems_):
        self.ops[eng].append(([(s, self.cnt[s]) for s in sems_], None, None, 0))

    def emit(self, block, sems):
        for e in ENGS:
            ops = self.ops[e]

            def body(eng, ops=ops):
                for waits, fn, sem, inc in ops:
                    for s, v in waits:
                        eng.wait_ge(sems[s], v)
                    if fn is not None:
                        fn(eng).then_inc(sems[sem], inc)

            getattr(block, e)(body)


def build_program():
    nc = bass.Bass("TRN2", target_bir_lowering=False)
    P = Prog()

    def dram(name, shape, dt=F32, kind="ExternalInput"):
        return nc.dram_tensor(name, list(shape), dt, kind=kind).ap()

    xT_d = dram("xT", [D, S])
    cT_d = dram("cT", [128, 8])
    ada_w_d = dram("ada_w", [2, D, 9 * D])
    ada_bT_d = dram("ada_bT", [128, 2, 72])
    normg_d = dram("normg", [128, 56])
    ffn_w_in_d = dram("ffn_w_in", [2, 2, D, 2 * DFF])
    ffn_w_out_d = dram("ffn_w_out", [2, 2, DFF, D])
    mixw_d = dram("mixw", [2, D, NEXT])
    peT_d = dram("cmp_peT", [2, 64, 2, 32])
    w1_d = dram("cmp_w1", [2, 2, 32, 64, 128])
    w2e_d = dram("cmp_w2e", [2, 128, 192])
    gmln_d = dram("gm_ln", [2, 2, 512])
    wsT_d = dram("gm_wsT", [2, 128, 8, 128])
    bsT_d = dram("gm_bsT", [2, 128, 4, 128])
    lnbT_d = dram("gm_lnbT", [2, 128, 4])
    proj_a_d = dram("proj_a", [2, 512, D])
    proj_b_d = dram("proj_b", [2, 512, D])
    w_out_d = dram("w_out", [2, D, D])
    rope_d = dram("c_rope", [64, 2, S])
    crope_d = dram("c_crope", [64, 2, NCMP])
    tri_d = dram("c_tri", [128, 2, 128], BF16)
    mneg_d = dram("c_mneg", [128, 2, 512], BF16)
    erows_d = dram("c_erows", [32, S], BF16)
    vcc_d = dram("c_vcc", [128, 33], BF16)
    fv_d = dram("c_fv", [128, 16, 2, 32])
    identb_d = dram("c_identb", [128, 128], BF16)
    outT_d = dram("outT", [D, S], F32, kind="ExternalOutput")

    es = ExitStack()
    with es:
        def sb(name, shape, dt):
            return es.enter_context(nc.sbuf_tensor(name, list(shape), dt))

        hT = sb("hT", [128, 8 * S], F32)
        nT = sb("nT", [128, 8 * S], BF16)
        arena = sb("arena", [128, ARENA_KIB * 256], F32)
        psum = es.enter_context(nc.psum_tensor("ps", [128, 4096], F32))
        tri = sb("tri", [128, 2, 128], BF16)
        mneg = sb("mneg", [128, 2, 512], BF16)
        fv = sb("fv", [128, 16, 2, 32], F32)
        identb = sb("identb", [128, 128], BF16)
        onesm = sb("onesm", [128, 128], BF16)
        onesb = sb("onesb", [128, 128], BF16)
        lnbT = sb("lnbT", [128, 4], F32)
        epsc = sb("epsc", [128, 1], F32)
        nhalf = sb("nhalf", [128, 1], F32)
        cT = sb("cTs", [128, 8], F32)
        scb = sb("scb", [128, 8], BF16)
        ada_bT = sb("ada_bTs", [128, 2, 72], F32)
        normg = sb("normgs", [128, 56], F32)
        modT = [sb(f"modT{l}", [128, 72], F32) for l in range(2)]
        der = [sb(f"der{l}", [128, 3, 16], F32) for l in range(2)]
        sm = {}
        for nm, shp, dt in [("rden", [128, 4], F32), ("coef", [128, 4], F32), ("imp", [128, 32], F32),
                            ("impf", [128, 32], F32), ("imp2", [128, 32], F32), ("m8", [128, 16], F32),
                            ("selpen", [128, 96], BF16), ("bnst", [128, 6], F32), ("bnmv", [128, 2], F32),
                            ("lnr", [128, 1], F32), ("pebias", [128, 2], F32), ("peT", [64, 2, 32], BF16),
                            ("w2e", [128, 192], BF16), ("hcT", [128, 128], BF16), ("kcT", [64, 2, 128], BF16),
                            ("vcaug", [128, 2, 97], BF16), ("crope", [64, 2, NCMP], F32),
                            ("rden2", [128, 4], F32), ("coef2", [128, 4], F32),
                            ("rden3", [128, 4], F32), ("coef3", [128, 4], F32)]:
            sm[nm] = sb(nm, shp, dt)

        sem_names = ["e_" + e for e in ENGS] + ["d_r0", "d_r1", "d_r2", "d_r3", "d_c", "d_x", "d_o", "d_q", "d_m1s", "d_m1h", "d_m2s", "d_m2h", "d_m3h"]
        sems = {n: es.enter_context(nc.semaphore(n)) for n in sem_names}
        block = es.enter_context(nc.Block())

        hv = hT[:].rearrange("p (c t) -> p c t", c=8)
        nv = nT[:].rearrange("p (c t) -> p c t", c=8)

        def H(c, tb):
            return hv[:, c, tb * 512:(tb + 1) * 512]

        def N_(c, tb):
            return nv[:, c, tb * 512:(tb + 1) * 512]

        def av(off_b, size_b, dt):
            a = arena[:, off_b // 4:(off_b + size_b) // 4]
            return a.bitcast(BF16) if dt == BF16 else a

        KB = 1024

        def rk(*aps):
            ks = []
            for a in aps:
                if a is None or isinstance(a, (int, float)):
                    continue
                ks += ap_keys(a)
            return ks

        def mm(out, lhsT, rhs, start=True, stop=True, sgc=False):
            P.op("tensor", lambda e: e.matmul(out, lhsT=lhsT, rhs=rhs, start=start, stop=stop, skip_group_check=sgc),
                 reads=rk(lhsT, rhs), writes=rk(out))

        def transpose(out, in_, ident):
            P.op("tensor", lambda e: e.transpose(out, in_, ident), reads=rk(in_, ident), writes=rk(out))

        def act(out, in_, func, bias=None, scale=None):
            kw = {}
            if bias is not None:
                kw["bias"] = bias
            if scale is not None:
                kw["scale"] = scale
            P.op("scalar", lambda e: e.activation(out=out, in_=in_, func=func, **kw),
                 reads=rk(in_, bias, scale), writes=rk(out))

        def tt(out, in0, in1, op, eng="vector"):
            P.op(eng, lambda e: e.tensor_tensor(out=out, in0=in0, in1=in1, op=op), reads=rk(in0, in1), writes=rk(out))

        def ts(out, in0, s1, op0, s2=None, op1=None, eng="vector"):
            if op1 is None:
                P.op(eng, lambda e: e.tensor_scalar(out=out, in0=in0, scalar1=s1, scalar2=None, op0=op0),
                     reads=rk(in0, s1), writes=rk(out))
            else:
                P.op(eng, lambda e: e.tensor_scalar(out=out, in0=in0, scalar1=s1, scalar2=s2, op0=op0, op1=op1),
                     reads=rk(in0, s1, s2), writes=rk(out))

        def stt(out, in0, scalar, in1, op0, op1):
            P.op("vector", lambda e: e.scalar_tensor_tensor(out=out, in0=in0, scalar=scalar, in1=in1, op0=op0, op1=op1),
                 reads=rk(in0, scalar, in1), writes=rk(out))

        def vcopy(out, in_, eng="vector"):
            P.op(eng, lambda e: e.tensor_copy(out=out, in_=in_), reads=rk(in_), writes=rk(out))

        def memset(ap, val, eng="vector"):
            P.op(eng, lambda e: e.memset(ap, val), writes=rk(ap))

        def dma(eng, out, in_, sem, **kw):
            rd = rk(in_) if in_.tensor.name in SBN else []
            wr = rk(out) if out.tensor.name in SBN else [out.tensor.name]
            if sem.startswith("d_m"):
                sem = sem + ("s" if eng == "gpsimd" else "h")
            return P.op(eng, lambda e: e.dma_start(out=out, in_=in_, **kw), reads=rd, writes=wr, dma=sem)

        SBN = set(["hT", "nT", "arena", "tri", "mneg", "fv", "identb", "onesm", "onesb", "lnbT", "epsc", "nhalf", "cTs", "scb", "ada_bTs", "normgs",
                   "modT0", "modT1", "der0", "der1"] + list(sm.keys()))

        bstate = {"next": 0, "held": set()}

        def bank(hold=False, fixed=None):
            if fixed is not None:
                if hold:
                    bstate["held"].add(fixed)
                return fixed
            for _ in range(16):
                i = bstate["next"]
                bstate["next"] = (i + 1) % 8
                if i not in bstate["held"]:
                    if hold:
                        bstate["held"].add(i)
                    return i
            raise RuntimeError("no psum bank")

        def bank_pair():
            for _ in range(16):
                i = bstate["next"]
                if i % 2 == 1:
                    i = (i + 1) % 8
                bstate["next"] = (i + 2) % 8
                if i not in bstate["held"] and (i + 1) not in bstate["held"]:
                    return i
            raise RuntimeError("no psum bank pair")

        def release(i):
            bstate["held"].discard(i)

        def PS(i, p0=0, p1=128, c0=0, c1=512):
            return psum[p0:p1, i * 512 + c0:i * 512 + c1]

        def PSB(i, p0, p1, c0, c1):
            return psum[p0:p1, i * 512:(i + 1) * 512].bitcast(BF16)[:, c0:c1]

        rstate = {"next": 0}

        def ring():
            i = rstate["next"]
            rstate["next"] = (i + 1) % 3
            return av(i * 8 * KB, 8 * KB, BF16), f"d_r{i}"

        for c in range(8):
            dma("sync", hv[:, c, :], xT_d[c * 128:(c + 1) * 128, :], "d_x")
        for dst, src in [(cT, cT_d), (ada_bT, ada_bT_d), (normg, normg_d), (tri, tri_d), (mneg, mneg_d),
                         (fv, fv_d), (identb, identb_d)]:
            dma("sync", dst[:], src, "d_c")
        memset(onesm[:], 1.0 / 1024.0)
        memset(epsc[:], EPS)
        memset(onesb[:], 1.0)
        memset(nhalf[:], -0.5)
        act(scb[:], cT[:], AF.Silu)

        def mod_steps(l):
            bm = bank(hold=True, fixed=7)
            awv = ada_w_d[l].rearrange("(kc p) f -> p kc f", p=128)
            for s in range(18):
                slot, sem = ring()
                sv = slot.rearrange("p (k f) -> p k f", k=8)
                if l == 0 and s == 0:
                    P.prewait("gpsimd", rk(hv[:, :, :]))
                dma("gpsimd", sv, awv[:, :, s * 512:(s + 1) * 512], sem)
                for fc in range(4):
                    j = s * 4 + fc
                    for kc in range(8):
                        mm(PS(bm, 0, 128, j, j + 1), sv[:, kc, fc * 128:(fc + 1) * 128], scb[:, kc:kc + 1],
                           start=(kc == 0), stop=(kc == 7))
                if s % 6 == 3:
                    sub = s // 6
                    tt(modT[l][:, sub * 24:sub * 24 + 16], PS(bm, 0, 128, sub * 24, sub * 24 + 16),
                       ada_bT[:, l, sub * 24:sub * 24 + 16], ALU.add)
                    stt(der[l][:, sub, 0:8], modT[l][:, (sub * 3 + 1) * 8:(sub * 3 + 2) * 8], 1.0,
                        normg[:, (l * 3 + sub) * 8:(l * 3 + sub + 1) * 8], ALU.add, ALU.mult)
                if s % 6 == 5:
                    sub = s // 6
                    tt(modT[l][:, sub * 24 + 16:sub * 24 + 24], PS(bm, 0, 128, sub * 24 + 16, sub * 24 + 24),
                       ada_bT[:, l, sub * 24 + 16:sub * 24 + 24], ALU.add)
                    ts(der[l][:, sub, 8:16], modT[l][:, (sub * 3 + 2) * 8:(sub * 3 + 3) * 8],
                       1.0 if sub == 1 else 0.5, ALU.mult)
                    if s == 17:
                        release(bm)
                yield

        NSCR = 84 * KB

        def rms_stats(tb, rstd, all_act=False, scr=None):
            scr = NSCR if scr is None else scr
            bk = bank()
            for c in range(8):
                sq = av(scr + (c % 4) * KB, KB, BF16)
                if all_act or c in (0, 2, 5, 7):
                    act(sq, H(c, tb), AF.Square)
                else:
                    tt(sq, H(c, tb), H(c, tb), ALU.mult)
                mm(PS(bk), onesm[:], sq, start=(c == 0), stop=(c == 7))
            act(rstd, PS(bk), AF.Ln, bias=epsc[:, 0:1])
            act(rstd, rstd, AF.Exp, scale=-0.5)

        def rstd_of(tb, scr=None):
            scr = NSCR if scr is None else scr
            return av(scr + 4 * KB + (tb % 2) * 2 * KB, 2 * KB, F32)

        def norm_tb(l, sub, tb, pre=0, scr=None):
            scr = NSCR if scr is None else scr
            rstd = rstd_of(tb, scr)
            if tb >= pre:
                rms_stats(tb, rstd, scr=scr)
            for c in range(8):
                tmp = av(scr + 8 * KB + (c % 2) * 2 * KB, 2 * KB, F32)
                tt(tmp, H(c, tb), rstd, ALU.mult)
                act(N_(c, tb), tmp, AF.Identity, bias=modT[l][:, sub * 24 + c:sub * 24 + c + 1],
                    scale=der[l][:, sub, c:c + 1])

        def norm_mod(l, sub, pre=0):
            for tb in range(4):
                norm_tb(l, sub, tb, pre)

        outv = outT_d.rearrange("(c p) t -> p c t", p=128)

        def final_tb(tb):
            rstd = rstd_of(tb)
            rms_stats(tb, rstd, all_act=True)
            for c in range(8):
                stt(H(c, tb), H(c, tb), normg[:, 48 + c:49 + c], rstd, ALU.mult, ALU.mult)
            if not STOP:
                dma("sync", outv[:, :, tb * 512:(tb + 1) * 512], hv[:, :, tb * 512:(tb + 1) * 512], "d_o")

        def ffn(l, i, sub, interleave=None, tail=None):
            winv = ffn_w_in_d[l, i].rearrange("(kc p) f -> p kc f", p=128)
            woutv = ffn_w_out_d[l, i].rearrange("(kc p) f -> p kc f", p=128)
            gbuf = av(24 * KB, 48 * KB, BF16).rearrange("p (c t) -> p c t", c=12)
            SA = 72 * KB
            nsa_ = [0]
            def load_pair(ca, il=True):
                if il and interleave is not None:
                    next(interleave, None)
                slot, sem = ring()
                sv = slot.rearrange("p (k a f) -> p k a f", k=8, a=2)
                dma("gpsimd", sv[:, :, 0, :], winv[:, :, ca * 128:ca * 128 + 256], sem)
                dma("gpsimd", sv[:, :, 1, :], winv[:, :, DFF + ca * 128:DFF + ca * 128 + 256], sem)
                return sv

            def pair_tb(sv, gi, cc, tb):
                bA = bank()
                for kc in range(8):
                    mm(PS(bA), sv[:, kc, 0, cc * 128:(cc + 1) * 128], N_(kc, tb), start=(kc == 0), stop=(kc == 7))
                bB = bank()
                for kc in range(8):
                    mm(PS(bB), sv[:, kc, 1, cc * 128:(cc + 1) * 128], N_(kc, tb), start=(kc == 0), stop=(kc == 7))
                sa = av(SA + (nsa_[0] % 3) * 2 * KB, 2 * KB, F32)
                nsa_[0] += 1
                act(sa, PS(bA), AF.Silu)
                tt(gbuf[:, gi, tb * 512:(tb + 1) * 512], sa, PS(bB), ALU.mult)

            for (c0, c1) in [(0, 12), (12, 22)]:
                npairs = (c1 - c0) // 2
                jp0 = 0
                if c0 == 0:
                    svs = [load_pair(c0 + 2 * jp, il=False) for jp in range(3)]
                    for tb in range(4):
                        for jp in range(3):
                            for cc in range(2):
                                pair_tb(svs[jp], 2 * jp + cc, cc, tb)
                    jp0 = 3
                for jp in range(jp0, npairs):
                    ca = c0 + 2 * jp
                    sv = load_pair(ca)
                    for cc in range(2):
                        gi = ca + cc - c0
                        for tb in range(4):
                            pair_tb(sv, gi, cc, tb)
                nk = c1 - c0
                if c0 != 0 and tail is not None:
                    if interleave is not None:
                        for _ in interleave:
                            pass
                    svs_o = []
                    for fp in range(4):
                        if fp < 3:
                            slot, sem = ring()
                        else:
                            slot, sem = av(64 * KB, 8 * KB, BF16), "d_r3"
                        sv = slot[:, 0:nk * 256].rearrange("p (k f) -> p k f", k=nk)
                        dma("gpsimd", sv, woutv[:, c0:c1, fp * 256:(fp + 1) * 256], sem)
                        svs_o.append(sv)
                    for tb in range(4):
                        for fo in range(8):
                            sv, fl = svs_o[fo // 2], fo % 2
                            bk = bank()
                            for k in range(nk):
                                mm(PS(bk), sv[:, k, fl * 128:(fl + 1) * 128], gbuf[:, k, tb * 512:(tb + 1) * 512],
                                   start=(k == 0), stop=(k == nk - 1))
                            stt(H(fo, tb), PS(bk), der[l][:, sub, 8 + fo:9 + fo], H(fo, tb), ALU.mult, ALU.add)
                        tail(tb)
                    continue
                for fp in range(4):
                    if interleave is not None:
                        next(interleave, None)
                    slot, sem = ring()
                    sv = slot[:, 0:nk * 256].rearrange("p (k f) -> p k f", k=nk)
                    dma("gpsimd", sv, woutv[:, c0:c1, fp * 256:(fp + 1) * 256], sem)
                    for fl in range(2):
                        fo = fp * 2 + fl
                        for tb in range(4):
                            bk = bank()
                            for k in range(nk):
                                mm(PS(bk), sv[:, k, fl * 128:(fl + 1) * 128], gbuf[:, k, tb * 512:(tb + 1) * 512],
                                   start=(k == 0), stop=(k == nk - 1))
                            stt(H(fo, tb), PS(bk), der[l][:, sub, 8 + fo:9 + fo], H(fo, tb), ALU.mult, ALU.add)

        yaT = av(24 * KB, 16 * KB, BF16).rearrange("p (c t) -> p c t", c=4)
        ybT = av(40 * KB, 16 * KB, BF16).rearrange("p (c t) -> p c t", c=4)
        SCR = 40 * KB
        cmpT = av(56 * KB, 16 * KB, BF16).rearrange("p (i t) -> p i t", i=4)
        w1v = av(72 * KB, 16 * KB, BF16).rearrange("p (j l f) -> p j l f", j=2, l=32)
        rope = av(56 * KB, 16 * KB, F32).rearrange("p (a t) -> p a t", a=2)
        Qaug = av(72 * KB, 16 * KB, BF16).rearrange("p (q h t) -> p q h t", q=16, h=4)
        Ksel = av(88 * KB, 4 * KB, BF16)
        Kwin = av(92 * KB, 4 * KB, BF16)
        Vs = av(40 * KB, 2112, BF16)[:, 0:16 * 65].rearrange("p (t d) -> p t d", t=16)
        Vw = av(40 * KB + 2112, 2112, BF16)[:, 0:16 * 65].rearrange("p (t d) -> p t d", t=16)
        gates = av(40 * KB + 4224, 768, F32).rearrange("p (t h b) -> p t h b", t=16, h=4)
        uT = av(56 * KB, 16 * KB, BF16).rearrange("p (c t) -> p c t", c=4)
        merged = av(56 * KB, 32 * KB, BF16).rearrange("p (c t) -> p c t", c=8)

        def mixv(l):
            return mixw_d[l].rearrange("(kc p) f -> p kc f", p=128)

        def stage_cmp(l):
            kcT, vcaug, w2e, peT, pebias, hcT, crope = (sm[k] for k in ["kcT", "vcaug", "w2e", "peT", "pebias", "hcT", "crope"])
            for j in range(2):
                dma("gpsimd", w1v[0:64, j, :, :], w1_d[l, j].rearrange("l d f -> d l f"), "d_m2")
            dma("gpsimd", w2e[:], w2e_d[l], "d_m2")
            dma("gpsimd", peT[:], peT_d[l], "d_m2")
            dma("sync", crope[:], crope_d, "d_m2")
            for g in range(2):
                dma("sync", vcaug[:, g, 64:97], vcc_d, "d_m2")
            slot, sem = ring()
            sv = slot[:, 0:2048].rearrange("p (k f) -> p k f", k=8)
            dma("gpsimd", sv, mixv(l)[:, :, OFF_KC:OFF_KC + 256], sem)
            for tb in range(4):
                for idx in range(4):
                    bk = bank()
                    for kc in range(8):
                        mm(PS(bk, 0, 64), sv[:, kc, idx * 64:(idx + 1) * 64], N_(kc, tb), start=(kc == 0), stop=(kc == 7))
                    act(cmpT[0:64, idx, tb * 512:(tb + 1) * 512], PS(bk, 0, 64), AF.Copy)
            for j in range(2):
                bp = bank()
                for ll in range(32):
                    mm(PS(bp, 0, 128, 0, 1), w1v[0:64, j, ll, :], peT[0:64, j, ll:ll + 1], start=(ll == 0), stop=(ll == 31))
                vcopy(pebias[:, j:j + 1], PS(bp, 0, 128, 0, 1))
                for g in range(2):
                    bh = bank()
                    for ll in range(32):
                        mm(PS(bh, 0, 128, 0, NCMP), w1v[0:64, j, ll, :], cmpT[0:64, j * 2 + g, ll:ll + 16 * (NCMP - 1) + 1:16],
                           start=(ll == 0), stop=(ll == 31))
                    act(hcT[:, 0:NCMP], PS(bh, 0, 128, 0, NCMP), AF.Silu, bias=pebias[:, j:j + 1])
                    if j == 0:
                        b1 = bank()
                        mm(PS(b1, 0, 64, 0, NCMP), w2e[:, 0:64], hcT[:, 0:NCMP])
                        b2 = bank()
                        mm(PS(b2, 0, 64, 0, NCMP), w2e[:, 64:128], hcT[:, 0:NCMP])
                        t1 = av(SCR, 2 * KB, F32)[0:64, 0:NCMP]
                        t2 = av(SCR + 2 * KB, 2 * KB, F32)[0:64, 0:NCMP]
                        tt(t1, PS(b1, 0, 64, 0, NCMP), crope[:, 0, :], ALU.mult)
                        tt(t2, PS(b2, 0, 64, 0, NCMP), crope[:, 1, :], ALU.mult)
                        tt(kcT[:, g, 0:NCMP], t1, t2, ALU.add)
                    else:
                        b2 = bank()
                        mm(PS(b2, 0, NCMP, 0, 64), hcT[:, 0:NCMP], w2e[:, 128:192])
                        act(vcaug[0:NCMP, g, 0:64], PS(b2, 0, NCMP, 0, 64), AF.Copy)

        def stage_nsa_group(l, g, interleave):
            kcT, vcaug = sm["kcT"], sm["vcaug"]
            rden, coef, imp, impf, imp2, m8, selpen = (sm[k] for k in ["rden", "coef", "imp", "impf", "imp2", "m8", "selpen"])
            rden2, coef2 = sm["rden2"], sm["coef2"]
            mv = mixv(l)
            slot, sem = ring()
            sv = slot.rearrange("p (k f) -> p k f", k=8)
            dma("gpsimd", sv, mv[:, :, OFF_QG[g]:OFF_QG[g] + 512], sem)
            T1 = SCR
            nrot = [0]

            def rope_pair(bq, br, tb):
                i = nrot[0] % 2
                nrot[0] += 1
                t1 = av(T1 + i * 4 * KB, 2 * KB, F32)
                t2 = av(T1 + i * 4 * KB + 2 * KB, 2 * KB, F32)
                stg = av(SCR + 14 * KB + i * KB, KB, BF16)
                tt(t1, PS(bq), rope[:, 0, tb * 512:(tb + 1) * 512], ALU.mult)
                tt(t2, PS(br), rope[:, 1, tb * 512:(tb + 1) * 512], ALU.mult)
                return t1, t2, stg

            for hp in range(2):
                for tb in range(4):
                    bq = bank()
                    for kc in range(8):
                        mm(PS(bq), sv[:, kc, hp * 128:(hp + 1) * 128], N_(kc, tb), start=(kc == 0), stop=(kc == 7))
                    br = bank()
                    for kc in range(8):
                        mm(PS(br), sv[:, kc, 256 + hp * 128:256 + (hp + 1) * 128], N_(kc, tb), start=(kc == 0), stop=(kc == 7))
                    t1, t2, stg = rope_pair(bq, br, tb)
                    tt(Qaug[0:64, tb * 4:(tb + 1) * 4, 2 * hp, :], t1[0:64].rearrange("p (q t) -> p q t", q=4),
                       t2[0:64].rearrange("p (q t) -> p q t", q=4), ALU.add)
                    tt(stg[64:128], t1[64:128], t2[64:128], ALU.add)
                    dma("sync", Qaug[0:64, tb * 4:(tb + 1) * 4, 2 * hp + 1, :], stg[64:128].rearrange("p (q t) -> p q t", q=4), "d_q")
            slot, sem = ring()
            sv = slot[:, 0:2048].rearrange("p (k f) -> p k f", k=8)
            dma("gpsimd", sv, mv[:, :, OFF_KG[g]:OFF_KG[g] + 256], sem)
            for tb in range(4):
                bq = bank()
                for kc in range(8):
                    mm(PS(bq), sv[:, kc, 0:128], N_(kc, tb), start=(kc == 0), stop=(kc == 7))
                br = bank()
                for kc in range(8):
                    mm(PS(br), sv[:, kc, 128:256], N_(kc, tb), start=(kc == 0), stop=(kc == 7))
                t1, t2, stg = rope_pair(bq, br, tb)
                tt(Ksel[0:64, tb * 512:(tb + 1) * 512], t1[0:64], t2[0:64], ALU.add)
                tt(stg[64:128], t1[64:128], t2[64:128], ALU.add)
                dma("sync", Kwin[0:64, tb * 512:(tb + 1) * 512], stg[64:128], "d_q")
            memset(Vs[:, :, 64:65], 1.0)
            memset(Vw[:, :, 64:65], 1.0)
            slot, sem = ring()
            sv = slot[:, 0:8 * 140].rearrange("p (k f) -> p k f", k=8)
            dma("gpsimd", sv, mv[:, :, OFF_VG[g]:OFF_VG[g] + 140], sem)
            for tq in range(16):
                bv = bank()
                for kc in range(8):
                    mm(PS(bv, 0, 128, 0, 140), nv[:, kc, tq * 128:(tq + 1) * 128], sv[:, kc, :], start=(kc == 0), stop=(kc == 7))
                act(Vs[:, tq, 0:64], PS(bv, 0, 128, 0, 64), AF.Copy)
                act(Vw[:, tq, 0:64], PS(bv, 0, 128, 64, 128), AF.Copy)
                act(gates[:, tq, :, :], PS(bv, 0, 128, 128, 140).rearrange("p (h b) -> p h b", h=4), AF.Sigmoid)
            PT0 = SCR + 8 * KB
            npt = [0]
            nsc = [0]

            def new_pT():
                i = npt[0] % 4
                npt[0] += 1
                return av(PT0 + i * KB, KB, BF16)

            def yatok_of(qt):
                return av(PT0 + 6 * KB + (qt % 2) * KB, KB, F32).rearrange("p (h d) -> p h d", h=4)

            def cmp_qk(qt):
                ncq = min(NCMP, 8 * qt + 7)
                q64 = Qaug[0:64, qt, :, :].rearrange("p h t -> p (h t)")
                bc = bank(fixed=6)
                mm(PS(bc, 0, ncq), kcT[0:64, g, 0:ncq], q64)
                ea = av(PT0 + 4 * KB, KB, BF16)
                eb = av(PT0 + 5 * KB, KB, BF16)
                act(ea[0:ncq, :], PS(bc, 0, ncq), AF.Exp, scale=0.125)
                eb3 = eb[0:ncq, :].rearrange("p (h t) -> p h t", h=4)
                ea3 = ea[0:ncq, :].rearrange("p (h t) -> p h t", h=4)
                P.op("gpsimd", lambda e, eb3=eb3, ea3=ea3, qt=qt: e.affine_select(
                    out=eb3, in_=ea3, pattern=[[0, 4], [1, 128]], compare_op=ALU.is_ge, fill=0.0,
                    base=128 * qt - 31, channel_multiplier=-16), reads=rk(ea3), writes=rk(eb3))

            def cmp_pv(qt):
                ncq = min(NCMP, 8 * qt + 7)
                yatok = yatok_of(qt)
                eb = av(PT0 + 5 * KB, KB, BF16)
                bo = bank(fixed=6)
                for hh in range(4):
                    mm(PS(bo, 0, 128, hh * 128, hh * 128 + 97), eb[0:ncq, hh * 128:(hh + 1) * 128], vcaug[0:ncq, g, :])
                bo3 = PS(bo).rearrange("p (h c) -> p h c", h=4)
                ts(rden[:].unsqueeze(2), bo3[:, :, 64:65], 1e-30, ALU.max)
                P.op("vector", lambda e: e.reciprocal(out=rden[:], in_=rden[:]), reads=rk(rden[:]), writes=rk(rden[:]))
                ts(imp[:], bo3[:, 0, 65:97], rden[:, 0:1], ALU.mult)
                for hh in range(1, 4):
                    stt(imp[:], bo3[:, hh, 65:97], rden[:, hh:hh + 1], imp[:], ALU.mult, ALU.add)
                tt(impf[:], imp[:], fv[:, qt, 1, :], ALU.mult)
                tt(impf[:], impf[:], fv[:, qt, 0, :], ALU.add)
                P.op("vector", lambda e: e.max(out=m8[:, 0:8], in_=impf[:]), reads=rk(impf[:]), writes=rk(m8[:]))
                P.op("vector", lambda e: e.match_replace(out=imp2[:], in_to_replace=m8[:, 0:8], in_values=impf[:], imm_value=-1e30),
                     reads=rk(impf[:], m8[:]), writes=rk(imp2[:]))
                P.op("vector", lambda e: e.max(out=m8[:, 8:16], in_=imp2[:]), reads=rk(imp2[:], m8[:]), writes=rk(m8[:]))
                ts(selpen[:, 64:96], impf[:], m8[:, 15:16], ALU.is_ge, 1.0, ALU.subtract)
                tt(coef[:].unsqueeze(2), gates[:, qt, :, 0:1], rden[:].unsqueeze(2), ALU.mult)
                tt(yatok, bo3[:, :, 0:64], coef[:].unsqueeze(2).broadcast_to([128, 4, 64]), ALU.mult)

            def selpen_T(qt):
                bt = bank(fixed=6)
                transpose(PSB(bt, 0, 96, 0, 128), selpen[:, 0:96], identb[:])
                vcopy(Qaug[64:96, qt, :, :], PSB(bt, 64, 96, 0, 128).unsqueeze(1).broadcast_to([32, 4, 128]))

            BR = {2: (Kwin, 64, Vw, 4, sm["rden3"], sm["coef3"]), 1: (Ksel, 96, Vs, 5, rden2, coef2)}

            def issue(it):
                qt, br, kt = it["qt"], it["br"], it["kt"]
                Kd, krows = BR[br][0], BR[br][1]
                qrhs = Qaug[0:krows, qt, :, :].rearrange("p h t -> p (h t)")
                b = nsc[0] % 4
                nsc[0] += 1
                mi = None
                if kt == qt:
                    mi = 0
                elif br == 2 and kt == qt - 4:
                    mi = 1
                mm(PS(b), Kd[0:krows, kt * 128:(kt + 1) * 128], qrhs, start=True, stop=(mi is None))
                if mi is not None:
                    mm(PS(b), identb[:], mneg[:, mi, :], start=False, stop=True)
                return b

            def process(it, b):
                br, kt = it["br"], it["kt"]
                Vd, bacc = BR[br][2], BR[br][3]
                pT = av(PT0 + (npt[0] % 4) * KB, KB, BF16)
                npt[0] += 1
                act(pT, PS(b), AF.Exp, scale=0.125)
                for hh in range(4):
                    mm(PS(bacc, 0, 128, hh * 128, hh * 128 + 65), pT[:, hh * 128:(hh + 1) * 128],
                       Vd[:, kt, :], start=(it["first"] and hh == 0), stop=it["last"], sgc=True)

            def fin_branch(qt, br):
                bacc, rd, cf = BR[br][3], BR[br][4], BR[br][5]
                yatok = yatok_of(qt)
                ba3 = PS(bacc).rearrange("p (h c) -> p h c", h=4)
                P.op("vector", lambda e: e.reciprocal(out=rd[:].unsqueeze(2), in_=ba3[:, :, 64:65]),
                     reads=rk(ba3[:, :, 64:65]), writes=rk(rd[:]))
                tt(cf[:].unsqueeze(2), gates[:, qt, :, br:br + 1], rd[:].unsqueeze(2), ALU.mult)
                tmp = av(SCR + 5 * KB, KB, F32).rearrange("p (h d) -> p h d", h=4)
                tt(tmp, ba3[:, :, 0:64], cf[:].unsqueeze(2).broadcast_to([128, 4, 64]), ALU.mult)
                tt(yatok, yatok, tmp, ALU.add)

            def yab_of(qt):
                return av(SCR + 6 * KB + (qt % 2) * KB, 512, BF16)

            def fin_dve(qt):
                vcopy(yab_of(qt), yatok_of(qt).rearrange("p h d -> p (h d)"))

            def fin_pe(qt):
                yab = yab_of(qt)
                by = bank(fixed=4)
                for j in range(2):
                    transpose(PSB(by, 0, 128, j * 128, (j + 1) * 128), yab[:, j * 128:(j + 1) * 128], identb[:])
                act(yaT[:, 2 * g:2 * g + 2, qt * 128:(qt + 1) * 128],
                    PSB(by, 0, 128, 0, 256).rearrange("p (j t) -> p j t", j=2), AF.Copy)

            items = []
            for qt in range(16):
                for br, kts in ((2, list(range(max(0, qt - 4), qt + 1))), (1, list(range(0, qt + 1)))):
                    for ki, kt in enumerate(kts):
                        items.append(dict(qt=qt, br=br, kt=kt, ki=ki, first=(ki == 0), last=(ki == len(kts) - 1)))
            for q0 in range(2):
                cmp_qk(q0)
                cmp_pv(q0)
                selpen_T(q0)
            LOOK = 3
            inflight = [issue(items[k]) for k in range(LOOK)]
            nexti = LOOK
            for idx, it in enumerate(items):
                qt, br = it["qt"], it["br"]
                if it["first"] and br == 2 and interleave is not None:
                    next(interleave, None)
                b = inflight.pop(0)
                if nexti < len(items):
                    inflight.append(issue(items[nexti]))
                    nexti += 1
                process(it, b)
                if br == 2 and 2 <= qt + 1 < 16:
                    if it["ki"] == 0:
                        cmp_qk(qt + 1)
                    elif it["ki"] == 1:
                        cmp_pv(qt + 1)
                if it["last"]:
                    fin_branch(qt, br)
                    if br == 1:
                        if 2 <= qt + 1 < 16:
                            selpen_T(qt + 1)
                        fin_dve(qt)
                        if qt >= 1:
                            fin_pe(qt - 1)
            fin_pe(15)

        def stage_nsa(l, interleave):
            dma("sync", rope[0:64, :, :], rope_d, "d_m3")
            dma("sync", rope[64:128, :, :], rope_d, "d_m3")
            dma("sync", Ksel[64:96, :], erows_d, "d_m3")
            memset(sm["selpen"][:, 0:64], 0.0)
            for g in range(2):
                stage_nsa_group(l, g, interleave)

        def stage_gmlp(l):
            G0 = 72 * KB
            wsT = av(G0, 2 * KB, BF16).rearrange("p (g t) -> p g t", g=8)
            bsT = av(G0 + 2 * KB, 2 * KB, F32).rearrange("p (j t) -> p j t", j=4)
            lng = av(G0 + 4 * KB, 2 * KB, F32)
            lnb = av(G0 + 6 * KB, 2 * KB, F32)
            dma("gpsimd", wsT, wsT_d[l], "d_m1")
            dma("sync", bsT, bsT_d[l], "d_m1")
            dma("sync", lng, gmln_d[l, 0].partition_broadcast(128), "d_m1")
            dma("sync", lnbT[:], lnbT_d[l], "d_m1")
            tt(wsT, wsT, tri[:, 0, :].unsqueeze(1).broadcast_to([128, 8, 128]), ALU.mult)
            BTp = lnb.rearrange("p (j t) -> p j t", j=4)
            bp0 = bank_pair()
            bq4 = psum[:, bp0 * 512:(bp0 + 2) * 512].rearrange("p (j a t) -> p j a t", j=4, a=2)
            for j in range(4):
                for a in range(2):
                    mm(bq4[:, j, a, :], onesb[:], wsT[:, 2 * j + a, :])
            for j in range(4):
                stt(BTp[0:64, j, :], bq4[0:64, j, 0, :], lnbT[0:64, j:j + 1], bsT[0:64, j, :], ALU.mult, ALU.add)
                stt(BTp[64:128, j, :], bq4[64:128, j, 1, :], lnbT[64:128, j:j + 1], bsT[64:128, j, :], ALU.mult, ALU.add)
            mv = mixv(l)
            slot, sem = ring()
            su = slot.rearrange("p (k f) -> p k f", k=8)
            dma("gpsimd", su, mv[:, :, OFF_U:OFF_U + 512], sem)
            for fc in range(4):
                for tb in range(4):
                    bk = bank()
                    for kc in range(8):
                        mm(PS(bk), su[:, kc, fc * 128:(fc + 1) * 128], N_(kc, tb), start=(kc == 0), stop=(kc == 7))
                    act(uT[:, fc, tb * 512:(tb + 1) * 512], PS(bk), AF.Gelu_apprx_tanh)
            slot, sem = ring()
            svv = slot.rearrange("p (k f) -> p k f", k=8)
            dma("gpsimd", svv, mv[:, :, OFF_V:OFF_V + 512], sem)
            bnst, bnmv, lnr = sm["bnst"], sm["bnmv"], sm["lnr"]

            def vproj(tq):
                bk = bank()
                for kc in range(8):
                    mm(PS(bk), nv[:, kc, tq * 128:(tq + 1) * 128], svv[:, kc, :], start=(kc == 0), stop=(kc == 7))
                return bk

            bk_next = vproj(0)
            for tq in range(16):
                bk = bk_next
                if tq + 1 < 16:
                    bk_next = vproj(tq + 1)
                vg = av(G0 + 8 * KB + (tq % 2) * 2 * KB, 2 * KB, F32)
                act(vg, PS(bk), AF.Gelu_apprx_tanh)
                P.op("vector", lambda e, vg=vg: e.bn_stats(out=bnst[:], in_=vg), reads=rk(vg), writes=rk(bnst[:]))
                P.op("vector", lambda e: e.bn_aggr(out=bnmv[:], in_=bnst[:]), reads=rk(bnst[:]), writes=rk(bnmv[:]))
                ts(lnr[:], bnmv[:, 1:2], EPS, ALU.add)
                tt(lnr[:], lnr[:], nhalf[:, 0:1], ALU.pow, eng="gpsimd")
                ts(vg, vg, bnmv[:, 0:1], ALU.subtract, lnr[:, 0:1], ALU.mult)
                vtok = av(G0 + 12 * KB + (tq % 2) * KB, KB, BF16)
                tt(vtok, vg, lng, ALU.mult)
                bp = bank_pair()
                bp4 = psum[:, bp * 512:(bp + 2) * 512].rearrange("p (j a t) -> p j a t", j=4, a=2)
                for j in range(4):
                    for a in range(2):
                        mm(bp4[:, j, a, :], vtok[:, j * 128:(j + 1) * 128], wsT[:, 2 * j + a, :])
                tmp = av(G0 + 14 * KB, 2 * KB, F32).rearrange("p (j t) -> p j t", j=4)
                tt(tmp[0:64], bp4[0:64, :, 0, :], BTp[0:64], ALU.add)
                tt(tmp[64:128], bp4[64:128, :, 1, :], BTp[64:128], ALU.add)
                tt(ybT[:, :, tq * 128:(tq + 1) * 128], tmp, uT[:, :, tq * 128:(tq + 1) * 128], ALU.mult)

        def stage_merge(l, tail=None):
            mv = mixv(l)
            pav = proj_a_d[l].rearrange("(kc p) f -> p kc f", p=128)
            pbv = proj_b_d[l].rearrange("(kc p) f -> p kc f", p=128)
            X0 = 88 * KB
            nx = [0]
            for fc in range(8):
                slot, sem = ring()
                sg = slot[:, 0:2048].rearrange("p (k f) -> p k f", k=8)
                spa = slot[:, 2048:2560].rearrange("p (k f) -> p k f", k=4)
                spb = slot[:, 2560:3072].rearrange("p (k f) -> p k f", k=4)
                dma("gpsimd", sg, mv[:, :, OFF_GAB + fc * 256:OFF_GAB + (fc + 1) * 256], sem)
                dma("gpsimd", spa, pav[:, :, fc * 128:(fc + 1) * 128], sem)
                dma("gpsimd", spb, pbv[:, :, fc * 128:(fc + 1) * 128], sem)
                for tb in range(4):
                    bA = bank()
                    for kc in range(8):
                        mm(PS(bA), sg[:, kc, 0:128], N_(kc, tb), start=(kc == 0), stop=(kc == 7))
                    bB = bank()
                    for kc in range(8):
                        mm(PS(bB), sg[:, kc, 128:256], N_(kc, tb), start=(kc == 0), stop=(kc == 7))
                    bPA = bank()
                    for kc in range(4):
                        mm(PS(bPA), spa[:, kc, :], yaT[:, kc, tb * 512:(tb + 1) * 512], start=(kc == 0), stop=(kc == 3))
                    bPB = bank()
                    for kc in range(4):
                        mm(PS(bPB), spb[:, kc, :], ybT[:, kc, tb * 512:(tb + 1) * 512], start=(kc == 0), stop=(kc == 3))
                    i = nx[0] % 2
                    nx[0] += 1
                    sga = av(X0 + i * 4 * KB, 2 * KB, F32)
                    sgb = av(X0 + i * 4 * KB + 2 * KB, 2 * KB, F32)
                    act(sga, PS(bA), AF.Sigmoid)
                    act(sgb, PS(bB), AF.Sigmoid)
                    tt(sga, sga, PS(bPA), ALU.mult)
                    tt(sgb, sgb, PS(bPB), ALU.mult)
                    tt(merged[:, fc, tb * 512:(tb + 1) * 512], sga, sgb, ALU.add)
            if STOP == "merged":
                return
            wov = w_out_d[l].rearrange("(kc p) f -> p kc f", p=128)
            svs_o = []
            for half in range(2):
                slot, sem = ring()
                sv = slot.rearrange("p (k f) -> p k f", k=8)
                dma("gpsimd", sv, wov[:, :, half * 512:(half + 1) * 512], sem)
                svs_o.append(sv)
            for tb in range(4):
                for fo in range(8):
                    sv, fl = svs_o[fo // 4], fo % 4
                    bk = bank()
                    for kc in range(8):
                        mm(PS(bk), sv[:, kc, fl * 128:(fl + 1) * 128], merged[:, kc, tb * 512:(tb + 1) * 512],
                           start=(kc == 0), stop=(kc == 7))
                    stt(H(fo, tb), PS(bk), der[l][:, 1, 8 + fo:9 + fo], H(fo, tb), ALU.mult, ALU.add)
                if tail is not None:
                    tail(tb)

        def dump_bf16(view3, nchunks):
            for c in range(nchunks):
                for tb in range(4):
                    vcopy(H(c, tb), view3[:, c, tb * 512:(tb + 1) * 512])

        def program():
            rms_stats(0, rstd_of(0))
            rms_stats(1, rstd_of(1))
            mod0 = mod_steps(0)
            for _ in range(4 if STOP != "mod" else 18):
                next(mod0)
            if STOP == "mod":
                vcopy(hv[:, 0, 0:72], modT[0][:])
                vcopy(hv[:, 0, 72:120], der[0][:].rearrange("p a b -> p (a b)"))
                return
            for l in range(2):
                if l == 0:
                    norm_mod(l, 0, pre=2)
                    if STOP == "n0":
                        dump_bf16(nv, 8)
                        return
                ffn(l, 0, 0, mod0 if l == 0 else None, tail=(lambda tb, l=l: norm_tb(l, 1, tb)))
                if l == 0:
                    for _ in mod0:
                        pass
                if STOP == "h1" and l == 0:
                    return
                stage_cmp(l)
                inter = mod_steps(1) if l == 0 else None
                stage_nsa(l, inter)
                if inter is not None:
                    for _ in inter:
                        pass
                if STOP == "ya" and l == 0:
                    dump_bf16(yaT, 4)
                    return
                stage_gmlp(l)
                if STOP == "yb" and l == 0:
                    dump_bf16(ybT, 4)
                    return
                stage_merge(l, tail=(lambda tb, l=l: norm_tb(l, 2, tb, scr=24 * KB)))
                if STOP == "h2" and l == 0:
                    return
                if l == 0:
                    ffn(l, 1, 2, tail=(lambda tb: norm_tb(1, 0, tb)))
                else:
                    ffn(l, 1, 2, tail=final_tb)
                if STOP == "h3" and l == 0:
                    return

        program()
        if STOP:
            for tb in range(4):
                dma("sync", outv[:, :, tb * 512:(tb + 1) * 512], hv[:, :, tb * 512:(tb + 1) * 512], "d_o")
        P.final_wait("sync", ["d_o"])
        P.emit(block, sems)
    return nc


def _consts():
    inv = 1.0 / (10000.0 ** (np.arange(0, 64, 2, dtype=np.float32) / 64.0))
    pos = np.arange(S, dtype=np.float32)
    ang = pos[:, None] * inv[None, :]
    cos, sin = np.cos(ang).astype(np.float32), np.sin(ang).astype(np.float32)
    rope = np.stack([np.concatenate([cos, cos], 1).T, np.concatenate([-sin, sin], 1).T], 1)
    cend = (np.arange(NCMP) * 16 + 31).astype(np.float32)
    angc = cend[:, None] * inv[None, :]
    cc, cs = np.cos(angc).astype(np.float32), np.sin(angc).astype(np.float32)
    crope = np.stack([np.concatenate([cc, cc], 1).T, np.concatenate([-cs, cs], 1).T], 1)
    cmask = np.zeros((128, S), np.float32)
    cmask[:NCMP] = (cend[:, None] <= pos[None, :]).astype(np.float32)
    k = np.arange(128)
    triS = (k[:, None] <= k[None, :]).astype(np.float32)
    triW = (k[:, None] > k[None, :]).astype(np.float32)
    tri = np.stack([triS, triW], 1)
    mneg = np.stack([np.tile((1.0 - triS) * -30000.0, (1, 4)), np.tile((1.0 - triW) * -30000.0, (1, 4))], 1)
    erows = (np.arange(S)[None, :] // 64 == np.arange(32)[:, None]).astype(np.float32) * 32768.0
    starts = np.arange(NCMP) * 16
    sel_start = np.arange(32) * 64
    ov = np.clip(np.minimum(starts[:, None] + 32, sel_start[None, :] + 64) - np.maximum(starts[:, None], sel_start[None, :]),
                 0, None).astype(np.float32) / 32.0
    vcc = np.zeros((128, 33), np.float32)
    vcc[:, 0] = 1.0
    vcc[:NCMP, 1:] = ov
    t = np.arange(S)
    cur = t // 64
    blk = np.arange(32)
    forced = (blk[None, :] == 0) | (blk[None, :] == cur[:, None]) | (blk[None, :] == cur[:, None] - 1)
    valid = blk[None, :] <= cur[:, None]
    fadd = np.where(forced, 1e4, np.where(valid, 0.0, -1e4)).astype(np.float32)
    vnf = (valid & ~forced).astype(np.float32)
    fv = np.stack([fadd, vnf], 1).reshape(16, 128, 2, 32).transpose(1, 0, 2, 3)
    bf = ml_dtypes.bfloat16
    return {
        "c_rope": np.ascontiguousarray(rope, np.float32), "c_crope": np.ascontiguousarray(crope, np.float32),
        "c_tri": np.ascontiguousarray(tri).astype(bf), "c_mneg": np.ascontiguousarray(mneg).astype(bf), "c_erows": erows.astype(bf),
        "c_vcc": vcc.astype(bf), "c_fv": np.ascontiguousarray(fv, np.float32), "c_identb": np.eye(128, dtype=np.float32).astype(bf),
    }


def _prep_shared(inp):
    perm = np.concatenate([np.arange(32, 64), np.arange(0, 32)])
    mw = inp["mix_w_in"]

    def kvcol(s, g):
        return 512 + (s * 2 + g) * 64 + np.arange(64)

    cols = [np.arange(1304, 1816), np.arange(1816, 2328), np.arange(512, 768)]
    for g in range(2):
        qh = [g * 256 + hh * 64 + np.arange(64) for hh in range(4)]
        cols += qh + [q[perm] for q in qh]
        cols += [kvcol(2, g), kvcol(4, g), kvcol(2, g)[perm], kvcol(4, g)[perm]]
        cols += [kvcol(3, g), kvcol(5, g), 1280 + g * 12 + np.arange(12)]
    for fc in range(8):
        cols += [2328 + fc * 128 + np.arange(128), 3352 + fc * 128 + np.arange(128)]
    cols = np.concatenate(cols)
    assert cols.shape[0] == NEXT
    mixw = np.ascontiguousarray(mw[:, :, cols])
    w2 = inp["cmp_w2"]
    w2e = np.ascontiguousarray(np.concatenate([w2[:, 0], w2[:, 0][:, :, perm], w2[:, 1]], axis=2))
    sh = {
        "ada_w": inp["ada_w"],
        "ada_bT": np.ascontiguousarray(inp["ada_b"].reshape(2, 72, 128).transpose(2, 0, 1)),
        "normg": np.ascontiguousarray(np.concatenate([inp["norm_g"].reshape(6, 8, 128), inp["final_g"].reshape(1, 8, 128)], 0)
                                      .reshape(56, 128).T),
        "ffn_w_in": inp["ffn_w_in"], "ffn_w_out": inp["ffn_w_out"], "mixw": mixw,
        "cmp_peT": np.ascontiguousarray(inp["cmp_pe"].transpose(0, 3, 1, 2)),
        "cmp_w1": inp["cmp_w1"], "cmp_w2e": w2e,
        "gm_ln": np.ascontiguousarray(np.stack([inp["gm_ln_g"], inp["gm_ln_b"]], 1)),
        "gm_wsT": np.ascontiguousarray(inp["gm_ws"].transpose(0, 3, 1, 2)),
        "gm_lnbT": np.ascontiguousarray(inp["gm_ln_b"].reshape(2, 4, 128).transpose(0, 2, 1)),
        "gm_bsT": np.ascontiguousarray(np.repeat(inp["gm_bs"], 64, axis=1).reshape(2, 4, 128, 128).transpose(0, 2, 1, 3)),
        "proj_a": inp["proj_a"], "proj_b": inp["proj_b"], "w_out": inp["w_out"],
    }
    sh.update(_consts())
    return sh


def kernel(**inputs):
    inp = {k: np.asarray(v) for k, v in inputs.items()}
    shared = _prep_shared(inp)
    ncores = int(os.environ.get("MK_NCORES", "8"))
    in_maps = []
    for b in range(ncores):
        m = dict(shared)
        m["xT"] = np.ascontiguousarray(inp["x"][b].T)
        m["cT"] = np.ascontiguousarray(inp["c"][b].reshape(8, 128).T)
        in_maps.append(m)
    nc = build_program()
    res = run_bass_kernel_spmd(nc, in_maps, core_ids=list(range(ncores)))
    out = np.stack([np.ascontiguousarray(r["outT"].T) for r in res.results], 0)
    return out.astype(np.float32)
```
